# Optimizing a Trainium2 kernel written in Bass

```python
import jax, jax.numpy as jnp
from jax import lax
import numpy as np

D_MODEL = 2048
BATCH = 8
SEQ = 4096
DEPTH = 1
DEC_BATCH = 32
DEC_SEQ = 16
PAST_LEN = 4096

CHUNK = 64
N_META = 16
RMS_EPS = 1e-6
D_FF = 5504
RW_WIDTH = D_MODEL // 2
RW_HEAD = 64
RW_HEADS = RW_WIDTH // RW_HEAD
RW_DECAY_LORA = 64
RW_A_LORA = 64
RW_GATE_LORA = 160
RW_COLS = 3 * RW_WIDTH + RW_DECAY_LORA + RW_A_LORA + RW_GATE_LORA
RW_GN_EPS = 64e-5
ML_WIDTH = D_MODEL // 2
ML_HEADS = 4
ML_HEAD = ML_WIDTH // ML_HEADS
ML_CONV = 4
ML_LN_EPS = 1e-5
ML_COLS = 4 * ML_WIDTH + 2 * ML_HEADS
GATE_COLS = 2 * D_MODEL
IN_COLS = RW_COLS + ML_COLS + GATE_COLS

kernel_name = "rwkv7_mlstm_gated_hybrid_stream_step"


def _rmsnorm(x, g):
    xf = x.astype(jnp.float32)
    y = xf * lax.rsqrt(jnp.mean(xf * xf, -1, keepdims=True) + RMS_EPS)
    return (y * g.astype(jnp.float32)).astype(x.dtype)


def _swiglu(x, w_gate, w_up, w_down):
    return (jax.nn.silu(x @ w_gate) * (x @ w_up)) @ w_down


def _norm_last(y, eps):
    mu = jnp.mean(y, -1, keepdims=True)
    var = jnp.mean(jnp.square(y - mu), -1, keepdims=True)
    return (y - mu) * lax.rsqrt(var + eps)


def _causal_conv(u, buf, w, b):
    T = u.shape[1]
    padded = jnp.concatenate([buf.astype(u.dtype), u], 1)
    out = b.astype(u.dtype)
    for j in range(ML_CONV):
        out = out + w[j].astype(u.dtype) * padded[:, j:j + T]
    return out, padded[:, T:]


def _rwkv7_scan(r, w, k, v, a_vec, b_vec, S0):
    def step(S, inp):
        r_t, w_t, k_t, v_t, a_t, b_t = inp
        sa = jnp.einsum('bhvk,bhk->bhv', S, a_t)
        S = S * w_t[:, :, None, :] + sa[..., None] * b_t[:, :, None, :] + v_t[..., None] * k_t[:, :, None, :]
        return S, jnp.einsum('bhvk,bhk->bhv', S, r_t)
    xs = tuple(jnp.moveaxis(t, 1, 0) for t in (r, w, k, v, a_vec, b_vec))
    S, ys = lax.scan(step, S0, xs)
    return jnp.moveaxis(ys, 0, 1), S


def _mlstm_chunkwise(q, k, v, i_pre, logf, C0, n0, m0, L):
    B, H, T, d = q.shape
    nc = T // L

    def to_chunks(t):
        t = t.reshape(B, H, nc, L, *t.shape[3:])
        return jnp.moveaxis(t, 2, 0)

    causal = jnp.tril(jnp.ones((L, L), bool))

    def step(carry, inp):
        C, n, m = carry
        qc, kc, vc, ic, fc = inp
        b = jnp.cumsum(fc, -1)
        log_inter = b + m[..., None]
        D = b[..., :, None] - b[..., None, :] + ic[..., None, :]
        D = jnp.where(causal, D, -jnp.inf)
        m_q = jnp.maximum(log_inter, jnp.max(D, -1))
        w_inter = jnp.exp(log_inter - m_q)
        s = jnp.einsum('bhtd,bhsd->bhts', qc, kc) * jnp.exp(D - m_q[..., None])
        num = w_inter[..., None] * jnp.einsum('bhtk,bhkv->bhtv', qc, C) + jnp.einsum('bhts,bhsv->bhtv', s, vc)
        den = w_inter * jnp.einsum('bhtk,bhk->bht', qc, n) + jnp.sum(s, -1)
        h = num / jnp.maximum(jnp.abs(den), jnp.exp(-m_q))[..., None]
        g = b[..., -1:] - b + ic
        m_new = jnp.maximum(b[..., -1] + m, jnp.max(g, -1))
        a_st = jnp.exp(b[..., -1] + m - m_new)
        wk = jnp.exp(g - m_new[..., None])
        C = a_st[..., None, None] * C + jnp.einsum('bhs,bhsk,bhsv->bhkv', wk, kc, vc)
        n = a_st[..., None] * n + jnp.einsum('bhs,bhsk->bhk', wk, kc)
        return (C, n, m_new), h

    (C, n, m), hs = lax.scan(step, (C0, n0, m0), tuple(map(to_chunks, (q, k, v, i_pre, logf))))
    h = jnp.moveaxis(hs, 0, 2).reshape(B, H, T, d)
    return h, C, n, m


def _mixer(h, states, prm, l, n_lead):
    f32 = jnp.float32
    rw_shift0, rw_S0, ml_conv0, ml_C0, ml_n0, ml_m0 = states
    B, T, _ = h.shape
    proj = (h @ prm['w_in'][l]).astype(f32)
    o1 = RW_COLS
    o2 = o1 + 2 * ML_WIDTH
    o3 = o2 + ML_WIDTH
    o4 = o3 + ML_WIDTH
    o5 = o4 + ML_HEADS
    o6 = o5 + ML_HEADS
    p_rw, p_qk, p_v, p_o = proj[..., :o1], proj[..., o1:o2], proj[..., o2:o3], proj[..., o3:o4]
    p_i, p_f, p_gate = proj[..., o4:o5], proj[..., o5:o6], proj[..., o6:]

    prev = jnp.concatenate([rw_shift0[:, None].astype(f32), p_rw[:, :-1]], 1)
    u = p_rw + prm['rw_mu'][l].astype(f32) * (prev - p_rw)
    new_rw_shift = p_rw[:, -1]
    W = RW_WIDTH
    r, k, v = u[..., :W], u[..., W:2 * W], u[..., 2 * W:3 * W]
    lw = u[..., 3 * W:3 * W + RW_DECAY_LORA]
    la = u[..., 3 * W + RW_DECAY_LORA:3 * W + RW_DECAY_LORA + RW_A_LORA]
    lg = u[..., 3 * W + RW_DECAY_LORA + RW_A_LORA:]
    wlog = -jax.nn.softplus(-(prm['rw_w0'][l] + jnp.tanh(lw) @ prm['rw_w2'][l])) - 0.5
    decay = jnp.exp(-jnp.exp(wlog.astype(f32)))
    a = jax.nn.sigmoid((prm['rw_a0'][l] + la @ prm['rw_a2'][l]).astype(f32))
    g = (jax.nn.sigmoid(lg) @ prm['rw_g2'][l]).astype(f32)
    heads = lambda t: t.reshape(B, T, RW_HEADS, RW_HEAD)
    kk = heads(k * prm['rw_kk'][l])
    kk = (kk / jnp.maximum(jnp.sqrt(jnp.sum(kk * kk, -1, keepdims=True)), 1e-12)).astype(f32)
    k = (k * (1.0 + (a - 1.0) * prm['rw_ka'][l])).astype(f32)
    rh, kh, vh, ah, wh = heads(r), heads(k), heads(v), heads(a), heads(decay)
    y, new_rw_S = _rwkv7_scan(rh, wh, kh, vh, -kk, kk * ah, rw_S0.astype(f32))
    y = _norm_last(y, RW_GN_EPS).reshape(B, T, W) * prm['rw_ln_w'][l] + prm['rw_ln_b'][l]
    bonus = jnp.sum(rh * kh * prm['rw_rk'][l].astype(f32), -1, keepdims=True) * vh
    y_rw = ((y + bonus.reshape(B, T, W)) * g).astype(f32)

    qk, new_ml_conv = _causal_conv(p_qk, ml_conv0, prm['ml_conv_w'][l], prm['ml_conv_b'][l])
    qk = jax.nn.silu(qk.astype(f32))
    mh = lambda t: t.reshape(B, T, ML_HEADS, ML_HEAD).transpose(0, 2, 1, 3)
    q = mh(qk[..., :ML_WIDTH])
    km = mh(qk[..., ML_WIDTH:] * (ML_HEAD ** -0.5))
    vm = mh(p_v)
    i_pre = (p_i + prm['ml_i_b'][l]).astype(f32).transpose(0, 2, 1)
    logf = jax.nn.log_sigmoid((p_f + prm['ml_f_b'][l]).astype(f32)).transpose(0, 2, 1)
    C, n, m = ml_C0.astype(f32), ml_n0.astype(f32), ml_m0.astype(f32)
    segments = [(0, n_lead, n_lead), (n_lead, T, min(CHUNK, T - n_lead))] if n_lead > 0 else [(0, T, min(CHUNK, T))]
    hs = []
    for (s0, s1, L) in segments:
        hseg, C, n, m = _mlstm_chunkwise(q[:, :, s0:s1], km[:, :, s0:s1], vm[:, :, s0:s1],
                                         i_pre[..., s0:s1], logf[..., s0:s1], C, n, m, L)
        hs.append(hseg)
    hm = jnp.concatenate(hs, 2)
    hm = _norm_last(hm, ML_LN_EPS).transpose(0, 2, 1, 3).reshape(B, T, ML_WIDTH)
    y_ml = hm * prm['ml_norm_w'][l] * jax.nn.sigmoid(p_o)

    gates = jax.nn.sigmoid(p_gate)
    merged = gates[..., :D_MODEL] * (y_rw @ prm['w_br_rw'][l]) + gates[..., D_MODEL:] * (y_ml @ prm['w_br_ml'][l])
    out = (merged @ prm['w_out'][l]).astype(h.dtype)
    return out, (new_rw_shift, new_rw_S, new_ml_conv, C, n, m)


def _trunk(x, states, prm, n_lead):
    collected = []
    for l in range(DEPTH):
        x = x + 0.5 * _swiglu(_rmsnorm(x, prm['ffn1_norm'][l]), prm['ffn1_w_gate'][l], prm['ffn1_w_up'][l], prm['ffn1_w_down'][l])
        mix, ns = _mixer(_rmsnorm(x, prm['mix_norm'][l]), tuple(s[l] for s in states), prm, l, n_lead)
        x = x + mix
        x = x + 0.5 * _swiglu(_rmsnorm(x, prm['ffn2_norm'][l]), prm['ffn2_w_gate'][l], prm['ffn2_w_up'][l], prm['ffn2_w_down'][l])
        collected.append(ns)
    new_states = tuple(jnp.stack([ns[i] for ns in collected]) for i in range(len(collected[0])))
    return _rmsnorm(x, prm['final_norm']), new_states


def setup_inputs(seed: int = 0) -> dict:
    key = jax.random.key(seed)
    ks = iter(jax.random.split(key, 48))
    nrm = lambda shape, s: jax.random.normal(next(ks), shape, jnp.float32) * s
    uni = lambda shape, lo, hi: jax.random.uniform(next(ks), shape, jnp.float32, lo, hi)
    Dp = DEPTH
    f_bias = jnp.broadcast_to(jnp.linspace(3.0, 6.0, ML_HEADS, dtype=jnp.float32), (Dp, ML_HEADS))
    return {
        "x_prompt": nrm((BATCH, SEQ, D_MODEL), 1.0),
        "x_sample": nrm((DEC_BATCH, DEC_SEQ, D_MODEL), 1.0),
        "state_rwkv_shift": nrm((Dp, DEC_BATCH, RW_COLS), 1.0),
        "state_rwkv_wkv": nrm((Dp, DEC_BATCH, RW_HEADS, RW_HEAD, RW_HEAD), 0.1),
        "state_mlstm_conv": nrm((Dp, DEC_BATCH, ML_CONV - 1, 2 * ML_WIDTH), 1.0),
        "state_mlstm_C": nrm((Dp, DEC_BATCH, ML_HEADS, ML_HEAD, ML_HEAD), 0.05),
        "state_mlstm_n": nrm((Dp, DEC_BATCH, ML_HEADS, ML_HEAD), 0.5),
        "state_mlstm_m": nrm((Dp, DEC_BATCH, ML_HEADS), 1.0),
        "meta_tokens": nrm((N_META, D_MODEL), 1.0),
        "ffn1_norm": 1.0 + nrm((Dp, D_MODEL), 0.02),
        "ffn1_w_gate": nrm((Dp, D_MODEL, D_FF), D_MODEL ** -0.5),
        "ffn1_w_up": nrm((Dp, D_MODEL, D_FF), D_MODEL ** -0.5),
        "ffn1_w_down": nrm((Dp, D_FF, D_MODEL), D_FF ** -0.5),
        "mix_norm": 1.0 + nrm((Dp, D_MODEL), 0.02),
        "w_in": nrm((Dp, D_MODEL, IN_COLS), D_MODEL ** -0.5),
        "rw_mu": uni((Dp, RW_COLS), 0.0, 1.0),
        "rw_w0": uni((Dp, RW_WIDTH), -6.0, 0.0),
        "rw_w2": nrm((Dp, RW_DECAY_LORA, RW_WIDTH), 0.1),
        "rw_a0": nrm((Dp, RW_WIDTH), 0.5),
        "rw_a2": nrm((Dp, RW_A_LORA, RW_WIDTH), 0.5 * RW_A_LORA ** -0.5),
        "rw_g2": nrm((Dp, RW_GATE_LORA, RW_WIDTH), RW_GATE_LORA ** -0.5),
        "rw_kk": 1.0 + nrm((Dp, RW_WIDTH), 0.1),
        "rw_ka": 1.0 + nrm((Dp, RW_WIDTH), 0.1),
        "rw_rk": nrm((Dp, RW_HEADS, RW_HEAD), 0.1),
        "rw_ln_w": 1.0 + nrm((Dp, RW_WIDTH), 0.02),
        "rw_ln_b": nrm((Dp, RW_WIDTH), 0.02),
        "ml_conv_w": nrm((Dp, ML_CONV, 2 * ML_WIDTH), 0.5),
        "ml_conv_b": nrm((Dp, 2 * ML_WIDTH), 0.02),
        "ml_i_b": nrm((Dp, ML_HEADS), 0.1),
        "ml_f_b": f_bias + nrm((Dp, ML_HEADS), 0.1),
        "ml_norm_w": 1.0 + nrm((Dp, ML_WIDTH), 0.02),
        "w_br_rw": nrm((Dp, RW_WIDTH, D_MODEL), RW_WIDTH ** -0.5),
        "w_br_ml": nrm((Dp, ML_WIDTH, D_MODEL), ML_WIDTH ** -0.5),
        "w_out": nrm((Dp, D_MODEL, D_MODEL), D_MODEL ** -0.5),
        "ffn2_norm": 1.0 + nrm((Dp, D_MODEL), 0.02),
        "ffn2_w_gate": nrm((Dp, D_MODEL, D_FF), D_MODEL ** -0.5),
        "ffn2_w_up": nrm((Dp, D_MODEL, D_FF), D_MODEL ** -0.5),
        "ffn2_w_down": nrm((Dp, D_FF, D_MODEL), D_FF ** -0.5),
        "final_norm": 1.0 + nrm((D_MODEL,), 0.02),
    }


def reference(x_prompt, x_sample, state_rwkv_shift, state_rwkv_wkv, state_mlstm_conv, state_mlstm_C,
              state_mlstm_n, state_mlstm_m, meta_tokens, ffn1_norm, ffn1_w_gate, ffn1_w_up, ffn1_w_down,
              mix_norm, w_in, rw_mu, rw_w0, rw_w2, rw_a0, rw_a2, rw_g2, rw_kk, rw_ka, rw_rk, rw_ln_w, rw_ln_b,
              ml_conv_w, ml_conv_b, ml_i_b, ml_f_b, ml_norm_w, w_br_rw, w_br_ml, w_out,
              ffn2_norm, ffn2_w_gate, ffn2_w_up, ffn2_w_down, final_norm):
    f32 = jnp.float32
    prm = dict(ffn1_norm=ffn1_norm, ffn1_w_gate=ffn1_w_gate, ffn1_w_up=ffn1_w_up, ffn1_w_down=ffn1_w_down,
               mix_norm=mix_norm, w_in=w_in, rw_mu=rw_mu, rw_w0=rw_w0, rw_w2=rw_w2, rw_a0=rw_a0, rw_a2=rw_a2,
               rw_g2=rw_g2, rw_kk=rw_kk, rw_ka=rw_ka, rw_rk=rw_rk, rw_ln_w=rw_ln_w, rw_ln_b=rw_ln_b,
               ml_conv_w=ml_conv_w, ml_conv_b=ml_conv_b, ml_i_b=ml_i_b, ml_f_b=ml_f_b, ml_norm_w=ml_norm_w,
               w_br_rw=w_br_rw, w_br_ml=w_br_ml, w_out=w_out, ffn2_norm=ffn2_norm, ffn2_w_gate=ffn2_w_gate,
               ffn2_w_up=ffn2_w_up, ffn2_w_down=ffn2_w_down, final_norm=final_norm)

    B = x_prompt.shape[0]
    meta = jnp.broadcast_to(meta_tokens.astype(x_prompt.dtype)[None], (B, N_META, D_MODEL))
    xp = jnp.concatenate([meta, x_prompt], 1)
    zero_states = (jnp.zeros((DEPTH, B, RW_COLS), f32),
                   jnp.zeros((DEPTH, B, RW_HEADS, RW_HEAD, RW_HEAD), f32),
                   jnp.zeros((DEPTH, B, ML_CONV - 1, 2 * ML_WIDTH), f32),
                   jnp.zeros((DEPTH, B, ML_HEADS, ML_HEAD, ML_HEAD), f32),
                   jnp.zeros((DEPTH, B, ML_HEADS, ML_HEAD), f32),
                   jnp.zeros((DEPTH, B, ML_HEADS), f32))
    yp, (p_rw_shift, p_rw_wkv, p_ml_conv, p_ml_C, p_ml_n, p_ml_m) = _trunk(xp, zero_states, prm, N_META)
    y_prompt = yp[:, N_META:]

    s_states = (state_rwkv_shift, state_rwkv_wkv, state_mlstm_conv, state_mlstm_C, state_mlstm_n, state_mlstm_m)
    y_sample, (s_rw_shift, s_rw_wkv, s_ml_conv, s_ml_C, s_ml_n, s_ml_m) = _trunk(x_sample, s_states, prm, 0)

    return (y_prompt, y_sample,
            p_rw_shift, p_rw_wkv, p_ml_conv, p_ml_C, p_ml_n, p_ml_m,
            s_rw_shift, s_rw_wkv, s_ml_conv, s_ml_C, s_ml_n, s_ml_m)
```

```python
import numpy as np
from contextlib import ExitStack
import concourse.bass as bass
import concourse.mybir as mybir
from concourse.bass_utils import run_bass_kernel_spmd

F32 = mybir.dt.float32
BF16 = mybir.dt.bfloat16
AF = mybir.ActivationFunctionType
ALU = mybir.AluOpType
AX = mybir.AxisListType

SEM_ROT = 20000
_DTSZ = {}


def _dsz(dt):
    s = _DTSZ.get(dt)
    if s is None:
        s = 2 if dt == BF16 else 4
        _DTSZ[dt] = s
    return s


def _rect(ap):
    a = ap.ap
    pstep, pn = a[0]
    off = ap.offset
    if pstep == 0:
        p0 = 0
        f0 = off
    else:
        p0 = off // pstep
        f0 = off % pstep
    ext = 0
    for st, cnt in a[1:]:
        ext += (cnt - 1) * abs(st)
    sz = _dsz(ap.dtype)
    return (ap.tensor.name, p0, p0 + pn, f0 * sz, (f0 + ext + 1) * sz)


class Op:
    __slots__ = ("eng", "fn", "waits", "sig", "dkey", "dcount", "pos", "semidx", "count", "idx")

    def __init__(self, eng, fn):
        self.eng = eng
        self.fn = fn
        self.waits = []
        self.sig = False
        self.dkey = None
        self.dcount = 0
        self.pos = 0


class Prog:
    ENGS = ("pe", "act", "dve", "pool", "sync")

    def __init__(self, nc):
        self.nc = nc
        self.ops = []
        self.by_eng = {e: [] for e in self.ENGS}
        self.recs = {}
        self.waited = {}
        self.dma_counts = {}

    @staticmethod
    def _is_ap(v):
        return hasattr(v, "ap") and hasattr(v, "tensor") and hasattr(v, "offset")

    def _track(self, v):
        if not self._is_ap(v):
            return None
        sp = str(v.space)
        if "PSUM" in sp:
            return (v.tensor.name, 0, 128, 0, 1 << 20)
        if "SB" in sp:
            return _rect(v)
        return None

    def add(self, eng, fn, reads, writes, dkey=None, after=None):
        op = Op(eng, fn)
        op.pos = len(self.by_eng[eng])
        idx = len(self.ops)
        op.idx = idx
        deps = set(o.idx for o in (after or []))
        rl = list(dict.fromkeys(r for r in (self._track(v) for v in reads) if r is not None))
        wl = list(dict.fromkeys(r for r in (self._track(v) for v in writes) if r is not None))
        wl = list(dict.fromkeys(wl + [r for r in rl if r[0].startswith("ps")]))
        rl = [r for r in rl if not r[0].startswith("ps")]
        for (nm, p0, p1, f0, f1) in rl:
            for rec in self.recs.setdefault(nm, []):
                if rec[5] and rec[0] < p1 and p0 < rec[1] and rec[2] < f1 and f0 < rec[3]:
                    deps.add(rec[4])
        for (nm, p0, p1, f0, f1) in wl:
            for rec in self.recs.setdefault(nm, []):
                if rec[0] < p1 and p0 < rec[1] and rec[2] < f1 and f0 < rec[3]:
                    deps.add(rec[4])
        for (nm, p0, p1, f0, f1) in wl:
            lst = self.recs[nm]
            lst[:] = [rec for rec in lst if not (p0 <= rec[0] and rec[1] <= p1 and f0 <= rec[2] and rec[3] <= f1)]
            lst.append([p0, p1, f0, f1, idx, True])
        for (nm, p0, p1, f0, f1) in rl:
            lst = self.recs[nm]
            lst[:] = [rec for rec in lst if not ((not rec[5]) and rec[0] == p0 and rec[1] == p1 and rec[2] == f0
                                                  and rec[3] == f1 and rec[4] != idx and self.ops[rec[4]].eng == eng)]
            lst.append([p0, p1, f0, f1, idx, False])
        deps.discard(idx)
        for j in sorted(deps):
            oj = self.ops[j]
            if oj.dkey is not None:
                k = (eng, "d", oj.dkey)
                if self.waited.get(k, 0) >= oj.dcount:
                    continue
                self.waited[k] = oj.dcount
                op.waits.append(("d", oj.dkey, j))
            else:
                if oj.eng == eng and eng == "pe":
                    continue
                k = (eng, "e", oj.eng)
                if self.waited.get(k, -1) >= oj.pos:
                    continue
                self.waited[k] = oj.pos
                oj.sig = True
                op.waits.append(("e", oj.eng, j))
        if dkey is not None:
            op.dkey = dkey
            self.dma_counts[dkey] = self.dma_counts.get(dkey, 0) + 16
            op.dcount = self.dma_counts[dkey]
        self.ops.append(op)
        self.by_eng[eng].append(op)
        return op

    def seal_key(self, key):
        tot = self.dma_counts.get(key, 0)
        for op in self.ops:
            if op.dkey == key:
                op.dcount = tot

    def ins(self, eng, meth, *, reads=None, writes=None, **kw):
        r = list(reads or [])
        w = list(writes or [])
        for k, v in kw.items():
            if self._is_ap(v):
                if k in ("out", "accum_out", "ap"):
                    w.append(v)
                else:
                    r.append(v)
        return self.add(eng, lambda e: getattr(e, meth)(**kw), r, w)

    def mm(self, out, lhsT, rhs, start=True, stop=True, tp=None):
        if tp is None:
            return self.add("pe", lambda e: e.matmul(out, lhsT, rhs, start=start, stop=stop), [lhsT, rhs], [out])
        return self.add("pe", lambda e: e.matmul(out, lhsT, rhs, start=start, stop=stop, tile_position=tp),
                        [lhsT, rhs], [out])

    def dma(self, eng, out, in_, key, after=None, **kw):
        return self.add(eng, lambda e: e.dma_start(out=out, in_=in_, **kw), [in_], [out], dkey=key, after=after)

    def emit(self):
        nc = self.nc
        with ExitStack() as es:
            esems = {}
            for eng in self.ENGS:
                c = 0
                for op in self.by_eng[eng]:
                    if op.sig:
                        c += 1
                        op.semidx = (c - 1) // SEM_ROT
                        op.count = (c - 1) % SEM_ROT + 1
                nsem = (c + SEM_ROT - 1) // SEM_ROT
                esems[eng] = [es.enter_context(nc.semaphore(f"s_{eng}_{i}")) for i in range(max(nsem, 1))]
            dsems = {k: es.enter_context(nc.semaphore(f"d_{k}")) for k in self.dma_counts}
            block = es.enter_context(nc.Block())
            ops = self.ops

            def run(eng_name):
                def body(e):
                    for op in self.by_eng[eng_name]:
                        for (kind, key, j) in op.waits:
                            oj = ops[j]
                            if kind == "d":
                                e.wait_ge(dsems[key], oj.dcount)
                            else:
                                e.wait_ge(esems[key][oj.semidx], oj.count)
                        inst = op.fn(e)
                        if op.dkey is not None:
                            inst.then_inc(dsems[op.dkey], 16)
                        elif op.sig:
                            inst.then_inc(esems[eng_name][op.semidx], 1)
                    if eng_name == "sync":
                        for k, tot in self.dma_counts.items():
                            e.wait_ge(dsems[k], tot)
                return body

            block.tensor(run("pe"))
            block.scalar(run("act"))
            block.vector(run("dve"))
            block.gpsimd(run("pool"))
            block.sync(run("sync"))


D = 2048
DFF = 5504
NKC = 16
NFC = 43
RWC = 3360
O_R, O_K, O_V, O_LW, O_LG = 0, 1024, 2048, 3072, 3200
O_MQ, O_MV, O_MO, O_MI, O_MF, O_G1, O_G2 = 3360, 5408, 6432, 7456, 7460, 7464, 9512
INC = 11560
KAPPA = -0.6065306597126334
RW_EPS = 64e-5
ML_EPS = 1e-5

WSHAPES = [
    ("ffn1_norm", (D,)), ("ffn1_w_gate", (D, DFF)), ("ffn1_w_up", (D, DFF)), ("ffn1_w_down", (DFF, D)),
    ("mix_norm", (D,)), ("w_in", (D, INC)), ("rw_mu", (RWC,)), ("rw_w0", (1024,)), ("rw_w2", (64, 1024)),
    ("rw_a0", (1024,)), ("rw_a2", (64, 1024)), ("rw_g2", (160, 1024)), ("rw_kk", (1024,)), ("rw_ka", (1024,)),
    ("rw_rk", (1024,)), ("rw_ln_w", (1024,)), ("rw_ln_b", (1024,)), ("ml_conv_w", (4, 2048)),
    ("ml_conv_b", (2048,)), ("ml_i_b", (4,)), ("ml_f_b", (4,)), ("ml_norm_w", (1024,)),
    ("w_br_rw", (1024, D)), ("w_br_ml", (1024, D)), ("w_out", (D, D)), ("ffn2_norm", (D,)),
    ("ffn2_w_gate", (D, DFF)), ("ffn2_w_up", (D, DFF)), ("ffn2_w_down", (DFF, D)), ("final_norm", (D,)),
]

C_ID = 0
C_M5 = 128
C_OBD = 448
C_R64 = 576
C_R16 = 1088
C_ML = 1600
C_SEL = 1728
C_ONE = 2240
CST_N = 2752


class _Stop(Exception):
    pass


def make_consts():
    c = np.zeros((128, CST_N), np.float32)
    c[:, C_ID:C_ID + 128] = np.eye(128, dtype=np.float32)
    s = np.arange(64)[:, None]
    t = np.arange(64)[None, :]
    strict = (s < t).astype(np.float32)
    strictT = (t < s).astype(np.float32)
    incl = (s <= t).astype(np.float32)
    for h in range(2):
        r = slice(h * 64, h * 64 + 64)
        for i, m in enumerate((strict, strictT, strict, incl, incl)):
            c[r, C_M5 + i * 64:C_M5 + (i + 1) * 64] = m
        c[r, C_OBD + h * 64:C_OBD + h * 64 + 64] = 1.0
    r64 = np.ones(512, np.float32)
    r64[::64] = 0.0
    r16 = np.ones(512, np.float32)
    r16[::16] = 0.0
    c[:, C_R64:C_R64 + 512] = r64[None]
    c[:, C_R16:C_R16 + 512] = r16[None]
    s2 = np.arange(128)[:, None]
    t2 = np.arange(128)[None, :]
    c[:, C_ML:C_ML + 128] = (s2 <= t2).astype(np.float32)
    for h in range(4):
        c[h, C_SEL + h * 128:C_SEL + (h + 1) * 128] = 1.0
    c[:, C_ONE:C_ONE + 512] = 1.0
    return c


def build(NPT, dbg=False, stop=None):
    SEQ = NPT * 512
    nc = bass.Bass("TRN2", target_bir_lowering=False)

    def din(name, shape):
        return nc.dram_tensor(name, list(shape), F32, kind="ExternalInput").ap()

    def dout(name, shape):
        return nc.dram_tensor(name, list(shape), F32, kind="ExternalOutput").ap()

    xp = din("xp", [SEQ, D])
    xs = din("xs", [64, D])
    meta = din("meta", [16, D])
    st_shift = din("st_shift", [4, RWC])
    st_wkv = din("st_wkv", [4, 16, 64, 64])
    st_conv = din("st_conv", [4, 3, 2048])
    st_C = din("st_C", [4, 4, 256, 256])
    st_n = din("st_n", [4, 4, 256])
    st_m = din("st_m", [4, 4])
    cst_d = din("cst", [128, CST_N])
    W = {nm: din(nm, shp) for nm, shp in WSHAPES}
    yp = dout("yp", [SEQ, D])
    ys = dout("ys", [64, D])
    o_shift = dout("o_shift", [5, RWC])
    o_wkv = dout("o_wkv", [5, 16, 64, 64])
    o_conv = dout("o_conv", [5, 3, 2048])
    o_C = dout("o_C", [5, 4, 256, 256])
    o_n = dout("o_n", [5, 4, 256])
    o_m = dout("o_m", [5, 4])
    dbg_out = {}
    if dbg:
        dbg_out["d_x1"] = dout("d_x1", [512, D])
        dbg_out["d_x2"] = dout("d_x2", [512, D])
        dbg_out["d_yrw"] = dout("d_yrw", [128, 8, 512])
        dbg_out["d_yml"] = dout("d_yml", [128, 8, 512])

    es = ExitStack()
    with es:
        def sb(name, shape, dt):
            return es.enter_context(nc.sbuf_tensor(name, list(shape), dt))

        P = Prog(nc)
        NRING = 6
        xres = sb("xres", [128, 4, D], F32)
        xnT = sb("xnT", [128, NKC, 512], BF16)
        SCRB = 51200
        scr = sb("scr", [128, SCRB // 4], F32)
        wring = sb("wring", [128, NRING, 4096], BF16)
        sgb = sb("sgb", [128, 1, 512], F32)
        yrwT = sb("yrwT", [128, 8, 512], BF16)
        ymlT = sb("ymlT", [128, 8, 512], BF16)
        Pst = sb("Pst", [128, 8, 64], F32)
        Pbf = sb("Pbf", [128, 8, 64], BF16)
        Cst = sb("Cst", [128, 4, 2, 257], F32)
        Cbf = sb("Cbf", [128, 4, 2, 258], BF16)
        gfin = sb("gfin", [128, D], F32)
        identb = sb("identb", [128, 128], BF16)
        identf = sb("identf", [128, 128], F32)
        mask5 = sb("mask5", [128, 5, 64], BF16)
        onesbd = sb("onesbd", [128, 128], F32)
        r64 = sb("r64", [128, 512], BF16)
        r16 = sb("r16", [128, 512], BF16)
        one512 = sb("one512", [128, 512], BF16)
        mlmask = sb("mlmask", [128, 128], BF16)
        sel4 = sb("sel4", [4, 512], F32)
        w2a2 = sb("w2a2", [128, 1024], BF16)
        g2a = sb("g2a", [128, 1024], BF16)
        g2b = sb("g2b", [32, 1024], BF16)
        gcols = sb("gcols", [128, 3, 16], F32)
        mucols = sb("mucols", [128, 27], F32)
        rwc = sb("rwc", [128, 7, 8], F32)
        cwc = sb("cwc", [128, 16, 4], F32)
        cbc = sb("cbc", [128, 16], F32)
        nwc = sb("nwc", [128, 8], F32)
        ifb = sb("ifb", [4, 2], F32)
        carry = sb("carry", [128, 27, 5], F32)
        ccar = sb("ccar", [128, 16, 5, 3], F32)
        mrow = sb("mrow", [4, 8], F32)
        small = sb("small", [128, 64], F32)

        banks = [es.enter_context(nc.psum_tensor(f"ps{i}", [128, 512], F32)) for i in range(8)]
        bstate = {"i": 0, "w": 0}
        cur = {"ti": 0}

        chk_cnt = {}

        def chk(tag, ti=None):
            if stop is None:
                return
            want = stop[0]
            k = 1
            if "#" in want:
                want, k = want.split("#")
                k = int(k)
            if want == tag and (ti is None or stop[1] == ti):
                chk_cnt[tag] = chk_cnt.get(tag, 0) + 1
                if chk_cnt[tag] >= k:
                    raise _Stop()

        def bank():
            b = banks[bstate["i"] % 8]
            bstate["i"] += 1
            return b

        def carve(off, shape, dt):
            n = int(np.prod(shape))
            sz = _dsz(dt)
            assert off % 4 == 0 and off + n * sz <= SCRB, (off, shape)
            nf = (n * sz + 3) // 4
            v = scr[:, off // 4: off // 4 + nf]
            if dt == BF16:
                v = v.bitcast(BF16)[:, 0:n]
            if len(shape) == 1:
                return v
            names = " ".join(f"a{i}" for i in range(len(shape)))
            kw = {f"a{i}": int(s) for i, s in enumerate(shape)}
            return v.rearrange(f"p ({names}) -> p {names}", **kw)

        NWL = 219
        wscr = nc.dram_tensor("wscr", [NWL, 128, 4096], BF16).ap()
        wr_ops = {}
        wl_state = {"n": 0}

        def wload(src, shape):
            s = bstate["w"] % NRING
            bstate["w"] += 1
            i = wl_state["n"]
            wl_state["n"] += 1
            a, b = shape
            assert a * b <= 4096 and i < NWL
            flat = wring[:, s, 0:a * b]
            dst = flat.rearrange("p (a b) -> p a b", a=a)
            if cur["ti"] == 0:
                P.dma("pool", dst, src, f"w{s}")
                if NPT > 0:
                    wr_ops[i] = P.dma("sync", wscr[i, :, 0:a * b], flat, f"sw{s}")
            else:
                P.dma("sync", flat, wscr[i, :, 0:a * b], f"w{s}", after=[wr_ops[i]])
            return dst

        xsb = carve(SCRB - 4096, [D], BF16)
        cstg = carve(0, [CST_N], F32)
        stgA = carve(11008, [3392], F32)
        stgB = carve(11008 + 13568, [2048], F32)
        P.dma("sync", cstg, cst_d, "const")
        P.dma("sync", gfin[:], W["final_norm"].partition_broadcast(128), "const")
        stgV = [carve(36864, [128], F32), carve(36864 + 512, [128], F32)]
        P.ins("pool", "memset", ap=stgV[0][:, :], constant=0.0)
        P.ins("pool", "memset", ap=stgV[1][:, :], constant=0.0)

        def vrows(t, r0, ap, n):
            P.dma("sync", stgV[t][r0:r0 + n, :], ap.rearrange("(c p) -> c p", p=128), "const")
        vrows(0, 0, W["ffn1_norm"], 16)
        vrows(0, 16, W["mix_norm"], 16)
        vrows(0, 32, W["ffn2_norm"], 16)
        vrows(0, 48, W["rw_mu"][0:3328], 26)
        P.dma("sync", stgV[0][74:75, 0:32], W["rw_mu"][3328:3360].rearrange("(c p) -> c p", p=32), "const")
        for i, nm in enumerate(("rw_w0", "rw_a0", "rw_kk", "rw_ka", "rw_rk", "rw_ln_w")):
            vrows(0, 75 + 8 * i, W[nm], 8)
        vrows(1, 0, W["rw_ln_b"], 8)
        for j in range(4):
            vrows(1, 8 + 16 * j, W["ml_conv_w"][j], 16)
        vrows(1, 72, W["ml_conv_b"], 16)
        vrows(1, 88, W["ml_norm_w"], 8)
        P.dma("sync", ifb[:, 0:1], W["ml_i_b"].rearrange("(p c) -> p c", c=1), "const",
              allow_slow_non_contiguous=True)
        P.dma("sync", ifb[:, 1:2], W["ml_f_b"].rearrange("(p c) -> p c", c=1), "const",
              allow_slow_non_contiguous=True)
        P.dma("pool", w2a2[0:64, :], W["rw_w2"], "const")
        P.dma("pool", w2a2[64:128, :], W["rw_a2"], "const")
        P.dma("pool", g2a[:, :], W["rw_g2"][0:128, :], "const")
        P.dma("pool", g2b[:, :], W["rw_g2"][128:160, :], "const")
        P.dma("sync", stgA[0:4, 0:RWC], st_shift, "const")
        P.dma("sync", stgB[0:12, 0:2048], st_conv.rearrange("s j c -> (s j) c"), "const")
        P.dma("sync", mrow[0:4, 0:4], st_m.rearrange("s h -> h s"), "const", allow_slow_non_contiguous=True)
        P.seal_key("const")

        P.ins("dve", "tensor_copy", out=identb[:], in_=cstg[:, C_ID:C_ID + 128])
        P.ins("dve", "tensor_copy", out=identf[:], in_=cstg[:, C_ID:C_ID + 128])
        bA = bank()
        bB = bank()
        P.mm(bA[:, 0:123], stgV[0][0:123, :], identf[0:123, 0:123])
        P.mm(bB[:, 0:96], stgV[1][0:96, :], identf[0:96, 0:96])
        P.ins("dve", "tensor_copy", out=gcols[:].rearrange("p a b -> p (a b)"), in_=bA[:, 0:48])
        P.ins("dve", "tensor_copy", out=mucols[:, :], in_=bA[:, 48:75])
        P.ins("dve", "tensor_copy", out=rwc[:, 0:6, :].rearrange("p a b -> p (a b)"), in_=bA[:, 75:123])
        P.ins("dve", "tensor_copy", out=rwc[:, 6, :], in_=bB[:, 0:8])
        P.ins("dve", "tensor_copy", out=cwc[:].rearrange("p c j -> p j c"),
              in_=bB[:, 8:72].rearrange("p (j c) -> p j c", j=4))
        P.ins("dve", "tensor_copy", out=cbc[:, :], in_=bB[:, 72:88])
        P.ins("dve", "tensor_copy", out=nwc[:, :], in_=bB[:, 88:96])
        P.ins("dve", "tensor_copy", out=mask5[:].rearrange("p a b -> p (a b)"), in_=cstg[:, C_M5:C_M5 + 320])
        P.ins("dve", "tensor_copy", out=onesbd[:], in_=cstg[:, C_OBD:C_OBD + 128])
        P.ins("dve", "tensor_copy", out=r64[:], in_=cstg[:, C_R64:C_R64 + 512])
        P.ins("dve", "tensor_copy", out=r16[:], in_=cstg[:, C_R16:C_R16 + 512])
        P.ins("dve", "tensor_copy", out=one512[:], in_=cstg[:, C_ONE:C_ONE + 512])
        P.ins("dve", "tensor_copy", out=mlmask[:], in_=cstg[:, C_ML:C_ML + 128])
        P.ins("dve", "tensor_copy", out=sel4[:], in_=cstg[0:4, C_SEL:C_SEL + 512])
        P.ins("pool", "memset", ap=carry[:], constant=0.0)
        P.ins("pool", "memset", ap=ccar[:], constant=0.0)
        P.ins("pool", "memset", ap=mrow[0:4, 4:8], constant=0.0)
        P.ins("pool", "memset", ap=small[:], constant=0.0)
        b = bank()
        for g in range(27):
            n = 128 if g < 26 else 32
            P.mm(b[:n, g * 4:g * 4 + 4], stgA[0:4, g * 128:g * 128 + n], identf[0:4, 0:4])
        P.ins("dve", "tensor_copy", out=carry[:, 0:26, 0:4], in_=b[:, 0:104].rearrange("p (g s) -> p g s", s=4))
        P.ins("dve", "tensor_copy", out=carry[0:32, 26, 0:4], in_=b[0:32, 104:108])
        b = bank()
        for g in range(16):
            P.mm(b[:, g * 12:g * 12 + 12], stgB[0:12, g * 128:g * 128 + 128], identf[0:12, 0:12])
        P.ins("dve", "tensor_copy", out=ccar[:, :, 0:4, :],
              in_=b[:, 0:192].rearrange("p (g s j) -> p g s j", s=4, j=3))

        def rmsnorm_T(gi, subt):
            for st, n in subt:
                ssq = small[:n, 0:1]
                rstd = small[:n, 1:2]
                P.ins("act", "activation", out=xsb[:n, :], in_=xres[:n, st, :], func=AF.Square, accum_out=ssq)
                P.ins("dve", "tensor_scalar", out=rstd, in0=ssq, scalar1=1.0 / D, scalar2=1e-6,
                      op0=ALU.mult, op1=ALU.add)
                P.ins("act", "activation", out=rstd, in_=rstd, func=AF.Sqrt)
                P.ins("dve", "reciprocal", out=rstd, in_=rstd)
                P.ins("dve", "tensor_scalar", out=xsb[:n, :], in0=xres[:n, st, :], scalar1=rstd, scalar2=None,
                      op0=ALU.mult)
                for c0 in range(0, 16, 4):
                    bb = bank()
                    for c in range(4):
                        P.mm(bb[:, c * 128:c * 128 + n], xsb[:n, (c0 + c) * 128:(c0 + c + 1) * 128], identb[:n, :n])
                    P.ins("dve", "tensor_tensor",
                          out=xnT[:, c0:c0 + 4, st * 128:st * 128 + n],
                          in0=bb[:, :].rearrange("p (c t) -> p c t", t=128)[:, :, 0:n],
                          in1=gcols[:, gi, c0:c0 + 4].unsqueeze(2).to_broadcast([128, 4, n]), op=ALU.mult)

        def ffn(pref, gi, NT, subt):
            rmsnorm_T(gi, subt)
            chk(pref + "_norm", cur["ti"])
            hT = carve(0, [NFC, 512], BF16)
            Wg = W[pref + "_w_gate"].rearrange("(kc p) f -> p kc f", p=128)
            Wu = W[pref + "_w_up"].rearrange("(kc p) f -> p kc f", p=128)
            Wd = W[pref + "_w_down"].rearrange("(kc p) d -> p kc d", p=128)
            for f0 in range(0, DFF, 256):
                fw = min(256, DFF - f0)
                wg = wload(Wg[:, :, f0:f0 + fw], [16, fw])
                wu = wload(Wu[:, :, f0:f0 + fw], [16, fw])
                for j in range(fw // 128):
                    fc = f0 // 128 + j
                    bg = bank()
                    bu = bank()
                    for kc in range(16):
                        P.mm(bg[:, :NT], wg[:, kc, j * 128:(j + 1) * 128], xnT[:, kc, :NT], start=kc == 0, stop=kc == 15)
                    for kc in range(16):
                        P.mm(bu[:, :NT], wu[:, kc, j * 128:(j + 1) * 128], xnT[:, kc, :NT], start=kc == 0, stop=kc == 15)
                    sg = sgb[:, 0, :NT]
                    P.ins("act", "activation", out=sg, in_=bg[:, :NT], func=AF.Silu)
                    P.ins("dve", "tensor_tensor", out=hT[:, fc, :NT], in0=sg, in1=bu[:, :NT], op=ALU.mult)
            chk(pref + "_gu", cur["ti"])
            for cb in range(4):
                bks = [bank() for _ in subt]
                for k0 in range(0, NFC, 8):
                    nk = min(8, NFC - k0)
                    wd = wload(Wd[:, k0:k0 + nk, cb * 512:(cb + 1) * 512], [nk, 512])
                    for si, (st, n) in enumerate(subt):
                        for k in range(nk):
                            P.mm(bks[si][:n, :], hT[:, k0 + k, st * 128:st * 128 + n], wd[:, k, :],
                                 start=(k0 + k == 0), stop=(k0 + k == NFC - 1))
                chk(pref + "_dmm%d" % cb, cur["ti"])
                for si, (st, n) in enumerate(subt):
                    xs_ = xres[:n, st, cb * 512:(cb + 1) * 512]
                    P.ins("dve", "scalar_tensor_tensor", out=xs_, in0=bks[si][:n, :], scalar=0.5, in1=xs_,
                          op0=ALU.mult, op1=ALU.add)
                chk(pref + "_dev%d" % cb, cur["ti"])

        Win = W["w_in"].rearrange("(kc p) c -> p kc c", p=128)

        def proj_fm(col0, ncols, NT, outs):
            wt = wload(Win[:, :, col0:col0 + ncols], [16, ncols])
            for j in range((ncols + 127) // 128):
                m = min(128, ncols - j * 128)
                bb = bank()
                for kc in range(16):
                    P.mm(bb[:m, :NT], wt[:, kc, j * 128:j * 128 + m], xnT[:, kc, :NT], start=kc == 0, stop=kc == 15)
                outs(j, bb, m)

        def rwkv(NT, segs, C, last_tile):
            nseg = len(segs)
            chunks = []
            for (c0, ln, sq) in segs:
                for cc in range(0, ln, C):
                    chunks.append((c0 + cc, sq, cc == 0, cc + C >= ln))
            nch = len(chunks)
            rmask = r64 if C == 64 else r16
            NJ = 6 if C == 64 else 4
            o = 0

            def cv(shape, dt):
                nonlocal o
                v = carve(o, shape, dt)
                o += ((int(np.prod(shape)) * _dsz(dt) + 3) // 4) * 4
                return v
            T = [cv([514], F32) for _ in range(8)]
            YQ = carve(2 * 2056, [2, 8, 64], F32)
            YSQ = carve(4 * 2056, [2, 8, 64], F32)
            YN = carve(6 * 2056, [2, 8, 64], BF16)
            OPS = cv([2, 7, 512], BF16)
            GF = cv([2, 512], F32)
            BON = cv([2, 512], F32)
            WC = cv([2, 8], F32)
            LST = cv([6, 16], F32)
            SC = [cv([2, 5, 64], BF16) for _ in range(2)]
            TM3 = [cv([2, 3, 64], BF16) for _ in range(2)]
            RF = [cv([2, 64], BF16) for _ in range(2)]
            ao = [4 * 2056]

            def av_(shape, dt):
                v = carve(ao[0], shape, dt)
                ao[0] += ((int(np.prod(shape)) * _dsz(dt) + 3) // 4) * 4
                assert ao[0] <= 8 * 2056
                return v
            SC += [av_([2, 5, 64], BF16) for _ in range(2)]
            TM3 += [av_([2, 3, 64], BF16) for _ in range(2)]
            RF += [av_([2, 64], BF16) for _ in range(2)]
            QP = [[av_([2, 2, 64], BF16) for _ in range(2)] for _ in range(2)]
            RR = [[av_([2, 64], BF16) for _ in range(2)] for _ in range(2)]
            XU = cv([2, 2, 64], BF16)
            TL = cv([512], BF16)
            SG1 = cv([512], BF16)
            SG2 = cv([512], BF16)
            LTMP = T[7]

            def shift_u(bb, m, g, dst, NTl=NT):
                praw = T[0]
                dd = T[4]
                P.ins("act", "activation", out=praw[:m, 1:1 + NT], in_=bb[:m, :NT], func=AF.Copy)
                P.ins("dve", "tensor_tensor", out=dd[:m, 0:NT], in0=praw[:m, 0:NT], in1=praw[:m, 1:1 + NT],
                      op=ALU.subtract)
                for (c0, ln, sq) in segs:
                    P.ins("dve", "tensor_tensor", out=dd[:m, c0:c0 + 1], in0=carry[:m, g, sq:sq + 1],
                          in1=praw[:m, 1 + c0:2 + c0], op=ALU.subtract)
                P.ins("dve", "scalar_tensor_tensor", out=dst[:m, 0:NT], in0=dd[:m, 0:NT], scalar=mucols[:m, g:g + 1],
                      in1=praw[:m, 1:1 + NT], op0=ALU.mult, op1=ALU.add)
                for (c0, ln, sq) in segs:
                    P.ins("act", "activation", out=carry[:m, g, sq:sq + 1], in_=praw[:m, c0 + ln:c0 + ln + 1],
                          func=AF.Copy)

            def lora_out(j, bb, m):
                if m == 32:
                    j = 2
                if j == 0:
                    shift_u(bb, 128, 24, LTMP)
                    P.ins("act", "activation", out=TL[0:64, :NT], in_=LTMP[0:64, :NT], func=AF.Tanh)
                    P.ins("act", "activation", out=TL[64:128, :NT], in_=LTMP[64:128, :NT], func=AF.Copy)
                elif j == 1:
                    shift_u(bb, 128, 25, LTMP)
                    P.ins("act", "activation", out=SG1[:, :NT], in_=LTMP[:, :NT], func=AF.Sigmoid)
                else:
                    shift_u(bb, 32, 26, LTMP)
                    P.ins("act", "activation", out=SG2[0:32, :NT], in_=LTMP[0:32, :NT], func=AF.Sigmoid)
            proj_fm(O_LW, 256, NT, lora_out)
            proj_fm(O_LW + 256, 32, NT, lora_out)
            chk("rw_lora", cur["ti"])

            hk = [slice(0, 64), slice(64, 128)]
            hs = [slice(0, C), slice(64, 64 + C)]

            for q in range(4):
                for pi in range(2):
                    pp = 2 * q + pi
                    At, Rt, Bt, Kt, Bh, Kh, Vb = [OPS[:, pi, i, :] for i in range(7)]
                    ur, uk, uv = T[1], T[2], T[3]
                    wt1 = wload(Win[:, :, O_R + pp * 128:O_R + pp * 128 + 128], [16, 128])
                    wt2 = wload(Win[:, :, O_K + pp * 128:O_K + pp * 128 + 128], [16, 128])
                    wt3 = wload(Win[:, :, O_V + pp * 128:O_V + pp * 128 + 128], [16, 128])
                    for wt, g, dst in ((wt1, pp, ur), (wt2, 8 + pp, uk), (wt3, 16 + pp, uv)):
                        bb = bank()
                        for kc in range(16):
                            P.mm(bb[:, :NT], wt[:, kc, :], xnT[:, kc, :NT], start=kc == 0, stop=kc == 15)
                        shift_u(bb, 128, g, dst)
                    cols = slice(pp * 128, pp * 128 + 128)
                    ba = bank()
                    P.mm(ba[:, :NT], w2a2[64:128, cols], TL[64:128, :NT])
                    av = T[5]
                    P.ins("act", "activation", out=av[:, :NT], in_=ba[:, :NT], func=AF.Sigmoid, bias=rwc[:, 1, pp:pp + 1])
                    bw = bank()
                    P.mm(bw[:, :NT], w2a2[0:64, cols], TL[0:64, :NT])
                    sgm = T[6]
                    P.ins("act", "activation", out=sgm[:, :NT], in_=bw[:, :NT], func=AF.Sigmoid, bias=rwc[:, 0, pp:pp + 1])
                    bg_ = bank()
                    P.mm(bg_[:, :NT], g2a[:, cols], SG1[:, :NT], start=True, stop=False)
                    P.mm(bg_[:, :NT], g2b[0:32, cols], SG2[0:32, :NT], start=False, stop=True)
                    P.ins("act", "activation", out=GF[:, pi, :NT], in_=bg_[:, :NT], func=AF.Copy)
                    kkr = T[0]
                    P.ins("dve", "tensor_scalar", out=kkr[:, :NT], in0=uk[:, :NT], scalar1=rwc[:, 2, pp:pp + 1],
                          scalar2=None, op0=ALU.mult)
                    sq_ = T[4]
                    P.ins("act", "activation", out=sq_[:, :NT], in_=kkr[:, :NT], func=AF.Square)
                    bs = bank()
                    P.mm(bs[:, :NT], onesbd[:, :], sq_[:, :NT])
                    P.ins("dve", "tensor_scalar", out=sq_[:, :NT], in0=bs[:, :NT], scalar1=1e-24, scalar2=None,
                          op0=ALU.max)
                    P.ins("act", "activation", out=sq_[:, :NT], in_=sq_[:, :NT], func=AF.Sqrt)
                    P.ins("dve", "reciprocal", out=sq_[:, :NT], in_=sq_[:, :NT])
                    P.ins("dve", "tensor_tensor", out=kkr[:, :NT], in0=kkr[:, :NT], in1=sq_[:, :NT], op=ALU.mult)
                    P.ins("dve", "tensor_scalar", out=sq_[:, :NT], in0=av[:, :NT], scalar1=-1.0,
                          scalar2=rwc[:, 3, pp:pp + 1], op0=ALU.add, op1=ALU.mult)
                    P.ins("dve", "scalar_tensor_tensor", out=uk[:, :NT], in0=sq_[:, :NT], scalar=1.0, in1=uk[:, :NT],
                          op0=ALU.add, op1=ALU.mult)
                    P.ins("dve", "scalar_tensor_tensor", out=sq_[:, :NT], in0=ur[:, :NT], scalar=rwc[:, 4, pp:pp + 1],
                          in1=uk[:, :NT], op0=ALU.mult, op1=ALU.mult)
                    bs2 = bank()
                    P.mm(bs2[:, :NT], onesbd[:, :], sq_[:, :NT])
                    P.ins("dve", "tensor_tensor", out=BON[:, pi, :NT], in0=bs2[:, :NT], in1=uv[:, :NT], op=ALU.mult)
                    P.ins("dve", "tensor_scalar", out=BON[:, pi, :NT], in0=BON[:, pi, :NT],
                          scalar1=rwc[:, 6, pp:pp + 1], scalar2=None, op0=ALU.add)
                    P.ins("dve", "tensor_tensor", out=av[:, :NT], in0=kkr[:, :NT], in1=av[:, :NT], op=ALU.mult)
                    cs = T[7]
                    P.ins("dve", "tensor_tensor_scan", out=cs[:, :NT], data0=rmask[:, :NT], data1=sgm[:, :NT],
                          initial=0.0, op0=ALU.mult, op1=ALU.add)
                    P.ins("dve", "tensor_tensor", out=sgm[:, :NT], in0=cs[:, :NT], in1=sgm[:, :NT], op=ALU.subtract)
                    P.ins("act", "activation", out=sgm[:, :NT], in_=sgm[:, :NT], func=AF.Exp, scale=KAPPA)
                    P.ins("dve", "scalar_tensor_tensor", out=At[:, :NT], in0=kkr[:, :NT], scalar=-1.0, in1=sgm[:, :NT],
                          op0=ALU.mult, op1=ALU.mult)
                    P.ins("act", "activation", out=sgm[:, :NT], in_=cs[:, :NT], func=AF.Exp, scale=KAPPA)
                    P.ins("dve", "tensor_tensor", out=Rt[:, :NT], in0=ur[:, :NT], in1=sgm[:, :NT], op=ALU.mult)
                    P.ins("act", "activation", out=sgm[:, :NT], in_=cs[:, :NT], func=AF.Exp, scale=-KAPPA)
                    P.ins("dve", "tensor_tensor", out=Bt[:, :NT], in0=av[:, :NT], in1=sgm[:, :NT], op=ALU.mult)
                    P.ins("dve", "tensor_tensor", out=Kt[:, :NT], in0=uk[:, :NT], in1=sgm[:, :NT], op=ALU.mult)
                    csv = cs[:, 0:nch * C].rearrange("p (c t) -> p c t", t=C)
                    P.ins("act", "activation", out=WC[:, pi, 0:nch], in_=csv[:, :, C - 1], func=AF.Exp, scale=KAPPA)
                    P.ins("dve", "tensor_tensor", out=sgm[:, 0:nch * C].rearrange("p (c t) -> p c t", t=C),
                          in0=csv[:, :, C - 1:C].to_broadcast([128, nch, C]), in1=csv, op=ALU.subtract)
                    P.ins("act", "activation", out=sgm[:, :NT], in_=sgm[:, :NT], func=AF.Exp, scale=KAPPA)
                    P.ins("dve", "tensor_tensor", out=Bh[:, :NT], in0=av[:, :NT], in1=sgm[:, :NT], op=ALU.mult)
                    P.ins("dve", "tensor_tensor", out=Kh[:, :NT], in0=uk[:, :NT], in1=sgm[:, :NT], op=ALU.mult)
                    P.ins("act", "activation", out=Vb[:, :NT], in_=uv[:, :NT], func=AF.Copy)
                    chk("rw_pre", cur["ti"])
                    if pi == 1:
                        chk("rw_pre2", cur["ti"])

                def load_state(ci):
                    c0, sq, first, last = chunks[ci]
                    if not (first and NT == 80):
                        return
                    if sq < 4:
                        for pi in range(2):
                            pp = 2 * q + pi
                            stw = T[pi][:, 0:64]
                            P.dma("sync", stw, st_wkv[sq, 2 * pp:2 * pp + 2].rearrange("h v k -> (h v) k"),
                                  f"stw{pi}")
                            bb = bank()
                            for h in range(2):
                                P.mm(bb[hk[h], 0:64], T[pi][hk[h], 0:64], identf[hk[h], hk[h]],
                                     tp=(h * 64, h * 64))
                            P.ins("dve", "tensor_copy", out=Pst[:, pp, :], in_=bb[:, 0:64])
                            P.ins("act", "activation", out=Pbf[:, pp, :], in_=Pst[:, pp, :], func=AF.Copy)
                    else:
                        for pi in range(2):
                            pp = 2 * q + pi
                            P.ins("dve", "memset", ap=Pst[:, pp, :], constant=0.0)
                            P.ins("dve", "memset", ap=Pbf[:, pp, :], constant=0.0)

                def part_A(ci):
                    c0, sq, first, last = chunks[ci]
                    rg = ci % 4
                    cc = slice(c0, c0 + C)
                    for pi in range(2):
                        At, Rt, Bt, Kt, Bh, Kh, Vb = [OPS[:, pi, i, :] for i in range(7)]
                        bsc = bank()
                        for h in range(2):
                            tp = (h * 64, h * 64)
                            for i, (l_, r_) in enumerate(((Bt, At), (At, Bt), (Kt, At), (Bt, Rt), (Kt, Rt))):
                                P.mm(bsc[hs[h], i * 64:i * 64 + C], l_[hk[h], cc], r_[hk[h], cc], tp=tp)
                        for h in (range(1) if C == 64 else range(2)):
                            rws = slice(0, 128) if C == 64 else hs[h]
                            P.ins("dve", "tensor_tensor", out=SC[rg][rws, pi, :, 0:C],
                                  in0=bsc[rws, 0:320].rearrange("p (i t) -> p i t", t=64)[:, :, 0:C],
                                  in1=mask5[rws, :, 0:C], op=ALU.mult)
                        btm = bank()
                        for h in range(2):
                            tp = (h * 64, h * 64)
                            for i, src in enumerate((Vb, Bh, Kh)):
                                P.mm(btm[hs[h], i * 64:i * 64 + 64], src[hk[h], cc], identb[hk[h], hk[h]], tp=tp)
                        for h in (range(1) if C == 64 else range(2)):
                            rws = slice(0, 128) if C == 64 else hs[h]
                            P.ins("act", "activation", out=TM3[rg][rws, pi, :, :],
                                  in_=btm[rws, 0:192].rearrange("p (i t) -> p i t", t=64), func=AF.Copy)

                def part_B(ci, j):
                    rg4 = ci % 4
                    rg = ci % 2
                    for pi in range(2):
                        if j == 0:
                            Qc = SC[rg4][:, pi, 0, :]
                            Pc = SC[rg4][:, pi, 1, :]
                        else:
                            Qc = QP[rg][(j - 1) % 2][:, pi, 0, :]
                            Pc = QP[rg][(j - 1) % 2][:, pi, 1, :]
                        dstR = RF[rg4][:, pi, :] if j == NJ - 1 else RR[rg][j % 2][:, pi, :]
                        if j == 0:
                            for hh in range(2):
                                P.ins("dve", "tensor_tensor", out=dstR[hs[hh], 0:C], in0=Qc[hs[hh], 0:C],
                                      in1=identb[hs[hh], hh * 64:hh * 64 + C], op=ALU.add)
                        else:
                            Rp = RR[rg][(j - 1) % 2][:, pi, :]
                            bq = bank()
                            for h in range(2):
                                P.mm(bq[hs[h], 0:C], Pc[hs[h], 0:C], Rp[hs[h], 0:C], tp=(h * 64, h * 64))
                            for h in (range(1) if C == 64 else range(2)):
                                rws = slice(0, 128) if C == 64 else hs[h]
                                P.ins("dve", "tensor_tensor", out=dstR[rws, 0:C], in0=bq[rws, 0:C], in1=Rp[rws, 0:C],
                                      op=ALU.add)
                        if j < NJ - 1:
                            lastsq = j == NJ - 2
                            bq2 = bank()
                            for h in range(2):
                                tp = (h * 64, h * 64)
                                if not lastsq:
                                    P.mm(bq2[hs[h], 0:C], Pc[hs[h], 0:C], Qc[hs[h], 0:C], tp=tp)
                                P.mm(bq2[hs[h], 64:64 + C], Qc[hs[h], 0:C], Pc[hs[h], 0:C], tp=tp)
                            for h in (range(1) if C == 64 else range(2)):
                                rws = slice(0, 128) if C == 64 else hs[h]
                                if lastsq:
                                    P.ins("act", "activation", out=QP[rg][j % 2][rws, pi, 1, 0:C],
                                          in_=bq2[rws, 64:64 + C], func=AF.Copy)
                                else:
                                    P.ins("act", "activation", out=QP[rg][j % 2][rws, pi, :, 0:C],
                                          in_=bq2[rws, 0:128].rearrange("p (i t) -> p i t", t=64)[:, :, 0:C],
                                          func=AF.Copy)

                def part_C1(ci):
                    c0, sq, first, last = chunks[ci]
                    rg = ci % 4
                    cc = slice(c0, c0 + C)
                    load_state(ci)
                    for pi in range(2):
                        pp = 2 * q + pi
                        At = OPS[:, pi, 0, :]
                        X0 = XU[:, pi, 0, :]
                        bx = bank()
                        for h in range(2):
                            tp = (h * 64, h * 64)
                            P.mm(bx[hs[h], 0:64], At[hk[h], cc], Pbf[hk[h], pp, :], start=True, stop=False, tp=tp)
                            P.mm(bx[hs[h], 0:64], SC[rg][hs[h], pi, 2, 0:C], TM3[rg][hs[h], pi, 0, :],
                                 start=False, stop=True, tp=tp)
                        for h in (range(1) if C == 64 else range(2)):
                            rws = slice(0, 128) if C == 64 else hs[h]
                            P.ins("act", "activation", out=X0[rws, :], in_=bx[rws, 0:64], func=AF.Copy)

                def part_C2(ci):
                    rg = ci % 4
                    for pi in range(2):
                        X0 = XU[:, pi, 0, :]
                        U = XU[:, pi, 1, :]
                        bu_ = bank()
                        for h in range(2):
                            tp = (h * 64, h * 64)
                            P.mm(bu_[hs[h], 0:64], RF[rg][hs[h], pi, 0:C], X0[hs[h], :], tp=tp)
                        for h in (range(1) if C == 64 else range(2)):
                            rws = slice(0, 128) if C == 64 else hs[h]
                            P.ins("dve", "tensor_copy", out=U[rws, :], in_=bu_[rws, 0:64])

                def part_C3(ci):
                    c0, sq, first, last = chunks[ci]
                    rg = ci % 4
                    cc = slice(c0, c0 + C)
                    for pi in range(2):
                        pp = 2 * q + pi
                        Rt = OPS[:, pi, 1, :]
                        U = XU[:, pi, 1, :]
                        by = bank()
                        for h in range(2):
                            tp = (h * 64, h * 64)
                            P.mm(by[hs[h], 0:64], Rt[hk[h], cc], Pbf[hk[h], pp, :], start=True, stop=False, tp=tp)
                            P.mm(by[hs[h], 0:64], SC[rg][hs[h], pi, 3, 0:C], U[hs[h], :], start=False, stop=False, tp=tp)
                            P.mm(by[hs[h], 0:64], SC[rg][hs[h], pi, 4, 0:C], TM3[rg][hs[h], pi, 0, :],
                                 start=False, stop=True, tp=tp)
                        for h in (range(1) if C == 64 else range(2)):
                            rws = slice(0, 128) if C == 64 else hs[h]
                            P.ins("act", "activation", out=YQ[rws, pi, ci, :], in_=by[rws, 0:64], func=AF.Copy)
                        bp = bank()
                        for h in range(2):
                            tp = (h * 64, h * 64)
                            P.mm(bp[hk[h], 0:64], TM3[rg][hs[h], pi, 1, :], U[hs[h], :], start=True, stop=False, tp=tp)
                            P.mm(bp[hk[h], 0:64], TM3[rg][hs[h], pi, 2, :], TM3[rg][hs[h], pi, 0, :],
                                 start=False, stop=True, tp=tp)
                        P.ins("dve", "scalar_tensor_tensor", out=Pst[:, pp, :], in0=Pst[:, pp, :],
                              scalar=WC[:, pi, ci:ci + 1], in1=bp[:, 0:64], op0=ALU.mult, op1=ALU.add)
                        P.ins("act", "activation", out=Pbf[:, pp, :], in_=Pst[:, pp, :], func=AF.Copy)
                        if last and (sq < 4 or last_tile):
                            bb = bank()
                            for h in range(2):
                                P.mm(bb[hk[h], 0:64], Pst[hk[h], pp, :], identf[hk[h], hk[h]], tp=(h * 64, h * 64))
                            so = T[0][:, 64 * pi:64 * pi + 64]
                            P.ins("dve", "tensor_copy", out=so, in_=bb[:, 0:64])
                            P.dma("sync", o_wkv[sq, 2 * pp:2 * pp + 2].rearrange("h v k -> (h v) k"), so, f"owkv{pi}")

                def c_steps(grp):
                    st_ = []
                    for ci in grp:
                        st_ += [lambda ci=ci: part_C1(ci), lambda ci=ci: part_C2(ci), lambda ci=ci: part_C3(ci)]
                    return st_

                pend = []
                for g0 in range(0, nch, 2):
                    grp = list(range(g0, min(nch, g0 + 2)))
                    for ci in grp:
                        part_A(ci)
                    for j in range(NJ):
                        for ci in grp:
                            part_B(ci, j)
                        if pend:
                            pend.pop(0)()
                    while pend:
                        pend.pop(0)()
                    pend = c_steps(grp)
                while pend:
                    pend.pop(0)()
                chk("rw_dep", cur["ti"])

                ng = 2 * nch
                yq = YQ[:, :, 0:nch, :]
                s1 = LST[:, 0, 0:ng].rearrange("p (a b) -> p a b", a=2)
                s2 = LST[:, 1, 0:ng].rearrange("p (a b) -> p a b", a=2)
                mean = LST[:, 2, 0:ng].rearrange("p (a b) -> p a b", a=2)
                var = LST[:, 3, 0:ng].rearrange("p (a b) -> p a b", a=2)
                P.ins("dve", "tensor_reduce", out=s1, in_=yq, axis=AX.X, op=ALU.add)
                P.ins("act", "activation", out=YSQ[:, :, 0:nch, :], in_=yq, func=AF.Square)
                P.ins("dve", "tensor_reduce", out=s2, in_=YSQ[:, :, 0:nch, :], axis=AX.X, op=ALU.add)
                P.ins("dve", "tensor_scalar", out=mean, in0=s1, scalar1=1.0 / 64, scalar2=None, op0=ALU.mult)
                P.ins("dve", "tensor_tensor", out=var, in0=mean, in1=mean, op=ALU.mult)
                P.ins("dve", "scalar_tensor_tensor", out=var, in0=s2, scalar=1.0 / 64, in1=var, op0=ALU.mult,
                      op1=ALU.subtract)
                P.ins("dve", "tensor_scalar", out=var, in0=var, scalar1=RW_EPS, scalar2=None, op0=ALU.add)
                P.ins("act", "activation", out=var, in_=var, func=AF.Sqrt)
                P.ins("dve", "reciprocal", out=var, in_=var)
                P.ins("dve", "tensor_tensor", out=YSQ[:, :, 0:nch, :], in0=yq,
                      in1=mean.unsqueeze(3).to_broadcast([128, 2, nch, 64]), op=ALU.subtract)
                P.ins("dve", "tensor_tensor", out=YN[:, :, 0:nch, :], in0=YSQ[:, :, 0:nch, :],
                      in1=var.unsqueeze(3).to_broadcast([128, 2, nch, 64]), op=ALU.mult)
                for pi in range(2):
                    pp = 2 * q + pi
                    bt_ = bank()
                    for ci, (c0, sq, first, last) in enumerate(chunks):
                        for h in range(2):
                            P.mm(bt_[hk[h], c0:c0 + C], YN[hs[h], pi, ci, :], identb[hs[h], h * 64:h * 64 + C],
                                 tp=(h * 64, h * 64))
                    zt = T[0]
                    P.ins("dve", "scalar_tensor_tensor", out=zt[:, :NT], in0=bt_[:, :NT], scalar=rwc[:, 5, pp:pp + 1],
                          in1=BON[:, pi, :NT], op0=ALU.mult, op1=ALU.add)
                    P.ins("dve", "tensor_tensor", out=yrwT[:, pp, :NT], in0=zt[:, :NT], in1=GF[:, pi, :NT], op=ALU.mult)

        def mlstm(NT, segs, L, last_tile):
            nseg = len(segs)
            chunks = []
            for si, (c0, ln, sq) in enumerate(segs):
                for cc in range(0, ln, L):
                    chunks.append((c0 + cc, sq, cc == 0, cc + L >= ln, len(chunks), si))
            nch = len(chunks)
            o = 0

            def cv(shape, dt):
                nonlocal o
                v = carve(o, shape, dt)
                o += ((int(np.prod(shape)) * _dsz(dt) + 3) // 4) * 4
                return v
            cbuf = cv([544], F32)
            acc = cv([512], F32)
            stmp = cv([512], F32)
            irow = cv([520], F32)
            frow = cv([520], F32)
            Frow = cv([520], F32)
            Mrow = cv([520], F32)
            gcol = cv([5, 12], F32)
            qT = cv([4, 512], BF16)
            kT = cv([4, 512], BF16)
            vext = cv([5, 2, 258], BF16)
            sigo = cv([4, 512], F32)
            NUM = cv([2, 258], F32)
            tqc = [cv([258], F32) for _ in range(2)]
            Eb = [cv([128], F32) for _ in range(2)]
            STb = [cv([128], BF16) for _ in range(2)]
            kTM = [cv([256], BF16) for _ in range(2)]
            hn = cv([2, 256], F32)
            hsq = cv([2, 256], F32)
            ynb = cv([2, 256], BF16)
            mpe = cv([8], F32)
            lst = cv([8, 2], F32)
            wcol = cv([16], F32)

            wt = wload(Win[:, :, O_MI:O_MI + 8], [16, 8])
            bi = bank()
            bf_ = bank()
            for kc in range(16):
                P.mm(bi[0:4, :NT], wt[:, kc, 0:4], xnT[:, kc, :NT], start=kc == 0, stop=kc == 15)
            for kc in range(16):
                P.mm(bf_[0:4, :NT], wt[:, kc, 4:8], xnT[:, kc, :NT], start=kc == 0, stop=kc == 15)
            P.ins("act", "activation", out=irow[0:4, :NT], in_=bi[0:4, :NT], func=AF.Identity, bias=ifb[0:4, 0:1])
            P.ins("act", "activation", out=frow[0:4, :NT], in_=bf_[0:4, :NT], func=AF.Sigmoid, bias=ifb[0:4, 1:2])
            P.ins("act", "activation", out=frow[0:4, :NT], in_=frow[0:4, :NT], func=AF.Ln)
            for si, (c0, ln, sq) in enumerate(segs):
                sl = slice(c0, c0 + ln)
                P.ins("dve", "tensor_tensor_scan", out=Frow[0:4, sl], data0=one512[0:4, 0:ln], data1=frow[0:4, sl],
                      initial=0.0, op0=ALU.mult, op1=ALU.add)
                P.ins("dve", "tensor_tensor", out=irow[0:4, sl], in0=irow[0:4, sl], in1=Frow[0:4, sl], op=ALU.subtract)
                mp = c0 + si + 1
                P.ins("dve", "tensor_copy", out=Mrow[0:4, mp - 1:mp], in_=mrow[0:4, sq:sq + 1])
                P.ins("dve", "tensor_tensor_scan", out=Mrow[0:4, mp:mp + ln], data0=one512[0:4, 0:ln],
                      data1=irow[0:4, sl], initial=mrow[0:4, sq:sq + 1], op0=ALU.mult, op1=ALU.max)
                P.ins("dve", "tensor_tensor", out=Frow[0:4, sl], in0=Frow[0:4, sl], in1=Mrow[0:4, mp:mp + ln], op=ALU.add)
                P.ins("dve", "tensor_copy", out=mrow[0:4, sq:sq + 1], in_=Frow[0:4, c0 + ln - 1:c0 + ln])
            for (c0, sq, first, last, slot, si) in chunks:
                bb = bank()
                mp = c0 + si + 1
                P.mm(bb[:L, 0:4], Mrow[0:4, mp:mp + L], identf[0:4, 0:4])
                P.mm(bb[:L, 4:8], irow[0:4, c0:c0 + L], identf[0:4, 0:4])
                P.mm(bb[:L, 8:12], Frow[0:4, c0:c0 + L], identf[0:4, 0:4])
                P.ins("dve", "tensor_copy", out=gcol[:L, slot, :], in_=bb[:L, 0:12])

            P.ins("pool", "memset", ap=vext[:, :, :, 256:257], constant=1.0)
            chk("ml_gate", cur["ti"])

            for hp in range(2):
                for which, base, dstT, scl in (("q", O_MQ, qT, 1.0), ("k", O_MQ + 1024, kT, 0.0625)):
                    def conv_out(j, bb, m, which=which, base=base, dstT=dstT, scl=scl):
                        g = (0 if which == "q" else 8) + hp * 4 + 2 * half + j
                        for si, (c0, ln, sq) in enumerate(segs):
                            pos = c0 + 3 * si
                            P.ins("act", "activation", out=cbuf[:, pos + 3:pos + 3 + ln], in_=bb[:, c0:c0 + ln],
                                  func=AF.Copy)
                            P.ins("dve", "tensor_copy", out=cbuf[:, pos:pos + 3], in_=ccar[:, g, sq, :])
                        ln = segs[0][1]
                        cvw = cbuf[:, 0:nseg * (ln + 3)].rearrange("p (s t) -> p s t", t=ln + 3)
                        av_ = acc[:, 0:nseg * ln].rearrange("p (s t) -> p s t", t=ln)
                        P.ins("dve", "tensor_scalar", out=av_, in0=cvw[:, :, 0:ln], scalar1=cwc[:, g, 0:1],
                              scalar2=cbc[:, g:g + 1], op0=ALU.mult, op1=ALU.add)
                        for tap in range(1, 4):
                            P.ins("dve", "scalar_tensor_tensor", out=av_, in0=cvw[:, :, tap:tap + ln],
                                  scalar=cwc[:, g, tap:tap + 1], in1=av_, op0=ALU.mult, op1=ALU.add)
                        for si, (c0, ln_, sq) in enumerate(segs):
                            pos = c0 + 3 * si
                            P.ins("act", "activation", out=ccar[:, g, sq, :], in_=cbuf[:, pos + ln_:pos + ln_ + 3],
                                  func=AF.Copy)
                        dd = dstT[:, 2 * half + j, :NT]
                        if scl == 1.0:
                            P.ins("act", "activation", out=dd, in_=acc[:, :NT], func=AF.Silu)
                        else:
                            P.ins("act", "activation", out=stmp[:, :NT], in_=acc[:, :NT], func=AF.Silu)
                            P.ins("dve", "tensor_scalar", out=dd, in0=stmp[:, :NT], scalar1=scl, scalar2=None,
                                  op0=ALU.mult)
                    for half in range(2):
                        proj_fm(base + hp * 512 + half * 256, 256, NT, conv_out)
                def po_out(j, bb, m):
                    P.ins("act", "activation", out=sigo[:, 2 * half + j, :NT], in_=bb[:, :NT], func=AF.Sigmoid)
                for half in range(2):
                    proj_fm(O_MO + hp * 512 + half * 256, 256, NT, po_out)
                for hh in range(2):
                    h = 2 * hp + hh
                    wv = wload(Win[:, :, O_MV + h * 256:O_MV + h * 256 + 256], [16, 256])
                    for (c0, sq, first, last, slot, si) in chunks:
                        bb = bank()
                        for kc in range(16):
                            P.mm(bb[:L, 0:256], xnT[:, kc, c0:c0 + L], wv[:, kc, :], start=kc == 0, stop=kc == 15)
                        P.ins("act", "activation", out=vext[:L, slot, hh, 0:256], in_=bb[:L, 0:256], func=AF.Copy)
                chk("ml_proj", cur["ti"])

                for (c0, sq, first, last, slot, si) in chunks:
                    cc = slice(c0, c0 + L)
                    mp = c0 + si + 1
                    if first and NT == 80:
                        if sq < 4:
                            for hh in range(2):
                                h = 2 * hp + hh
                                P.dma("sync", Cst[:, h, :, 0:256],
                                      st_C[sq, h].rearrange("(dc p) v -> p dc v", p=128), f"stc{hh}")
                                P.dma("sync", Cst[:, h, :, 256:257],
                                      st_n[sq, h].rearrange("(dc p one) -> p dc one", p=128, one=1), f"stc{hh}",
                                      allow_slow_non_contiguous=True)
                        else:
                            for hh in range(2):
                                h = 2 * hp + hh
                                P.ins("dve", "memset", ap=Cst[:, h, :, :], constant=0.0)
                        for hh in range(2):
                            h = 2 * hp + hh
                            P.ins("act", "activation", out=Cbf[:, h, :, 0:257], in_=Cst[:, h, :, :], func=AF.Copy)
                    for hh in range(2):
                        h = 2 * hp + hh
                        r2 = (slot * 2 + hh) % 2
                        bS = bank()
                        for dc in range(2):
                            P.mm(bS[:L, 0:L], kT[:, 2 * hh + dc, cc], qT[:, 2 * hh + dc, cc], start=dc == 0, stop=dc == 1)
                        bM = bank()
                        P.mm(bM[:, 0:L + 1], sel4[0:4, h * 128:h * 128 + 128], Mrow[0:4, mp - 1:mp + L])
                        P.ins("act", "activation", out=mpe[:, 0:1], in_=bM[:, 0:1], func=AF.Copy)
                        P.ins("act", "activation", out=mpe[:, 1:2], in_=bM[:, L:L + 1], func=AF.Copy)
                        E = Eb[r2]
                        P.ins("act", "activation", out=E[:L, 0:L], in_=bM[:L, 1:L + 1], func=AF.Exp, scale=-1.0,
                              bias=gcol[:L, slot, 4 + h:5 + h])
                        P.ins("pool", "tensor_tensor", out=E[:L, 0:L], in0=E[:L, 0:L], in1=mlmask[:L, 0:L], op=ALU.mult)
                        ST = STb[r2]
                        P.ins("dve", "tensor_tensor", out=ST[:L, 0:L], in0=bS[:L, 0:L], in1=E[:L, 0:L], op=ALU.mult)
                        P.ins("act", "activation", out=wcol[:L, 0:1], in_=gcol[:L, slot, h:h + 1], func=AF.Exp,
                              scale=-1.0, bias=mpe[:L, 0:1])
                        P.ins("act", "activation", out=wcol[:, 1:2], in_=mpe[:, 1:2], func=AF.Exp, scale=-1.0,
                              bias=mpe[:, 0:1])
                        P.ins("act", "activation", out=wcol[:L, 2:3], in_=mpe[:L, 1:2], func=AF.Exp, scale=-1.0,
                              bias=gcol[:L, slot, 4 + h:5 + h])
                        bQ = bank()
                        for dc in range(2):
                            P.mm(bQ[:L, 0:257], qT[:, 2 * hh + dc, cc], Cbf[:, h, dc, 0:257], start=dc == 0, stop=dc == 1)
                        bI = bank()
                        P.mm(bI[:L, 0:257], ST[:L, 0:L], vext[:L, slot, hh, 0:257])
                        tq = tqc[r2]
                        P.ins("act", "activation", out=tq[:L, 0:257], in_=bQ[:L, 0:257], func=AF.Identity,
                              scale=wcol[:L, 0:1])
                        P.ins("dve", "tensor_tensor", out=NUM[:L, hh, 0:257], in0=tq[:L, 0:257], in1=bI[:L, 0:257],
                              op=ALU.add)
                        chk("ml_num", cur["ti"])
                        bK = bank()
                        for dc in range(2):
                            P.mm(bK[:L, dc * 128:(dc + 1) * 128], kT[:, 2 * hh + dc, cc], identb[:, :])
                        km = kTM[r2]
                        P.ins("act", "activation", out=km[:L, 0:256], in_=bK[:L, 0:256], func=AF.Identity,
                              scale=wcol[:L, 2:3])
                        for dc in range(2):
                            bC = bank()
                            P.mm(bC[:, 0:257], km[:L, dc * 128:(dc + 1) * 128], vext[:L, slot, hh, 0:257])
                            P.ins("dve", "scalar_tensor_tensor", out=Cst[:, h, dc, :], in0=Cst[:, h, dc, :],
                                  scalar=wcol[:, 1:2], in1=bC[:, 0:257], op0=ALU.mult, op1=ALU.add)
                        P.ins("act", "activation", out=Cbf[:, h, :, 0:257], in_=Cst[:, h, :, :], func=AF.Copy)
                        if last and (sq < 4 or last_tile):
                            pass
                        chk("ml_st", cur["ti"])
                        if last and (sq < 4 or last_tile):
                            P.dma("sync", o_C[sq, h].rearrange("(dc p) v -> p dc v", p=128), Cst[:, h, :, 0:256],
                                  f"oc{hh}")
                            P.dma("sync", o_n[sq, h].rearrange("(dc p one) -> p dc one", p=128, one=1),
                                  Cst[:, h, :, 256:257], f"oc{hh}", allow_slow_non_contiguous=True)
                    P.ins("act", "activation", out=lst[:L, 0, :], in_=NUM[:L, :, 256], func=AF.Abs)
                    P.ins("act", "activation", out=lst[:L, 1, :], in_=gcol[:L, slot, 8 + 2 * hp:10 + 2 * hp],
                          func=AF.Exp, scale=-1.0)
                    P.ins("dve", "tensor_tensor", out=lst[:L, 0, :], in0=lst[:L, 0, :], in1=lst[:L, 1, :], op=ALU.max)
                    P.ins("dve", "reciprocal", out=lst[:L, 0, :], in_=lst[:L, 0, :])
                    P.ins("dve", "tensor_tensor", out=hn[:L, :, :], in0=NUM[:L, :, 0:256],
                          in1=lst[:L, 0, :].unsqueeze(2).to_broadcast([L, 2, 256]), op=ALU.mult)
                    P.ins("dve", "tensor_reduce", out=lst[:L, 2, :], in_=hn[:L, :, :], axis=AX.X, op=ALU.add)
                    P.ins("act", "activation", out=hsq[:L, :, :], in_=hn[:L, :, :], func=AF.Square)
                    P.ins("dve", "tensor_reduce", out=lst[:L, 3, :], in_=hsq[:L, :, :], axis=AX.X, op=ALU.add)
                    P.ins("dve", "tensor_scalar", out=lst[:L, 2, :], in0=lst[:L, 2, :], scalar1=1.0 / 256, scalar2=None,
                          op0=ALU.mult)
                    P.ins("dve", "tensor_tensor", out=lst[:L, 4, :], in0=lst[:L, 2, :], in1=lst[:L, 2, :], op=ALU.mult)
                    P.ins("dve", "scalar_tensor_tensor", out=lst[:L, 4, :], in0=lst[:L, 3, :], scalar=1.0 / 256,
                          in1=lst[:L, 4, :], op0=ALU.mult, op1=ALU.subtract)
                    P.ins("dve", "tensor_scalar", out=lst[:L, 4, :], in0=lst[:L, 4, :], scalar1=ML_EPS, scalar2=None,
                          op0=ALU.add)
                    P.ins("act", "activation", out=lst[:L, 4, :], in_=lst[:L, 4, :], func=AF.Sqrt)
                    P.ins("dve", "reciprocal", out=lst[:L, 4, :], in_=lst[:L, 4, :])
                    P.ins("dve", "tensor_tensor", out=hsq[:L, :, :], in0=hn[:L, :, :],
                          in1=lst[:L, 2, :].unsqueeze(2).to_broadcast([L, 2, 256]), op=ALU.subtract)
                    P.ins("dve", "tensor_tensor", out=ynb[:L, :, :], in0=hsq[:L, :, :],
                          in1=lst[:L, 4, :].unsqueeze(2).to_broadcast([L, 2, 256]), op=ALU.mult)
                    chk("ml_ln", cur["ti"])
                    bT = bank()
                    ynf = ynb[:, :, :].rearrange("p a b -> p (a b)")
                    for gq in range(4):
                        P.mm(bT[:, gq * 128:gq * 128 + L], ynf[:L, gq * 128:(gq + 1) * 128], identb[:L, :L])
                    for gq in range(4):
                        g = hp * 4 + gq
                        P.ins("dve", "scalar_tensor_tensor", out=ymlT[:, g, cc], in0=bT[:, gq * 128:gq * 128 + L],
                              scalar=nwc[:, g:g + 1], in1=sigo[:, gq, cc], op0=ALU.mult, op1=ALU.mult)
                    chk("ml_epi", cur["ti"])

        def merge(NT, subt):
            mg = carve(0, [NKC, 512], BF16)
            t1 = carve(16384, [512], F32)
            t2 = carve(18432, [512], F32)
            t3 = carve(20480, [512], F32)
            Wr = W["w_br_rw"].rearrange("(kc p) d -> p kc d", p=128)
            Wm = W["w_br_ml"].rearrange("(kc p) d -> p kc d", p=128)
            for f0 in range(0, D, 256):
                wg1 = wload(Win[:, :, O_G1 + f0:O_G1 + f0 + 256], [16, 256])
                wg2 = wload(Win[:, :, O_G2 + f0:O_G2 + f0 + 256], [16, 256])
                wr = wload(Wr[:, :, f0:f0 + 256], [8, 256])
                wm = wload(Wm[:, :, f0:f0 + 256], [8, 256])
                for j in range(2):
                    fc = f0 // 128 + j
                    cs_ = slice(j * 128, (j + 1) * 128)
                    b1, b2, b3, b4 = bank(), bank(), bank(), bank()
                    for kc in range(16):
                        P.mm(b1[:, :NT], wg1[:, kc, cs_], xnT[:, kc, :NT], start=kc == 0, stop=kc == 15)
                    for kc in range(16):
                        P.mm(b2[:, :NT], wg2[:, kc, cs_], xnT[:, kc, :NT], start=kc == 0, stop=kc == 15)
                    for kc in range(8):
                        P.mm(b3[:, :NT], wr[:, kc, cs_], yrwT[:, kc, :NT], start=kc == 0, stop=kc == 7)
                    for kc in range(8):
                        P.mm(b4[:, :NT], wm[:, kc, cs_], ymlT[:, kc, :NT], start=kc == 0, stop=kc == 7)
                    P.ins("act", "activation", out=t1[:, :NT], in_=b1[:, :NT], func=AF.Sigmoid)
                    P.ins("act", "activation", out=t2[:, :NT], in_=b2[:, :NT], func=AF.Sigmoid)
                    P.ins("dve", "tensor_tensor", out=t1[:, :NT], in0=t1[:, :NT], in1=b3[:, :NT], op=ALU.mult)
                    P.ins("dve", "tensor_tensor", out=t2[:, :NT], in0=t2[:, :NT], in1=b4[:, :NT], op=ALU.mult)
                    P.ins("pool", "tensor_tensor", out=mg[:, fc, :NT], in0=t1[:, :NT], in1=t2[:, :NT], op=ALU.add)
            Wo = W["w_out"].rearrange("(kc p) d -> p kc d", p=128)
            for cb in range(4):
                bks = [bank() for _ in subt]
                for k0 in range(0, 16, 8):
                    wo = wload(Wo[:, k0:k0 + 8, cb * 512:(cb + 1) * 512], [8, 512])
                    for si, (st, n) in enumerate(subt):
                        for k in range(8):
                            P.mm(bks[si][:n, :], mg[:, k0 + k, st * 128:st * 128 + n], wo[:, k, :],
                                 start=(k0 + k == 0), stop=(k0 + k == 15))
                for si, (st, n) in enumerate(subt):
                    xs_ = xres[:n, st, cb * 512:(cb + 1) * 512]
                    P.ins("dve", "tensor_tensor", out=xs_, in0=bks[si][:n, :], in1=xs_, op=ALU.add)

        try:
            for ti in range(NPT + 1):
                last_tile = ti == NPT
                cur["ti"] = ti
                wl_state["n"] = 0
                if ti == 0:
                    NT = 80
                    subt = [(0, 80)]
                    P.dma("sync", xres[0:64, 0, :], xs, "xin0")
                    P.dma("sync", xres[64:80, 0, :], meta, "xin0")
                    segs = [(16 * j, 16, j) for j in range(5)]
                    Crw, Lml = 16, 16
                else:
                    NT = 512
                    subt = [(st, 128) for st in range(4)]
                    for st in range(4):
                        r0 = (ti - 1) * 512 + st * 128
                        P.dma("sync", xres[:, st, :], xp[r0:r0 + 128, :], f"xin{st}")
                    segs = [(0, 512, 4)]
                    Crw, Lml = 64, 128
                chk("load", ti)
                ffn("ffn1", 0, NT, subt)
                chk("ffn1", ti)
                if dbg and ti == 1:
                    for st in range(4):
                        P.dma("sync", dbg_out["d_x1"][st * 128:(st + 1) * 128, :], xres[:, st, :], "dbg")
                rmsnorm_T(1, subt)
                rwkv(NT, segs, Crw, last_tile)
                chk("rwkv", ti)
                mlstm(NT, segs, Lml, last_tile)
                chk("mlstm", ti)
                if dbg and ti == 1:
                    P.dma("sync", dbg_out["d_yrw"], yrwT[:], "dbg")
                    P.dma("sync", dbg_out["d_yml"], ymlT[:], "dbg")
                merge(NT, subt)
                chk("merge", ti)
                if dbg and ti == 1:
                    for st in range(4):
                        P.dma("sync", dbg_out["d_x2"][st * 128:(st + 1) * 128, :], xres[:, st, :], "dbg")
                ffn("ffn2", 2, NT, subt)
                for st, n in subt:
                    ssq = small[:n, 2:3]
                    rstd = small[:n, 3:4]
                    P.ins("act", "activation", out=xsb[:n, :], in_=xres[:n, st, :], func=AF.Square, accum_out=ssq)
                    P.ins("dve", "tensor_scalar", out=rstd, in0=ssq, scalar1=1.0 / D, scalar2=1e-6, op0=ALU.mult, op1=ALU.add)
                    P.ins("act", "activation", out=rstd, in_=rstd, func=AF.Sqrt)
                    P.ins("dve", "reciprocal", out=rstd, in_=rstd)
                    P.ins("dve", "scalar_tensor_tensor", out=xres[:n, st, :], in0=xres[:n, st, :], scalar=rstd,
                          in1=gfin[:n, :], op0=ALU.mult, op1=ALU.mult)
                    if ti == 0:
                        P.dma("sync", ys, xres[0:64, 0, :], "yout0")
                    else:
                        r0 = (ti - 1) * 512 + st * 128
                        P.dma("sync", yp[r0:r0 + 128, :], xres[:, st, :], f"yout{st}")

        except _Stop:
            pass
        if stop is not None:
            P.emit()
            return nc
        stg = carve(0, [3392], F32)
        stg2 = carve(16384, [2048], F32)
        for r in range(7):
            bb = bank()
            gs = list(range(r * 4, min(27, r * 4 + 4)))
            for g in gs:
                n = 128 if g < 26 else 32
                P.mm(bb[0:5, (g - r * 4) * 128:(g - r * 4) * 128 + n], carry[:n, g, :], identf[:n, :n])
            c0 = r * 512
            c1 = min(RWC, c0 + 512)
            P.ins("dve", "tensor_copy", out=stg[0:5, c0:c1], in_=bb[0:5, 0:c1 - c0])
        P.dma("sync", o_shift, stg[0:5, 0:RWC], "ofin")
        for r in range(4):
            bb = bank()
            for g in range(r * 4, r * 4 + 4):
                P.mm(bb[0:15, (g - r * 4) * 128:(g - r * 4 + 1) * 128],
                     ccar[:, g, :, :].rearrange("p s j -> p (s j)"), identf[:, :])
            P.ins("dve", "tensor_copy", out=stg2[0:15, r * 512:(r + 1) * 512], in_=bb[0:15, :])
        P.dma("sync", o_conv.rearrange("s j c -> (s j) c"), stg2[0:15, :], "ofin")
        P.dma("sync", o_m.rearrange("s h -> h s"), mrow[0:4, 0:5], "ofin", allow_slow_non_contiguous=True)
        P.emit()
    return nc


_CACHE = {}


def _get_nc(NPT, dbg=False):
    k = (NPT, dbg)
    if k not in _CACHE:
        _CACHE[k] = build(NPT, dbg)
    return _CACHE[k]


def make_in_maps(inputs, ncores, NPT):
    cst = make_consts()
    maps = []
    for c in range(ncores):
        m = {
            "xp": np.ascontiguousarray(inputs["x_prompt"][c, :NPT * 512]),
            "xs": np.ascontiguousarray(inputs["x_sample"][4 * c:4 * c + 4].reshape(64, D)),
            "meta": np.ascontiguousarray(inputs["meta_tokens"]),
            "st_shift": np.ascontiguousarray(inputs["state_rwkv_shift"][0, 4 * c:4 * c + 4]),
            "st_wkv": np.ascontiguousarray(inputs["state_rwkv_wkv"][0, 4 * c:4 * c + 4]),
            "st_conv": np.ascontiguousarray(inputs["state_mlstm_conv"][0, 4 * c:4 * c + 4]),
            "st_C": np.ascontiguousarray(inputs["state_mlstm_C"][0, 4 * c:4 * c + 4]),
            "st_n": np.ascontiguousarray(inputs["state_mlstm_n"][0, 4 * c:4 * c + 4]),
            "st_m": np.ascontiguousarray(inputs["state_mlstm_m"][0, 4 * c:4 * c + 4]),
            "cst": cst,
        }
        for nm, shp in WSHAPES:
            m[nm] = np.ascontiguousarray(np.asarray(inputs[nm]).reshape(shp))
        maps.append(m)
    return maps


def assemble(results, ncores):
    f = np.float32
    cat = lambda k, sl: np.concatenate([np.asarray(r[k])[sl] for r in results], 0)
    y_prompt = np.stack([np.asarray(r["yp"]) for r in results], 0).astype(f)
    y_sample = np.concatenate([np.asarray(r["ys"]).reshape(4, 16, D) for r in results], 0).astype(f)
    outs = [y_prompt, y_sample]
    for k in ("o_shift", "o_wkv", "o_conv", "o_C", "o_n", "o_m"):
        outs.append(cat(k, slice(4, 5))[None].astype(f))
    for k in ("o_shift", "o_wkv", "o_conv", "o_C", "o_n", "o_m"):
        outs.append(cat(k, slice(0, 4))[None].astype(f))
    return tuple(outs)


def kernel(**inputs):
    inputs = {k: np.asarray(v) for k, v in inputs.items()}
    NPT = inputs["x_prompt"].shape[1] // 512
    ncores = inputs["x_prompt"].shape[0]
    nc = _get_nc(NPT)
    in_maps = make_in_maps(inputs, ncores, NPT)
    res = run_bass_kernel_spmd(nc, in_maps, core_ids=list(range(ncores)))
    return assemble(res.results, ncores)
```

```python
import numpy as np
from contextlib import ExitStack
import concourse.bass as bass
import concourse.mybir as mybir
from concourse.bass_utils import run_bass_kernel_spmd

F32 = mybir.dt.float32
BF16 = mybir.dt.bfloat16
AF = mybir.ActivationFunctionType
ALU = mybir.AluOpType
AX = mybir.AxisListType

SEM_ROT = 20000
_DTSZ = {}


def _dsz(dt):
    s = _DTSZ.get(dt)
    if s is None:
        s = 2 if dt == BF16 else 4
        _DTSZ[dt] = s
    return s


def _rect(ap):
    a = ap.ap
    pstep, pn = a[0]
    off = ap.offset
    if pstep == 0:
        p0 = 0
        f0 = off
    else:
        p0 = off // pstep
        f0 = off % pstep
    ext = 0
    for st, cnt in a[1:]:
        ext += (cnt - 1) * abs(st)
    sz = _dsz(ap.dtype)
    return (ap.tensor.name, p0, p0 + pn, f0 * sz, (f0 + ext + 1) * sz)


class Op:
    __slots__ = ("eng", "fn", "waits", "sig", "dkey", "dcount", "pos", "semidx", "count", "idx")

    def __init__(self, eng, fn):
        self.eng = eng
        self.fn = fn
        self.waits = []
        self.sig = False
        self.dkey = None
        self.dcount = 0
        self.pos = 0


class Prog:
    ENGS = ("pe", "act", "dve", "pool", "sync")

    def __init__(self, nc):
        self.nc = nc
        self.ops = []
        self.by_eng = {e: [] for e in self.ENGS}
        self.recs = {}
        self.waited = {}
        self.dma_counts = {}

    @staticmethod
    def _is_ap(v):
        return hasattr(v, "ap") and hasattr(v, "tensor") and hasattr(v, "offset")

    def _track(self, v):
        if not self._is_ap(v):
            return None
        sp = str(v.space)
        if "PSUM" in sp:
            return (v.tensor.name, 0, 128, 0, 1 << 20)
        if "SB" in sp:
            return _rect(v)
        return None

    def add(self, eng, fn, reads, writes, dkey=None, after=None):
        op = Op(eng, fn)
        op.pos = len(self.by_eng[eng])
        idx = len(self.ops)
        op.idx = idx
        deps = set(o.idx for o in (after or []))
        rl = list(dict.fromkeys(r for r in (self._track(v) for v in reads) if r is not None))
        wl = list(dict.fromkeys(r for r in (self._track(v) for v in writes) if r is not None))
        wl = list(dict.fromkeys(wl + [r for r in rl if r[0].startswith("ps")]))
        rl = [r for r in rl if not r[0].startswith("ps")]
        for (nm, p0, p1, f0, f1) in rl:
            for rec in self.recs.setdefault(nm, []):
                if rec[5] and rec[0] < p1 and p0 < rec[1] and rec[2] < f1 and f0 < rec[3]:
                    deps.add(rec[4])
        for (nm, p0, p1, f0, f1) in wl:
            for rec in self.recs.setdefault(nm, []):
                if rec[0] < p1 and p0 < rec[1] and rec[2] < f1 and f0 < rec[3]:
                    deps.add(rec[4])
        for (nm, p0, p1, f0, f1) in wl:
            lst = self.recs[nm]
            lst[:] = [rec for rec in lst if not (p0 <= rec[0] and rec[1] <= p1 and f0 <= rec[2] and rec[3] <= f1)]
            lst.append([p0, p1, f0, f1, idx, True])
        for (nm, p0, p1, f0, f1) in rl:
            lst = self.recs[nm]
            lst[:] = [rec for rec in lst if not ((not rec[5]) and rec[0] == p0 and rec[1] == p1 and rec[2] == f0
                                                  and rec[3] == f1 and rec[4] != idx and self.ops[rec[4]].eng == eng)]
            lst.append([p0, p1, f0, f1, idx, False])
        deps.discard(idx)
        for j in sorted(deps):
            oj = self.ops[j]
            if oj.dkey is not None:
                k = (eng, "d", oj.dkey)
                if self.waited.get(k, 0) >= oj.dcount:
                    continue
                self.waited[k] = oj.dcount
                op.waits.append(("d", oj.dkey, j))
            else:
                if oj.eng == eng and eng == "pe":
                    continue
                k = (eng, "e", oj.eng)
                if self.waited.get(k, -1) >= oj.pos:
                    continue
                self.waited[k] = oj.pos
                oj.sig = True
                op.waits.append(("e", oj.eng, j))
        if dkey is not None:
            op.dkey = dkey
            self.dma_counts[dkey] = self.dma_counts.get(dkey, 0) + 16
            op.dcount = self.dma_counts[dkey]
        self.ops.append(op)
        self.by_eng[eng].append(op)
        return op

    def seal_key(self, key):
        tot = self.dma_counts.get(key, 0)
        for op in self.ops:
            if op.dkey == key:
                op.dcount = tot

    def ins(self, eng, meth, *, reads=None, writes=None, **kw):
        r = list(reads or [])
        w = list(writes or [])
        for k, v in kw.items():
            if self._is_ap(v):
                if k in ("out", "accum_out", "ap"):
                    w.append(v)
                else:
                    r.append(v)
        return self.add(eng, lambda e: getattr(e, meth)(**kw), r, w)

    def mm(self, out, lhsT, rhs, start=True, stop=True, tp=None):
        if tp is None:
            return self.add("pe", lambda e: e.matmul(out, lhsT, rhs, start=start, stop=stop), [lhsT, rhs], [out])
        return self.add("pe", lambda e: e.matmul(out, lhsT, rhs, start=start, stop=stop, tile_position=tp),
                        [lhsT, rhs], [out])

    def dma(self, eng, out, in_, key, after=None, **kw):
        return self.add(eng, lambda e: e.dma_start(out=out, in_=in_, **kw), [in_], [out], dkey=key, after=after)

    def emit(self):
        nc = self.nc
        with ExitStack() as es:
            esems = {}
            for eng in self.ENGS:
                c = 0
                for op in self.by_eng[eng]:
                    if op.sig:
                        c += 1
                        op.semidx = (c - 1) // SEM_ROT
                        op.count = (c - 1) % SEM_ROT + 1
                nsem = (c + SEM_ROT - 1) // SEM_ROT
                esems[eng] = [es.enter_context(nc.semaphore(f"s_{eng}_{i}")) for i in range(max(nsem, 1))]
            dsems = {k: es.enter_context(nc.semaphore(f"d_{k}")) for k in self.dma_counts}
            block = es.enter_context(nc.Block())
            ops = self.ops

            def run(eng_name):
                def body(e):
                    for op in self.by_eng[eng_name]:
                        for (kind, key, j) in op.waits:
                            oj = ops[j]
                            if kind == "d":
                                e.wait_ge(dsems[key], oj.dcount)
                            else:
                                e.wait_ge(esems[key][oj.semidx], oj.count)
                        inst = op.fn(e)
                        if op.dkey is not None:
                            inst.then_inc(dsems[op.dkey], 16)
                        elif op.sig:
                            inst.then_inc(esems[eng_name][op.semidx], 1)
                    if eng_name == "sync":
                        for k, tot in self.dma_counts.items():
                            e.wait_ge(dsems[k], tot)
                return body

            block.tensor(run("pe"))
            block.scalar(run("act"))
            block.vector(run("dve"))
            block.gpsimd(run("pool"))
            block.sync(run("sync"))


D = 2048
DFF = 5504
NKC = 16
NFC = 43
RWC = 3360
O_R, O_K, O_V, O_LW, O_LG = 0, 1024, 2048, 3072, 3200
O_MQ, O_MV, O_MO, O_MI, O_MF, O_G1, O_G2 = 3360, 5408, 6432, 7456, 7460, 7464, 9512
INC = 11560
KAPPA = -0.6065306597126334
RW_EPS = 64e-5
ML_EPS = 1e-5

WSHAPES = [
    ("ffn1_norm", (D,)), ("ffn1_w_gate", (D, DFF)), ("ffn1_w_up", (D, DFF)), ("ffn1_w_down", (DFF, D)),
    ("mix_norm", (D,)), ("w_in", (D, INC)), ("rw_mu", (RWC,)), ("rw_w0", (1024,)), ("rw_w2", (64, 1024)),
    ("rw_a0", (1024,)), ("rw_a2", (64, 1024)), ("rw_g2", (160, 1024)), ("rw_kk", (1024,)), ("rw_ka", (1024,)),
    ("rw_rk", (1024,)), ("rw_ln_w", (1024,)), ("rw_ln_b", (1024,)), ("ml_conv_w", (4, 2048)),
    ("ml_conv_b", (2048,)), ("ml_i_b", (4,)), ("ml_f_b", (4,)), ("ml_norm_w", (1024,)),
    ("w_br_rw", (1024, D)), ("w_br_ml", (1024, D)), ("w_out", (D, D)), ("ffn2_norm", (D,)),
    ("ffn2_w_gate", (D, DFF)), ("ffn2_w_up", (D, DFF)), ("ffn2_w_down", (DFF, D)), ("final_norm", (D,)),
]

C_ID = 0
C_M5 = 128
C_OBD = 448
C_R64 = 576
C_R16 = 1088
C_ML = 1600
C_SEL = 1728
C_ONE = 2240
CST_N = 2752


class _Stop(Exception):
    pass


def make_consts():
    c = np.zeros((128, CST_N), np.float32)
    c[:, C_ID:C_ID + 128] = np.eye(128, dtype=np.float32)
    s = np.arange(64)[:, None]
    t = np.arange(64)[None, :]
    strict = (s < t).astype(np.float32)
    strictT = (t < s).astype(np.float32)
    incl = (s <= t).astype(np.float32)
    for h in range(2):
        r = slice(h * 64, h * 64 + 64)
        for i, m in enumerate((strict, strictT, strict, incl, incl)):
            c[r, C_M5 + i * 64:C_M5 + (i + 1) * 64] = m
        c[r, C_OBD + h * 64:C_OBD + h * 64 + 64] = 1.0
    r64 = np.ones(512, np.float32)
    r64[::64] = 0.0
    r16 = np.ones(512, np.float32)
    r16[::16] = 0.0
    c[:, C_R64:C_R64 + 512] = r64[None]
    c[:, C_R16:C_R16 + 512] = r16[None]
    s2 = np.arange(128)[:, None]
    t2 = np.arange(128)[None, :]
    c[:, C_ML:C_ML + 128] = (s2 <= t2).astype(np.float32)
    for h in range(4):
        c[h, C_SEL + h * 128:C_SEL + (h + 1) * 128] = 1.0
    c[:, C_ONE:C_ONE + 512] = 1.0
    return c


def build(NPT, dbg=False, stop=None):
    SEQ = NPT * 512
    nc = bass.Bass("TRN2", target_bir_lowering=False)

    def din(name, shape):
        return nc.dram_tensor(name, list(shape), F32, kind="ExternalInput").ap()

    def dout(name, shape):
        return nc.dram_tensor(name, list(shape), F32, kind="ExternalOutput").ap()

    xp = din("xp", [SEQ, D])
    xs = din("xs", [64, D])
    meta = din("meta", [16, D])
    st_shift = din("st_shift", [4, RWC])
    st_wkv = din("st_wkv", [4, 16, 64, 64])
    st_conv = din("st_conv", [4, 3, 2048])
    st_C = din("st_C", [4, 4, 256, 256])
    st_n = din("st_n", [4, 4, 256])
    st_m = din("st_m", [4, 4])
    cst_d = din("cst", [128, CST_N])
    W = {nm: din(nm, shp) for nm, shp in WSHAPES}
    yp = dout("yp", [SEQ, D])
    ys = dout("ys", [64, D])
    o_shift = dout("o_shift", [5, RWC])
    o_wkv = dout("o_wkv", [5, 16, 64, 64])
    o_conv = dout("o_conv", [5, 3, 2048])
    o_C = dout("o_C", [5, 4, 256, 256])
    o_n = dout("o_n", [5, 4, 256])
    o_m = dout("o_m", [5, 4])
    dbg_out = {}
    if dbg:
        dbg_out["d_x1"] = dout("d_x1", [512, D])
        dbg_out["d_x2"] = dout("d_x2", [512, D])
        dbg_out["d_yrw"] = dout("d_yrw", [128, 8, 512])
        dbg_out["d_yml"] = dout("d_yml", [128, 8, 512])

    es = ExitStack()
    with es:
        def sb(name, shape, dt):
            return es.enter_context(nc.sbuf_tensor(name, list(shape), dt))

        P = Prog(nc)
        NRING = 6
        xres = sb("xres", [128, 4, D], F32)
        xnT = sb("xnT", [128, NKC, 512], BF16)
        SCRB = 52224
        scr = sb("scr", [128, SCRB // 4], F32)
        wring = sb("wring", [128, NRING, 4096], BF16)
        sgb = sb("sgb", [128, 1, 512], F32)
        yrwT = sb("yrwT", [128, 8, 512], BF16)
        ymlT = sb("ymlT", [128, 8, 512], BF16)
        Pst = sb("Pst", [128, 8, 64], F32)
        Pbf = sb("Pbf", [128, 8, 64], BF16)
        Cst = sb("Cst", [128, 4, 2, 257], F32)
        Cbf = sb("Cbf", [128, 4, 2, 258], BF16)
        gfin = sb("gfin", [128, D], F32)
        identb = sb("identb", [128, 128], BF16)
        identf = sb("identf", [128, 128], F32)
        mask5 = sb("mask5", [128, 5, 64], BF16)
        onesbd = sb("onesbd", [128, 128], F32)
        r64 = sb("r64", [128, 512], BF16)
        r16 = sb("r16", [128, 512], BF16)
        one512 = sb("one512", [128, 512], BF16)
        mlmask = sb("mlmask", [128, 128], BF16)
        sel4 = sb("sel4", [4, 512], F32)
        w2a2 = sb("w2a2", [128, 1024], BF16)
        g2a = sb("g2a", [128, 1024], BF16)
        g2b = sb("g2b", [32, 1024], BF16)
        gcols = sb("gcols", [128, 3, 16], F32)
        mucols = sb("mucols", [128, 27], F32)
        rwc = sb("rwc", [128, 7, 8], F32)
        cwc = sb("cwc", [128, 16, 4], F32)
        cbc = sb("cbc", [128, 16], F32)
        nwc = sb("nwc", [128, 8], F32)
        ifb = sb("ifb", [4, 2], F32)
        carry = sb("carry", [128, 27, 5], F32)
        ccar = sb("ccar", [128, 16, 5, 3], F32)
        mrow = sb("mrow", [4, 8], F32)
        small = sb("small", [128, 64], F32)

        banks = [es.enter_context(nc.psum_tensor(f"ps{i}", [128, 512], F32)) for i in range(8)]
        bstate = {"i": 0, "w": 0}
        cur = {"ti": 0}

        chk_cnt = {}

        def chk(tag, ti=None):
            if stop is None:
                return
            want = stop[0]
            k = 1
            if "#" in want:
                want, k = want.split("#")
                k = int(k)
            if want == tag and (ti is None or stop[1] == ti):
                chk_cnt[tag] = chk_cnt.get(tag, 0) + 1
                if chk_cnt[tag] >= k:
                    raise _Stop()

        def bank():
            b = banks[bstate["i"] % 8]
            bstate["i"] += 1
            return b

        def carve(off, shape, dt):
            n = int(np.prod(shape))
            sz = _dsz(dt)
            assert off % 4 == 0 and off + n * sz <= SCRB, (off, shape)
            nf = (n * sz + 3) // 4
            v = scr[:, off // 4: off // 4 + nf]
            if dt == BF16:
                v = v.bitcast(BF16)[:, 0:n]
            if len(shape) == 1:
                return v
            names = " ".join(f"a{i}" for i in range(len(shape)))
            kw = {f"a{i}": int(s) for i, s in enumerate(shape)}
            return v.rearrange(f"p ({names}) -> p {names}", **kw)

        NWL = 219
        wscr = nc.dram_tensor("wscr", [NWL, 128, 4096], BF16).ap()
        wr_ops = {}
        wl_state = {"n": 0}

        def wload(src, shape):
            s = bstate["w"] % NRING
            bstate["w"] += 1
            i = wl_state["n"]
            wl_state["n"] += 1
            a, b = shape
            assert a * b <= 4096 and i < NWL
            flat = wring[:, s, 0:a * b]
            dst = flat.rearrange("p (a b) -> p a b", a=a)
            if cur["ti"] == 0:
                P.dma("pool", dst, src, f"w{s}")
                if NPT > 0:
                    wr_ops[i] = P.dma("sync", wscr[i, :, 0:a * b], flat, f"sw{s}")
            else:
                P.dma("sync", flat, wscr[i, :, 0:a * b], f"w{s}", after=[wr_ops[i]])
            return dst

        xsbs = [carve(SCRB - 8192, [D], BF16), carve(SCRB - 4096, [D], BF16)]
        cstg = carve(0, [CST_N], F32)
        stgA = carve(11008, [3392], F32)
        stgB = carve(11008 + 13568, [2048], F32)
        P.dma("sync", cstg, cst_d, "const")
        P.dma("sync", gfin[:], W["final_norm"].partition_broadcast(128), "const")
        stgV = [carve(36864, [128], F32), carve(36864 + 512, [128], F32)]
        P.ins("pool", "memset", ap=stgV[0][:, :], constant=0.0)
        P.ins("pool", "memset", ap=stgV[1][:, :], constant=0.0)

        def vrows(t, r0, ap, n):
            P.dma("sync", stgV[t][r0:r0 + n, :], ap.rearrange("(c p) -> c p", p=128), "const")
        vrows(0, 0, W["ffn1_norm"], 16)
        vrows(0, 16, W["mix_norm"], 16)
        vrows(0, 32, W["ffn2_norm"], 16)
        vrows(0, 48, W["rw_mu"][0:3328], 26)
        P.dma("sync", stgV[0][74:75, 0:32], W["rw_mu"][3328:3360].rearrange("(c p) -> c p", p=32), "const")
        for i, nm in enumerate(("rw_w0", "rw_a0", "rw_kk", "rw_ka", "rw_rk", "rw_ln_w")):
            vrows(0, 75 + 8 * i, W[nm], 8)
        vrows(1, 0, W["rw_ln_b"], 8)
        for j in range(4):
            vrows(1, 8 + 16 * j, W["ml_conv_w"][j], 16)
        vrows(1, 72, W["ml_conv_b"], 16)
        vrows(1, 88, W["ml_norm_w"], 8)
        P.dma("sync", ifb[:, 0:1], W["ml_i_b"].rearrange("(p c) -> p c", c=1), "const",
              allow_slow_non_contiguous=True)
        P.dma("sync", ifb[:, 1:2], W["ml_f_b"].rearrange("(p c) -> p c", c=1), "const",
              allow_slow_non_contiguous=True)
        P.dma("pool", w2a2[0:64, :], W["rw_w2"], "const")
        P.dma("pool", w2a2[64:128, :], W["rw_a2"], "const")
        P.dma("pool", g2a[:, :], W["rw_g2"][0:128, :], "const")
        P.dma("pool", g2b[:, :], W["rw_g2"][128:160, :], "const")
        P.dma("sync", stgA[0:4, 0:RWC], st_shift, "const")
        P.dma("sync", stgB[0:12, 0:2048], st_conv.rearrange("s j c -> (s j) c"), "const")
        P.dma("sync", mrow[0:4, 0:4], st_m.rearrange("s h -> h s"), "const", allow_slow_non_contiguous=True)
        P.seal_key("const")

        P.ins("dve", "tensor_copy", out=identb[:], in_=cstg[:, C_ID:C_ID + 128])
        P.ins("dve", "tensor_copy", out=identf[:], in_=cstg[:, C_ID:C_ID + 128])
        bA = bank()
        bB = bank()
        P.mm(bA[:, 0:123], stgV[0][0:123, :], identf[0:123, 0:123])
        P.mm(bB[:, 0:96], stgV[1][0:96, :], identf[0:96, 0:96])
        P.ins("dve", "tensor_copy", out=gcols[:].rearrange("p a b -> p (a b)"), in_=bA[:, 0:48])
        P.ins("dve", "tensor_copy", out=mucols[:, :], in_=bA[:, 48:75])
        P.ins("dve", "tensor_copy", out=rwc[:, 0:6, :].rearrange("p a b -> p (a b)"), in_=bA[:, 75:123])
        P.ins("dve", "tensor_copy", out=rwc[:, 6, :], in_=bB[:, 0:8])
        P.ins("dve", "tensor_copy", out=cwc[:].rearrange("p c j -> p j c"),
              in_=bB[:, 8:72].rearrange("p (j c) -> p j c", j=4))
        P.ins("dve", "tensor_copy", out=cbc[:, :], in_=bB[:, 72:88])
        P.ins("dve", "tensor_copy", out=nwc[:, :], in_=bB[:, 88:96])
        P.ins("dve", "tensor_copy", out=mask5[:].rearrange("p a b -> p (a b)"), in_=cstg[:, C_M5:C_M5 + 320])
        P.ins("dve", "tensor_copy", out=onesbd[:], in_=cstg[:, C_OBD:C_OBD + 128])
        P.ins("dve", "tensor_copy", out=r64[:], in_=cstg[:, C_R64:C_R64 + 512])
        P.ins("dve", "tensor_copy", out=r16[:], in_=cstg[:, C_R16:C_R16 + 512])
        P.ins("dve", "tensor_copy", out=one512[:], in_=cstg[:, C_ONE:C_ONE + 512])
        P.ins("dve", "tensor_copy", out=mlmask[:], in_=cstg[:, C_ML:C_ML + 128])
        P.ins("dve", "tensor_copy", out=sel4[:], in_=cstg[0:4, C_SEL:C_SEL + 512])
        P.ins("pool", "memset", ap=carry[:], constant=0.0)
        P.ins("pool", "memset", ap=ccar[:], constant=0.0)
        P.ins("pool", "memset", ap=mrow[0:4, 4:8], constant=0.0)
        P.ins("pool", "memset", ap=small[:], constant=0.0)
        b = bank()
        for g in range(27):
            n = 128 if g < 26 else 32
            P.mm(b[:n, g * 4:g * 4 + 4], stgA[0:4, g * 128:g * 128 + n], identf[0:4, 0:4])
        P.ins("dve", "tensor_copy", out=carry[:, 0:26, 0:4], in_=b[:, 0:104].rearrange("p (g s) -> p g s", s=4))
        P.ins("dve", "tensor_copy", out=carry[0:32, 26, 0:4], in_=b[0:32, 104:108])
        b = bank()
        for g in range(16):
            P.mm(b[:, g * 12:g * 12 + 12], stgB[0:12, g * 128:g * 128 + 128], identf[0:12, 0:12])
        P.ins("dve", "tensor_copy", out=ccar[:, :, 0:4, :],
              in_=b[:, 0:192].rearrange("p (g s j) -> p g s j", s=4, j=3))

        def rmsnorm_T(gi, subt):
            for st, n in subt:
                xsb = xsbs[st % 2]
                ssq = small[:n, 8 + 2 * st:9 + 2 * st]
                rstd = small[:n, 9 + 2 * st:10 + 2 * st]
                P.ins("act", "activation", out=xsb[:n, :], in_=xres[:n, st, :], func=AF.Square, accum_out=ssq)
                P.ins("dve", "tensor_scalar", out=rstd, in0=ssq, scalar1=1.0 / D, scalar2=1e-6,
                      op0=ALU.mult, op1=ALU.add)
                P.ins("act", "activation", out=rstd, in_=rstd, func=AF.Sqrt)
                P.ins("dve", "reciprocal", out=rstd, in_=rstd)
                P.ins("dve", "tensor_scalar", out=xsb[:n, :], in0=xres[:n, st, :], scalar1=rstd, scalar2=None,
                      op0=ALU.mult)
                for c0 in range(0, 16, 4):
                    bb = bank()
                    for c in range(4):
                        P.mm(bb[:, c * 128:c * 128 + n], xsb[:n, (c0 + c) * 128:(c0 + c + 1) * 128], identb[:n, :n])
                    P.ins("dve", "tensor_tensor",
                          out=xnT[:, c0:c0 + 4, st * 128:st * 128 + n],
                          in0=bb[:, :].rearrange("p (c t) -> p c t", t=128)[:, :, 0:n],
                          in1=gcols[:, gi, c0:c0 + 4].unsqueeze(2).to_broadcast([128, 4, n]), op=ALU.mult)

        def ffn(pref, gi, NT, subt):
            rmsnorm_T(gi, subt)
            chk(pref + "_norm", cur["ti"])
            hT = carve(0, [NFC, 512], BF16)
            Wg = W[pref + "_w_gate"].rearrange("(kc p) f -> p kc f", p=128)
            Wu = W[pref + "_w_up"].rearrange("(kc p) f -> p kc f", p=128)
            Wd = W[pref + "_w_down"].rearrange("(kc p) d -> p kc d", p=128)
            for f0 in range(0, DFF, 256):
                fw = min(256, DFF - f0)
                wg = wload(Wg[:, :, f0:f0 + fw], [16, fw])
                wu = wload(Wu[:, :, f0:f0 + fw], [16, fw])
                for j in range(fw // 128):
                    fc = f0 // 128 + j
                    bg = bank()
                    bu = bank()
                    for kc in range(16):
                        P.mm(bg[:, :NT], wg[:, kc, j * 128:(j + 1) * 128], xnT[:, kc, :NT], start=kc == 0, stop=kc == 15)
                    for kc in range(16):
                        P.mm(bu[:, :NT], wu[:, kc, j * 128:(j + 1) * 128], xnT[:, kc, :NT], start=kc == 0, stop=kc == 15)
                    sg = sgb[:, 0, :NT]
                    P.ins("act", "activation", out=sg, in_=bg[:, :NT], func=AF.Silu)
                    P.ins("dve", "tensor_tensor", out=hT[:, fc, :NT], in0=sg, in1=bu[:, :NT], op=ALU.mult)
            chk(pref + "_gu", cur["ti"])
            for cb in range(4):
                bks = [bank() for _ in subt]
                for k0 in range(0, NFC, 8):
                    nk = min(8, NFC - k0)
                    wd = wload(Wd[:, k0:k0 + nk, cb * 512:(cb + 1) * 512], [nk, 512])
                    for si, (st, n) in enumerate(subt):
                        for k in range(nk):
                            P.mm(bks[si][:n, :], hT[:, k0 + k, st * 128:st * 128 + n], wd[:, k, :],
                                 start=(k0 + k == 0), stop=(k0 + k == NFC - 1))
                chk(pref + "_dmm%d" % cb, cur["ti"])
                for si, (st, n) in enumerate(subt):
                    xs_ = xres[:n, st, cb * 512:(cb + 1) * 512]
                    P.ins("dve", "scalar_tensor_tensor", out=xs_, in0=bks[si][:n, :], scalar=0.5, in1=xs_,
                          op0=ALU.mult, op1=ALU.add)
                chk(pref + "_dev%d" % cb, cur["ti"])

        Win = W["w_in"].rearrange("(kc p) c -> p kc c", p=128)

        def proj_fm(col0, ncols, NT, outs):
            wt = wload(Win[:, :, col0:col0 + ncols], [16, ncols])
            for j in range((ncols + 127) // 128):
                m = min(128, ncols - j * 128)
                bb = bank()
                for kc in range(16):
                    P.mm(bb[:m, :NT], wt[:, kc, j * 128:j * 128 + m], xnT[:, kc, :NT], start=kc == 0, stop=kc == 15)
                outs(j, bb, m)

        def rwkv(NT, segs, C, last_tile):
            nseg = len(segs)
            chunks = []
            for (c0, ln, sq) in segs:
                for cc in range(0, ln, C):
                    chunks.append((c0 + cc, sq, cc == 0, cc + C >= ln))
            nch = len(chunks)
            rmask = r64 if C == 64 else r16
            NJ = 6 if C == 64 else 4
            o = 0

            def cv(shape, dt):
                nonlocal o
                v = carve(o, shape, dt)
                o += ((int(np.prod(shape)) * _dsz(dt) + 3) // 4) * 4
                return v
            T = [cv([514], F32) for _ in range(8)]
            YQ = carve(2 * 2056, [2, 8, 64], F32)
            YSQ = carve(4 * 2056, [2, 8, 64], F32)
            YN = carve(6 * 2056, [2, 8, 64], BF16)
            OPS = cv([2, 7, 512], BF16)
            GF = cv([2, 512], F32)
            BON = cv([2, 512], F32)
            WC = cv([2, 8], F32)
            LST = cv([6, 16], F32)
            SC = [cv([2, 5, 64], BF16) for _ in range(2)]
            TM3 = [cv([2, 3, 64], BF16) for _ in range(2)]
            RF = [cv([2, 64], BF16) for _ in range(2)]
            ao = [4 * 2056]

            def av_(shape, dt):
                v = carve(ao[0], shape, dt)
                ao[0] += ((int(np.prod(shape)) * _dsz(dt) + 3) // 4) * 4
                assert ao[0] <= 8 * 2056
                return v
            SC += [av_([2, 5, 64], BF16) for _ in range(2)]
            TM3 += [av_([2, 3, 64], BF16) for _ in range(2)]
            RF += [av_([2, 64], BF16) for _ in range(2)]
            QP = [[av_([2, 2, 64], BF16) for _ in range(2)] for _ in range(2)]
            RR = [[av_([2, 64], BF16) for _ in range(2)] for _ in range(2)]
            XU = cv([2, 2, 64], BF16)
            TL = cv([512], BF16)
            SG1 = cv([512], BF16)
            SG2 = cv([512], BF16)
            LTMP = T[7]

            def shift_u(bb, m, g, dst, NTl=NT):
                praw = T[0]
                dd = T[4]
                P.ins("act", "activation", out=praw[:m, 1:1 + NT], in_=bb[:m, :NT], func=AF.Copy)
                P.ins("dve", "tensor_tensor", out=dd[:m, 0:NT], in0=praw[:m, 0:NT], in1=praw[:m, 1:1 + NT],
                      op=ALU.subtract)
                for (c0, ln, sq) in segs:
                    P.ins("dve", "tensor_tensor", out=dd[:m, c0:c0 + 1], in0=carry[:m, g, sq:sq + 1],
                          in1=praw[:m, 1 + c0:2 + c0], op=ALU.subtract)
                P.ins("dve", "scalar_tensor_tensor", out=dst[:m, 0:NT], in0=dd[:m, 0:NT], scalar=mucols[:m, g:g + 1],
                      in1=praw[:m, 1:1 + NT], op0=ALU.mult, op1=ALU.add)
                for (c0, ln, sq) in segs:
                    P.ins("act", "activation", out=carry[:m, g, sq:sq + 1], in_=praw[:m, c0 + ln:c0 + ln + 1],
                          func=AF.Copy)

            def lora_out(j, bb, m):
                if m == 32:
                    j = 2
                if j == 0:
                    shift_u(bb, 128, 24, LTMP)
                    P.ins("act", "activation", out=TL[0:64, :NT], in_=LTMP[0:64, :NT], func=AF.Tanh)
                    P.ins("act", "activation", out=TL[64:128, :NT], in_=LTMP[64:128, :NT], func=AF.Copy)
                elif j == 1:
                    shift_u(bb, 128, 25, LTMP)
                    P.ins("act", "activation", out=SG1[:, :NT], in_=LTMP[:, :NT], func=AF.Sigmoid)
                else:
                    shift_u(bb, 32, 26, LTMP)
                    P.ins("act", "activation", out=SG2[0:32, :NT], in_=LTMP[0:32, :NT], func=AF.Sigmoid)
            proj_fm(O_LW, 256, NT, lora_out)
            proj_fm(O_LW + 256, 32, NT, lora_out)
            chk("rw_lora", cur["ti"])

            hk = [slice(0, 64), slice(64, 128)]
            hs = [slice(0, C), slice(64, 64 + C)]

            for q in range(4):
                for pi in range(2):
                    pp = 2 * q + pi
                    At, Rt, Bt, Kt, Bh, Kh, Vb = [OPS[:, pi, i, :] for i in range(7)]
                    ur, uk, uv = T[1], T[2], T[3]
                    wt1 = wload(Win[:, :, O_R + pp * 128:O_R + pp * 128 + 128], [16, 128])
                    wt2 = wload(Win[:, :, O_K + pp * 128:O_K + pp * 128 + 128], [16, 128])
                    wt3 = wload(Win[:, :, O_V + pp * 128:O_V + pp * 128 + 128], [16, 128])
                    for wt, g, dst in ((wt1, pp, ur), (wt2, 8 + pp, uk), (wt3, 16 + pp, uv)):
                        bb = bank()
                        for kc in range(16):
                            P.mm(bb[:, :NT], wt[:, kc, :], xnT[:, kc, :NT], start=kc == 0, stop=kc == 15)
                        shift_u(bb, 128, g, dst)
                    cols = slice(pp * 128, pp * 128 + 128)
                    ba = bank()
                    P.mm(ba[:, :NT], w2a2[64:128, cols], TL[64:128, :NT])
                    av = T[5]
                    P.ins("act", "activation", out=av[:, :NT], in_=ba[:, :NT], func=AF.Sigmoid, bias=rwc[:, 1, pp:pp + 1])
                    bw = bank()
                    P.mm(bw[:, :NT], w2a2[0:64, cols], TL[0:64, :NT])
                    sgm = T[6]
                    P.ins("act", "activation", out=sgm[:, :NT], in_=bw[:, :NT], func=AF.Sigmoid, bias=rwc[:, 0, pp:pp + 1])
                    bg_ = bank()
                    P.mm(bg_[:, :NT], g2a[:, cols], SG1[:, :NT], start=True, stop=False)
                    P.mm(bg_[:, :NT], g2b[0:32, cols], SG2[0:32, :NT], start=False, stop=True)
                    P.ins("act", "activation", out=GF[:, pi, :NT], in_=bg_[:, :NT], func=AF.Copy)
                    kkr = T[0]
                    P.ins("dve", "tensor_scalar", out=kkr[:, :NT], in0=uk[:, :NT], scalar1=rwc[:, 2, pp:pp + 1],
                          scalar2=None, op0=ALU.mult)
                    sq_ = T[4]
                    P.ins("act", "activation", out=sq_[:, :NT], in_=kkr[:, :NT], func=AF.Square)
                    bs = bank()
                    P.mm(bs[:, :NT], onesbd[:, :], sq_[:, :NT])
                    P.ins("dve", "tensor_scalar", out=sq_[:, :NT], in0=bs[:, :NT], scalar1=1e-24, scalar2=None,
                          op0=ALU.max)
                    P.ins("act", "activation", out=sq_[:, :NT], in_=sq_[:, :NT], func=AF.Sqrt)
                    P.ins("dve", "reciprocal", out=sq_[:, :NT], in_=sq_[:, :NT])
                    P.ins("dve", "tensor_tensor", out=kkr[:, :NT], in0=kkr[:, :NT], in1=sq_[:, :NT], op=ALU.mult)
                    P.ins("dve", "tensor_scalar", out=sq_[:, :NT], in0=av[:, :NT], scalar1=-1.0,
                          scalar2=rwc[:, 3, pp:pp + 1], op0=ALU.add, op1=ALU.mult)
                    P.ins("dve", "scalar_tensor_tensor", out=uk[:, :NT], in0=sq_[:, :NT], scalar=1.0, in1=uk[:, :NT],
                          op0=ALU.add, op1=ALU.mult)
                    P.ins("dve", "scalar_tensor_tensor", out=sq_[:, :NT], in0=ur[:, :NT], scalar=rwc[:, 4, pp:pp + 1],
                          in1=uk[:, :NT], op0=ALU.mult, op1=ALU.mult)
                    bs2 = bank()
                    P.mm(bs2[:, :NT], onesbd[:, :], sq_[:, :NT])
                    P.ins("dve", "tensor_tensor", out=BON[:, pi, :NT], in0=bs2[:, :NT], in1=uv[:, :NT], op=ALU.mult)
                    P.ins("dve", "tensor_scalar", out=BON[:, pi, :NT], in0=BON[:, pi, :NT],
                          scalar1=rwc[:, 6, pp:pp + 1], scalar2=None, op0=ALU.add)
                    P.ins("dve", "tensor_tensor", out=av[:, :NT], in0=kkr[:, :NT], in1=av[:, :NT], op=ALU.mult)
                    cs = T[7]
                    P.ins("dve", "tensor_tensor_scan", out=cs[:, :NT], data0=rmask[:, :NT], data1=sgm[:, :NT],
                          initial=0.0, op0=ALU.mult, op1=ALU.add)
                    P.ins("dve", "tensor_tensor", out=sgm[:, :NT], in0=cs[:, :NT], in1=sgm[:, :NT], op=ALU.subtract)
                    P.ins("act", "activation", out=sgm[:, :NT], in_=sgm[:, :NT], func=AF.Exp, scale=KAPPA)
                    P.ins("dve", "scalar_tensor_tensor", out=At[:, :NT], in0=kkr[:, :NT], scalar=-1.0, in1=sgm[:, :NT],
                          op0=ALU.mult, op1=ALU.mult)
                    P.ins("act", "activation", out=sgm[:, :NT], in_=cs[:, :NT], func=AF.Exp, scale=KAPPA)
                    P.ins("dve", "tensor_tensor", out=Rt[:, :NT], in0=ur[:, :NT], in1=sgm[:, :NT], op=ALU.mult)
                    P.ins("act", "activation", out=sgm[:, :NT], in_=cs[:, :NT], func=AF.Exp, scale=-KAPPA)
                    P.ins("dve", "tensor_tensor", out=Bt[:, :NT], in0=av[:, :NT], in1=sgm[:, :NT], op=ALU.mult)
                    P.ins("dve", "tensor_tensor", out=Kt[:, :NT], in0=uk[:, :NT], in1=sgm[:, :NT], op=ALU.mult)
                    csv = cs[:, 0:nch * C].rearrange("p (c t) -> p c t", t=C)
                    P.ins("act", "activation", out=WC[:, pi, 0:nch], in_=csv[:, :, C - 1], func=AF.Exp, scale=KAPPA)
                    P.ins("dve", "tensor_tensor", out=sgm[:, 0:nch * C].rearrange("p (c t) -> p c t", t=C),
                          in0=csv[:, :, C - 1:C].to_broadcast([128, nch, C]), in1=csv, op=ALU.subtract)
                    P.ins("act", "activation", out=sgm[:, :NT], in_=sgm[:, :NT], func=AF.Exp, scale=KAPPA)
                    P.ins("dve", "tensor_tensor", out=Bh[:, :NT], in0=av[:, :NT], in1=sgm[:, :NT], op=ALU.mult)
                    P.ins("dve", "tensor_tensor", out=Kh[:, :NT], in0=uk[:, :NT], in1=sgm[:, :NT], op=ALU.mult)
                    P.ins("act", "activation", out=Vb[:, :NT], in_=uv[:, :NT], func=AF.Copy)
                    chk("rw_pre", cur["ti"])
                    if pi == 1:
                        chk("rw_pre2", cur["ti"])

                def load_state(ci):
                    c0, sq, first, last = chunks[ci]
                    if not (first and NT == 80):
                        return
                    if sq < 4:
                        for pi in range(2):
                            pp = 2 * q + pi
                            stw = T[pi][:, 0:64]
                            P.dma("sync", stw, st_wkv[sq, 2 * pp:2 * pp + 2].rearrange("h v k -> (h v) k"),
                                  f"stw{pi}")
                            bb = bank()
                            for h in range(2):
                                P.mm(bb[hk[h], 0:64], T[pi][hk[h], 0:64], identf[hk[h], hk[h]],
                                     tp=(h * 64, h * 64))
                            P.ins("dve", "tensor_copy", out=Pst[:, pp, :], in_=bb[:, 0:64])
                            P.ins("act", "activation", out=Pbf[:, pp, :], in_=Pst[:, pp, :], func=AF.Copy)
                    else:
                        for pi in range(2):
                            pp = 2 * q + pi
                            P.ins("dve", "memset", ap=Pst[:, pp, :], constant=0.0)
                            P.ins("dve", "memset", ap=Pbf[:, pp, :], constant=0.0)

                def part_A(ci):
                    c0, sq, first, last = chunks[ci]
                    rg = ci % 4
                    cc = slice(c0, c0 + C)
                    for pi in range(2):
                        At, Rt, Bt, Kt, Bh, Kh, Vb = [OPS[:, pi, i, :] for i in range(7)]
                        bsc = bank()
                        for h in range(2):
                            tp = (h * 64, h * 64)
                            for i, (l_, r_) in enumerate(((Bt, At), (At, Bt), (Kt, At), (Bt, Rt), (Kt, Rt))):
                                P.mm(bsc[hs[h], i * 64:i * 64 + C], l_[hk[h], cc], r_[hk[h], cc], tp=tp)
                        for h in (range(1) if C == 64 else range(2)):
                            rws = slice(0, 128) if C == 64 else hs[h]
                            P.ins("dve", "tensor_tensor", out=SC[rg][rws, pi, :, 0:C],
                                  in0=bsc[rws, 0:320].rearrange("p (i t) -> p i t", t=64)[:, :, 0:C],
                                  in1=mask5[rws, :, 0:C], op=ALU.mult)
                        btm = bank()
                        for h in range(2):
                            tp = (h * 64, h * 64)
                            for i, src in enumerate((Vb, Bh, Kh)):
                                P.mm(btm[hs[h], i * 64:i * 64 + 64], src[hk[h], cc], identb[hk[h], hk[h]], tp=tp)
                        for h in (range(1) if C == 64 else range(2)):
                            rws = slice(0, 128) if C == 64 else hs[h]
                            P.ins("act", "activation", out=TM3[rg][rws, pi, :, :],
                                  in_=btm[rws, 0:192].rearrange("p (i t) -> p i t", t=64), func=AF.Copy)

                def part_B(ci, j):
                    rg4 = ci % 4
                    rg = ci % 2
                    for pi in range(2):
                        if j == 0:
                            Qc = SC[rg4][:, pi, 0, :]
                            Pc = SC[rg4][:, pi, 1, :]
                        else:
                            Qc = QP[rg][(j - 1) % 2][:, pi, 0, :]
                            Pc = QP[rg][(j - 1) % 2][:, pi, 1, :]
                        dstR = RF[rg4][:, pi, :] if j == NJ - 1 else RR[rg][j % 2][:, pi, :]
                        if j == 0:
                            for hh in range(2):
                                P.ins("dve", "tensor_tensor", out=dstR[hs[hh], 0:C], in0=Qc[hs[hh], 0:C],
                                      in1=identb[hs[hh], hh * 64:hh * 64 + C], op=ALU.add)
                        else:
                            Rp = RR[rg][(j - 1) % 2][:, pi, :]
                            bq = bank()
                            for h in range(2):
                                P.mm(bq[hs[h], 0:C], Pc[hs[h], 0:C], Rp[hs[h], 0:C], tp=(h * 64, h * 64))
                            for h in (range(1) if C == 64 else range(2)):
                                rws = slice(0, 128) if C == 64 else hs[h]
                                P.ins("dve", "tensor_tensor", out=dstR[rws, 0:C], in0=bq[rws, 0:C], in1=Rp[rws, 0:C],
                                      op=ALU.add)
                        if j < NJ - 1:
                            lastsq = j == NJ - 2
                            bq2 = bank()
                            for h in range(2):
                                tp = (h * 64, h * 64)
                                if not lastsq:
                                    P.mm(bq2[hs[h], 0:C], Pc[hs[h], 0:C], Qc[hs[h], 0:C], tp=tp)
                                P.mm(bq2[hs[h], 64:64 + C], Qc[hs[h], 0:C], Pc[hs[h], 0:C], tp=tp)
                            for h in (range(1) if C == 64 else range(2)):
                                rws = slice(0, 128) if C == 64 else hs[h]
                                if lastsq:
                                    P.ins("act", "activation", out=QP[rg][j % 2][rws, pi, 1, 0:C],
                                          in_=bq2[rws, 64:64 + C], func=AF.Copy)
                                else:
                                    P.ins("act", "activation", out=QP[rg][j % 2][rws, pi, :, 0:C],
                                          in_=bq2[rws, 0:128].rearrange("p (i t) -> p i t", t=64)[:, :, 0:C],
                                          func=AF.Copy)

                def part_C1(ci):
                    c0, sq, first, last = chunks[ci]
                    rg = ci % 4
                    cc = slice(c0, c0 + C)
                    load_state(ci)
                    for pi in range(2):
                        pp = 2 * q + pi
                        At = OPS[:, pi, 0, :]
                        X0 = XU[:, pi, 0, :]
                        bx = bank()
                        for h in range(2):
                            tp = (h * 64, h * 64)
                            P.mm(bx[hs[h], 0:64], At[hk[h], cc], Pbf[hk[h], pp, :], start=True, stop=False, tp=tp)
                            P.mm(bx[hs[h], 0:64], SC[rg][hs[h], pi, 2, 0:C], TM3[rg][hs[h], pi, 0, :],
                                 start=False, stop=True, tp=tp)
                        for h in (range(1) if C == 64 else range(2)):
                            rws = slice(0, 128) if C == 64 else hs[h]
                            P.ins("act", "activation", out=X0[rws, :], in_=bx[rws, 0:64], func=AF.Copy)

                def part_C2(ci):
                    rg = ci % 4
                    for pi in range(2):
                        X0 = XU[:, pi, 0, :]
                        U = XU[:, pi, 1, :]
                        bu_ = bank()
                        for h in range(2):
                            tp = (h * 64, h * 64)
                            P.mm(bu_[hs[h], 0:64], RF[rg][hs[h], pi, 0:C], X0[hs[h], :], tp=tp)
                        for h in (range(1) if C == 64 else range(2)):
                            rws = slice(0, 128) if C == 64 else hs[h]
                            P.ins("dve", "tensor_copy", out=U[rws, :], in_=bu_[rws, 0:64])

                def part_C3(ci):
                    c0, sq, first, last = chunks[ci]
                    rg = ci % 4
                    cc = slice(c0, c0 + C)
                    for pi in range(2):
                        pp = 2 * q + pi
                        Rt = OPS[:, pi, 1, :]
                        U = XU[:, pi, 1, :]
                        by = bank()
                        for h in range(2):
                            tp = (h * 64, h * 64)
                            P.mm(by[hs[h], 0:64], Rt[hk[h], cc], Pbf[hk[h], pp, :], start=True, stop=False, tp=tp)
                            P.mm(by[hs[h], 0:64], SC[rg][hs[h], pi, 3, 0:C], U[hs[h], :], start=False, stop=False, tp=tp)
                            P.mm(by[hs[h], 0:64], SC[rg][hs[h], pi, 4, 0:C], TM3[rg][hs[h], pi, 0, :],
                                 start=False, stop=True, tp=tp)
                        for h in (range(1) if C == 64 else range(2)):
                            rws = slice(0, 128) if C == 64 else hs[h]
                            P.ins("act", "activation", out=YQ[rws, pi, ci, :], in_=by[rws, 0:64], func=AF.Copy)
                        bp = bank()
                        for h in range(2):
                            tp = (h * 64, h * 64)
                            P.mm(bp[hk[h], 0:64], TM3[rg][hs[h], pi, 1, :], U[hs[h], :], start=True, stop=False, tp=tp)
                            P.mm(bp[hk[h], 0:64], TM3[rg][hs[h], pi, 2, :], TM3[rg][hs[h], pi, 0, :],
                                 start=False, stop=True, tp=tp)
                        P.ins("dve", "scalar_tensor_tensor", out=Pst[:, pp, :], in0=Pst[:, pp, :],
                              scalar=WC[:, pi, ci:ci + 1], in1=bp[:, 0:64], op0=ALU.mult, op1=ALU.add)
                        P.ins("act", "activation", out=Pbf[:, pp, :], in_=Pst[:, pp, :], func=AF.Copy)
                        if last and (sq < 4 or last_tile):
                            bb = bank()
                            for h in range(2):
                                P.mm(bb[hk[h], 0:64], Pst[hk[h], pp, :], identf[hk[h], hk[h]], tp=(h * 64, h * 64))
                            so = T[0][:, 64 * pi:64 * pi + 64]
                            P.ins("dve", "tensor_copy", out=so, in_=bb[:, 0:64])
                            P.dma("sync", o_wkv[sq, 2 * pp:2 * pp + 2].rearrange("h v k -> (h v) k"), so, f"owkv{pi}")

                def c_steps(grp):
                    st_ = []
                    for ci in grp:
                        st_ += [lambda ci=ci: part_C1(ci), lambda ci=ci: part_C2(ci), lambda ci=ci: part_C3(ci)]
                    return st_

                pend = []
                for g0 in range(0, nch, 2):
                    grp = list(range(g0, min(nch, g0 + 2)))
                    for ci in grp:
                        part_A(ci)
                    for j in range(NJ):
                        for ci in grp:
                            part_B(ci, j)
                        if pend:
                            pend.pop(0)()
                    while pend:
                        pend.pop(0)()
                    pend = c_steps(grp)
                while pend:
                    pend.pop(0)()
                chk("rw_dep", cur["ti"])

                ng = 2 * nch
                yq = YQ[:, :, 0:nch, :]
                s1 = LST[:, 0, 0:ng].rearrange("p (a b) -> p a b", a=2)
                s2 = LST[:, 1, 0:ng].rearrange("p (a b) -> p a b", a=2)
                mean = LST[:, 2, 0:ng].rearrange("p (a b) -> p a b", a=2)
                var = LST[:, 3, 0:ng].rearrange("p (a b) -> p a b", a=2)
                P.ins("dve", "tensor_reduce", out=s1, in_=yq, axis=AX.X, op=ALU.add)
                P.ins("act", "activation", out=YSQ[:, :, 0:nch, :], in_=yq, func=AF.Square)
                P.ins("dve", "tensor_reduce", out=s2, in_=YSQ[:, :, 0:nch, :], axis=AX.X, op=ALU.add)
                P.ins("dve", "tensor_scalar", out=mean, in0=s1, scalar1=1.0 / 64, scalar2=None, op0=ALU.mult)
                P.ins("dve", "tensor_tensor", out=var, in0=mean, in1=mean, op=ALU.mult)
                P.ins("dve", "scalar_tensor_tensor", out=var, in0=s2, scalar=1.0 / 64, in1=var, op0=ALU.mult,
                      op1=ALU.subtract)
                P.ins("dve", "tensor_scalar", out=var, in0=var, scalar1=RW_EPS, scalar2=None, op0=ALU.add)
                P.ins("act", "activation", out=var, in_=var, func=AF.Sqrt)
                P.ins("dve", "reciprocal", out=var, in_=var)
                P.ins("dve", "tensor_tensor", out=YSQ[:, :, 0:nch, :], in0=yq,
                      in1=mean.unsqueeze(3).to_broadcast([128, 2, nch, 64]), op=ALU.subtract)
                P.ins("dve", "tensor_tensor", out=YN[:, :, 0:nch, :], in0=YSQ[:, :, 0:nch, :],
                      in1=var.unsqueeze(3).to_broadcast([128, 2, nch, 64]), op=ALU.mult)
                for pi in range(2):
                    pp = 2 * q + pi
                    bt_ = bank()
                    for ci, (c0, sq, first, last) in enumerate(chunks):
                        for h in range(2):
                            P.mm(bt_[hk[h], c0:c0 + C], YN[hs[h], pi, ci, :], identb[hs[h], h * 64:h * 64 + C],
                                 tp=(h * 64, h * 64))
                    zt = T[0]
                    P.ins("dve", "scalar_tensor_tensor", out=zt[:, :NT], in0=bt_[:, :NT], scalar=rwc[:, 5, pp:pp + 1],
                          in1=BON[:, pi, :NT], op0=ALU.mult, op1=ALU.add)
                    P.ins("dve", "tensor_tensor", out=yrwT[:, pp, :NT], in0=zt[:, :NT], in1=GF[:, pi, :NT], op=ALU.mult)

        def mlstm(NT, segs, L, last_tile):
            nseg = len(segs)
            chunks = []
            for si, (c0, ln, sq) in enumerate(segs):
                for cc in range(0, ln, L):
                    chunks.append((c0 + cc, sq, cc == 0, cc + L >= ln, len(chunks), si))
            nch = len(chunks)
            o = 0

            def cv(shape, dt):
                nonlocal o
                v = carve(o, shape, dt)
                o += ((int(np.prod(shape)) * _dsz(dt) + 3) // 4) * 4
                return v
            cbuf = cv([544], F32)
            acc = cv([512], F32)
            stmp = cv([512], F32)
            irow = cv([520], F32)
            frow = cv([520], F32)
            Frow = cv([520], F32)
            Mrow = cv([520], F32)
            gcol = cv([5, 12], F32)
            qT = cv([4, 512], BF16)
            kT = cv([4, 512], BF16)
            vext = cv([5, 2, 258], BF16)
            sigo = cv([4, 512], F32)
            NUMr = [cv([2, 258], F32) for _ in range(2)]
            tqc = [cv([258], F32) for _ in range(2)]
            Eb = [cv([128], F32) for _ in range(2)]
            STb = [cv([128], BF16) for _ in range(2)]
            kTM = [cv([256], BF16) for _ in range(2)]
            hn = cv([2, 256], F32)
            hsq = cv([2, 256], F32)
            ynb = cv([2, 256], BF16)
            mpe_r = [cv([8], F32) for _ in range(2)]
            lst = cv([8, 2], F32)
            wcol_r = [cv([16], F32) for _ in range(2)]

            wt = wload(Win[:, :, O_MI:O_MI + 8], [16, 8])
            bi = bank()
            bf_ = bank()
            for kc in range(16):
                P.mm(bi[0:4, :NT], wt[:, kc, 0:4], xnT[:, kc, :NT], start=kc == 0, stop=kc == 15)
            for kc in range(16):
                P.mm(bf_[0:4, :NT], wt[:, kc, 4:8], xnT[:, kc, :NT], start=kc == 0, stop=kc == 15)
            P.ins("act", "activation", out=irow[0:4, :NT], in_=bi[0:4, :NT], func=AF.Identity, bias=ifb[0:4, 0:1])
            P.ins("act", "activation", out=frow[0:4, :NT], in_=bf_[0:4, :NT], func=AF.Sigmoid, bias=ifb[0:4, 1:2])
            P.ins("act", "activation", out=frow[0:4, :NT], in_=frow[0:4, :NT], func=AF.Ln)
            for si, (c0, ln, sq) in enumerate(segs):
                sl = slice(c0, c0 + ln)
                P.ins("dve", "tensor_tensor_scan", out=Frow[0:4, sl], data0=one512[0:4, 0:ln], data1=frow[0:4, sl],
                      initial=0.0, op0=ALU.mult, op1=ALU.add)
                P.ins("dve", "tensor_tensor", out=irow[0:4, sl], in0=irow[0:4, sl], in1=Frow[0:4, sl], op=ALU.subtract)
                mp = c0 + si + 1
                P.ins("dve", "tensor_copy", out=Mrow[0:4, mp - 1:mp], in_=mrow[0:4, sq:sq + 1])
                P.ins("dve", "tensor_tensor_scan", out=Mrow[0:4, mp:mp + ln], data0=one512[0:4, 0:ln],
                      data1=irow[0:4, sl], initial=mrow[0:4, sq:sq + 1], op0=ALU.mult, op1=ALU.max)
                P.ins("dve", "tensor_tensor", out=Frow[0:4, sl], in0=Frow[0:4, sl], in1=Mrow[0:4, mp:mp + ln], op=ALU.add)
                P.ins("dve", "tensor_copy", out=mrow[0:4, sq:sq + 1], in_=Frow[0:4, c0 + ln - 1:c0 + ln])
            for (c0, sq, first, last, slot, si) in chunks:
                bb = bank()
                mp = c0 + si + 1
                P.mm(bb[:L, 0:4], Mrow[0:4, mp:mp + L], identf[0:4, 0:4])
                P.mm(bb[:L, 4:8], irow[0:4, c0:c0 + L], identf[0:4, 0:4])
                P.mm(bb[:L, 8:12], Frow[0:4, c0:c0 + L], identf[0:4, 0:4])
                P.ins("dve", "tensor_copy", out=gcol[:L, slot, :], in_=bb[:L, 0:12])

            P.ins("pool", "memset", ap=vext[:, :, :, 256:257], constant=1.0)
            chk("ml_gate", cur["ti"])

            for hp in range(2):
                for which, base, dstT, scl in (("q", O_MQ, qT, 1.0), ("k", O_MQ + 1024, kT, 0.0625)):
                    def conv_out(j, bb, m, which=which, base=base, dstT=dstT, scl=scl):
                        g = (0 if which == "q" else 8) + hp * 4 + 2 * half + j
                        for si, (c0, ln, sq) in enumerate(segs):
                            pos = c0 + 3 * si
                            P.ins("act", "activation", out=cbuf[:, pos + 3:pos + 3 + ln], in_=bb[:, c0:c0 + ln],
                                  func=AF.Copy)
                            P.ins("dve", "tensor_copy", out=cbuf[:, pos:pos + 3], in_=ccar[:, g, sq, :])
                        ln = segs[0][1]
                        cvw = cbuf[:, 0:nseg * (ln + 3)].rearrange("p (s t) -> p s t", t=ln + 3)
                        av_ = acc[:, 0:nseg * ln].rearrange("p (s t) -> p s t", t=ln)
                        P.ins("dve", "tensor_scalar", out=av_, in0=cvw[:, :, 0:ln], scalar1=cwc[:, g, 0:1],
                              scalar2=cbc[:, g:g + 1], op0=ALU.mult, op1=ALU.add)
                        for tap in range(1, 4):
                            P.ins("dve", "scalar_tensor_tensor", out=av_, in0=cvw[:, :, tap:tap + ln],
                                  scalar=cwc[:, g, tap:tap + 1], in1=av_, op0=ALU.mult, op1=ALU.add)
                        for si, (c0, ln_, sq) in enumerate(segs):
                            pos = c0 + 3 * si
                            P.ins("act", "activation", out=ccar[:, g, sq, :], in_=cbuf[:, pos + ln_:pos + ln_ + 3],
                                  func=AF.Copy)
                        dd = dstT[:, 2 * half + j, :NT]
                        if scl == 1.0:
                            P.ins("act", "activation", out=dd, in_=acc[:, :NT], func=AF.Silu)
                        else:
                            P.ins("act", "activation", out=stmp[:, :NT], in_=acc[:, :NT], func=AF.Silu)
                            P.ins("dve", "tensor_scalar", out=dd, in0=stmp[:, :NT], scalar1=scl, scalar2=None,
                                  op0=ALU.mult)
                    for half in range(2):
                        proj_fm(base + hp * 512 + half * 256, 256, NT, conv_out)
                def po_out(j, bb, m):
                    P.ins("act", "activation", out=sigo[:, 2 * half + j, :NT], in_=bb[:, :NT], func=AF.Sigmoid)
                for half in range(2):
                    proj_fm(O_MO + hp * 512 + half * 256, 256, NT, po_out)
                for hh in range(2):
                    h = 2 * hp + hh
                    wv = wload(Win[:, :, O_MV + h * 256:O_MV + h * 256 + 256], [16, 256])
                    for (c0, sq, first, last, slot, si) in chunks:
                        bb = bank()
                        for kc in range(16):
                            P.mm(bb[:L, 0:256], xnT[:, kc, c0:c0 + L], wv[:, kc, :], start=kc == 0, stop=kc == 15)
                        P.ins("act", "activation", out=vext[:L, slot, hh, 0:256], in_=bb[:L, 0:256], func=AF.Copy)
                chk("ml_proj", cur["ti"])

                for (c0, sq, first, last, slot, si) in chunks:
                    cc = slice(c0, c0 + L)
                    mp = c0 + si + 1
                    if first and NT == 80:
                        if sq < 4:
                            for hh in range(2):
                                h = 2 * hp + hh
                                P.dma("sync", Cst[:, h, :, 0:256],
                                      st_C[sq, h].rearrange("(dc p) v -> p dc v", p=128), f"stc{hh}")
                                P.dma("sync", Cst[:, h, :, 256:257],
                                      st_n[sq, h].rearrange("(dc p one) -> p dc one", p=128, one=1), f"stc{hh}",
                                      allow_slow_non_contiguous=True)
                        else:
                            for hh in range(2):
                                h = 2 * hp + hh
                                P.ins("dve", "memset", ap=Cst[:, h, :, :], constant=0.0)
                        for hh in range(2):
                            h = 2 * hp + hh
                            P.ins("act", "activation", out=Cbf[:, h, :, 0:257], in_=Cst[:, h, :, :], func=AF.Copy)
                    NUM = NUMr[slot % 2]
                    for hh in range(2):
                        h = 2 * hp + hh
                        r2 = (slot * 2 + hh) % 2
                        mpe = mpe_r[hh]
                        wcol = wcol_r[hh]
                        bS = bank()
                        for dc in range(2):
                            P.mm(bS[:L, 0:L], kT[:, 2 * hh + dc, cc], qT[:, 2 * hh + dc, cc], start=dc == 0, stop=dc == 1)
                        bM = bank()
                        P.mm(bM[:, 0:L + 1], sel4[0:4, h * 128:h * 128 + 128], Mrow[0:4, mp - 1:mp + L])
                        P.ins("act", "activation", out=mpe[:, 0:1], in_=bM[:, 0:1], func=AF.Copy)
                        P.ins("act", "activation", out=mpe[:, 1:2], in_=bM[:, L:L + 1], func=AF.Copy)
                        E = Eb[r2]
                        P.ins("act", "activation", out=E[:L, 0:L], in_=bM[:L, 1:L + 1], func=AF.Exp, scale=-1.0,
                              bias=gcol[:L, slot, 4 + h:5 + h])
                        P.ins("pool", "tensor_tensor", out=E[:L, 0:L], in0=E[:L, 0:L], in1=mlmask[:L, 0:L], op=ALU.mult)
                        ST = STb[r2]
                        P.ins("dve", "tensor_tensor", out=ST[:L, 0:L], in0=bS[:L, 0:L], in1=E[:L, 0:L], op=ALU.mult)
                        P.ins("act", "activation", out=wcol[:L, 0:1], in_=gcol[:L, slot, h:h + 1], func=AF.Exp,
                              scale=-1.0, bias=mpe[:L, 0:1])
                        P.ins("act", "activation", out=wcol[:, 1:2], in_=mpe[:, 1:2], func=AF.Exp, scale=-1.0,
                              bias=mpe[:, 0:1])
                        P.ins("act", "activation", out=wcol[:L, 2:3], in_=mpe[:L, 1:2], func=AF.Exp, scale=-1.0,
                              bias=gcol[:L, slot, 4 + h:5 + h])
                        bQ = bank()
                        for dc in range(2):
                            P.mm(bQ[:L, 0:257], qT[:, 2 * hh + dc, cc], Cbf[:, h, dc, 0:257], start=dc == 0, stop=dc == 1)
                        bI = bank()
                        P.mm(bI[:L, 0:257], ST[:L, 0:L], vext[:L, slot, hh, 0:257])
                        tq = tqc[r2]
                        P.ins("act", "activation", out=tq[:L, 0:257], in_=bQ[:L, 0:257], func=AF.Identity,
                              scale=wcol[:L, 0:1])
                        P.ins("dve", "tensor_tensor", out=NUM[:L, hh, 0:257], in0=tq[:L, 0:257], in1=bI[:L, 0:257],
                              op=ALU.add)
                        chk("ml_num", cur["ti"])
                        bK = bank()
                        for dc in range(2):
                            P.mm(bK[:L, dc * 128:(dc + 1) * 128], kT[:, 2 * hh + dc, cc], identb[:, :])
                        km = kTM[r2]
                        P.ins("act", "activation", out=km[:L, 0:256], in_=bK[:L, 0:256], func=AF.Identity,
                              scale=wcol[:L, 2:3])
                        for dc in range(2):
                            bC = bank()
                            P.mm(bC[:, 0:257], km[:L, dc * 128:(dc + 1) * 128], vext[:L, slot, hh, 0:257])
                            P.ins("dve", "scalar_tensor_tensor", out=Cst[:, h, dc, :], in0=Cst[:, h, dc, :],
                                  scalar=wcol[:, 1:2], in1=bC[:, 0:257], op0=ALU.mult, op1=ALU.add)
                        P.ins("act", "activation", out=Cbf[:, h, :, 0:257], in_=Cst[:, h, :, :], func=AF.Copy)
                        if last and (sq < 4 or last_tile):
                            pass
                        chk("ml_st", cur["ti"])
                        if last and (sq < 4 or last_tile):
                            P.dma("sync", o_C[sq, h].rearrange("(dc p) v -> p dc v", p=128), Cst[:, h, :, 0:256],
                                  f"oc{hh}")
                            P.dma("sync", o_n[sq, h].rearrange("(dc p one) -> p dc one", p=128, one=1),
                                  Cst[:, h, :, 256:257], f"oc{hh}", allow_slow_non_contiguous=True)
                    P.ins("act", "activation", out=lst[:L, 0, :], in_=NUM[:L, :, 256], func=AF.Abs)
                    P.ins("act", "activation", out=lst[:L, 1, :], in_=gcol[:L, slot, 8 + 2 * hp:10 + 2 * hp],
                          func=AF.Exp, scale=-1.0)
                    P.ins("dve", "tensor_tensor", out=lst[:L, 0, :], in0=lst[:L, 0, :], in1=lst[:L, 1, :], op=ALU.max)
                    P.ins("dve", "reciprocal", out=lst[:L, 0, :], in_=lst[:L, 0, :])
                    P.ins("dve", "tensor_tensor", out=hn[:L, :, :], in0=NUM[:L, :, 0:256],
                          in1=lst[:L, 0, :].unsqueeze(2).to_broadcast([L, 2, 256]), op=ALU.mult)
                    P.ins("dve", "tensor_reduce", out=lst[:L, 2, :], in_=hn[:L, :, :], axis=AX.X, op=ALU.add)
                    P.ins("act", "activation", out=hsq[:L, :, :], in_=hn[:L, :, :], func=AF.Square)
                    P.ins("dve", "tensor_reduce", out=lst[:L, 3, :], in_=hsq[:L, :, :], axis=AX.X, op=ALU.add)
                    P.ins("dve", "tensor_scalar", out=lst[:L, 2, :], in0=lst[:L, 2, :], scalar1=1.0 / 256, scalar2=None,
                          op0=ALU.mult)
                    P.ins("dve", "tensor_tensor", out=lst[:L, 4, :], in0=lst[:L, 2, :], in1=lst[:L, 2, :], op=ALU.mult)
                    P.ins("dve", "scalar_tensor_tensor", out=lst[:L, 4, :], in0=lst[:L, 3, :], scalar=1.0 / 256,
                          in1=lst[:L, 4, :], op0=ALU.mult, op1=ALU.subtract)
                    P.ins("dve", "tensor_scalar", out=lst[:L, 4, :], in0=lst[:L, 4, :], scalar1=ML_EPS, scalar2=None,
                          op0=ALU.add)
                    P.ins("act", "activation", out=lst[:L, 4, :], in_=lst[:L, 4, :], func=AF.Sqrt)
                    P.ins("dve", "reciprocal", out=lst[:L, 4, :], in_=lst[:L, 4, :])
                    P.ins("dve", "tensor_tensor", out=hsq[:L, :, :], in0=hn[:L, :, :],
                          in1=lst[:L, 2, :].unsqueeze(2).to_broadcast([L, 2, 256]), op=ALU.subtract)
                    P.ins("dve", "tensor_tensor", out=ynb[:L, :, :], in0=hsq[:L, :, :],
                          in1=lst[:L, 4, :].unsqueeze(2).to_broadcast([L, 2, 256]), op=ALU.mult)
                    chk("ml_ln", cur["ti"])
                    bT = bank()
                    ynf = ynb[:, :, :].rearrange("p a b -> p (a b)")
                    for gq in range(4):
                        P.mm(bT[:, gq * 128:gq * 128 + L], ynf[:L, gq * 128:(gq + 1) * 128], identb[:L, :L])
                    for gq in range(4):
                        g = hp * 4 + gq
                        P.ins("dve", "scalar_tensor_tensor", out=ymlT[:, g, cc], in0=bT[:, gq * 128:gq * 128 + L],
                              scalar=nwc[:, g:g + 1], in1=sigo[:, gq, cc], op0=ALU.mult, op1=ALU.mult)
                    chk("ml_epi", cur["ti"])

        def merge(NT, subt):
            mg = carve(0, [NKC, 512], BF16)
            t1 = carve(16384, [512], F32)
            t2 = carve(18432, [512], F32)
            t3 = carve(20480, [512], F32)
            Wr = W["w_br_rw"].rearrange("(kc p) d -> p kc d", p=128)
            Wm = W["w_br_ml"].rearrange("(kc p) d -> p kc d", p=128)
            for f0 in range(0, D, 256):
                wg1 = wload(Win[:, :, O_G1 + f0:O_G1 + f0 + 256], [16, 256])
                wg2 = wload(Win[:, :, O_G2 + f0:O_G2 + f0 + 256], [16, 256])
                wr = wload(Wr[:, :, f0:f0 + 256], [8, 256])
                wm = wload(Wm[:, :, f0:f0 + 256], [8, 256])
                for j in range(2):
                    fc = f0 // 128 + j
                    cs_ = slice(j * 128, (j + 1) * 128)
                    b1, b2, b3, b4 = bank(), bank(), bank(), bank()
                    for kc in range(16):
                        P.mm(b1[:, :NT], wg1[:, kc, cs_], xnT[:, kc, :NT], start=kc == 0, stop=kc == 15)
                    for kc in range(16):
                        P.mm(b2[:, :NT], wg2[:, kc, cs_], xnT[:, kc, :NT], start=kc == 0, stop=kc == 15)
                    for kc in range(8):
                        P.mm(b3[:, :NT], wr[:, kc, cs_], yrwT[:, kc, :NT], start=kc == 0, stop=kc == 7)
                    for kc in range(8):
                        P.mm(b4[:, :NT], wm[:, kc, cs_], ymlT[:, kc, :NT], start=kc == 0, stop=kc == 7)
                    P.ins("act", "activation", out=t1[:, :NT], in_=b1[:, :NT], func=AF.Sigmoid)
                    P.ins("act", "activation", out=t2[:, :NT], in_=b2[:, :NT], func=AF.Sigmoid)
                    P.ins("dve", "tensor_tensor", out=t1[:, :NT], in0=t1[:, :NT], in1=b3[:, :NT], op=ALU.mult)
                    P.ins("dve", "tensor_tensor", out=t2[:, :NT], in0=t2[:, :NT], in1=b4[:, :NT], op=ALU.mult)
                    P.ins("pool", "tensor_tensor", out=mg[:, fc, :NT], in0=t1[:, :NT], in1=t2[:, :NT], op=ALU.add)
            Wo = W["w_out"].rearrange("(kc p) d -> p kc d", p=128)
            for cb in range(4):
                bks = [bank() for _ in subt]
                for k0 in range(0, 16, 8):
                    wo = wload(Wo[:, k0:k0 + 8, cb * 512:(cb + 1) * 512], [8, 512])
                    for si, (st, n) in enumerate(subt):
                        for k in range(8):
                            P.mm(bks[si][:n, :], mg[:, k0 + k, st * 128:st * 128 + n], wo[:, k, :],
                                 start=(k0 + k == 0), stop=(k0 + k == 15))
                for si, (st, n) in enumerate(subt):
                    xs_ = xres[:n, st, cb * 512:(cb + 1) * 512]
                    P.ins("dve", "tensor_tensor", out=xs_, in0=bks[si][:n, :], in1=xs_, op=ALU.add)

        try:
            for ti in range(NPT + 1):
                last_tile = ti == NPT
                cur["ti"] = ti
                wl_state["n"] = 0
                if ti == 0:
                    NT = 80
                    subt = [(0, 80)]
                    P.dma("sync", xres[0:64, 0, :], xs, "xin0")
                    P.dma("sync", xres[64:80, 0, :], meta, "xin0")
                    segs = [(16 * j, 16, j) for j in range(5)]
                    Crw, Lml = 16, 16
                else:
                    NT = 512
                    subt = [(st, 128) for st in range(4)]
                    for st in range(4):
                        r0 = (ti - 1) * 512 + st * 128
                        P.dma("sync", xres[:, st, :], xp[r0:r0 + 128, :], f"xin{st}")
                    segs = [(0, 512, 4)]
                    Crw, Lml = 64, 128
                chk("load", ti)
                ffn("ffn1", 0, NT, subt)
                chk("ffn1", ti)
                if dbg and ti == 1:
                    for st in range(4):
                        P.dma("sync", dbg_out["d_x1"][st * 128:(st + 1) * 128, :], xres[:, st, :], "dbg")
                rmsnorm_T(1, subt)
                rwkv(NT, segs, Crw, last_tile)
                chk("rwkv", ti)
                mlstm(NT, segs, Lml, last_tile)
                chk("mlstm", ti)
                if dbg and ti == 1:
                    P.dma("sync", dbg_out["d_yrw"], yrwT[:], "dbg")
                    P.dma("sync", dbg_out["d_yml"], ymlT[:], "dbg")
                merge(NT, subt)
                chk("merge", ti)
                if dbg and ti == 1:
                    for st in range(4):
                        P.dma("sync", dbg_out["d_x2"][st * 128:(st + 1) * 128, :], xres[:, st, :], "dbg")
                ffn("ffn2", 2, NT, subt)
                for st, n in subt:
                    xsb = xsbs[st % 2]
                    ssq = small[:n, 16 + 2 * st:17 + 2 * st]
                    rstd = small[:n, 17 + 2 * st:18 + 2 * st]
                    P.ins("act", "activation", out=xsb[:n, :], in_=xres[:n, st, :], func=AF.Square, accum_out=ssq)
                    P.ins("dve", "tensor_scalar", out=rstd, in0=ssq, scalar1=1.0 / D, scalar2=1e-6, op0=ALU.mult, op1=ALU.add)
                    P.ins("act", "activation", out=rstd, in_=rstd, func=AF.Sqrt)
                    P.ins("dve", "reciprocal", out=rstd, in_=rstd)
                    P.ins("dve", "scalar_tensor_tensor", out=xres[:n, st, :], in0=xres[:n, st, :], scalar=rstd,
                          in1=gfin[:n, :], op0=ALU.mult, op1=ALU.mult)
                    if ti == 0:
                        P.dma("sync", ys, xres[0:64, 0, :], "yout0")
                    else:
                        r0 = (ti - 1) * 512 + st * 128
                        P.dma("sync", yp[r0:r0 + 128, :], xres[:, st, :], f"yout{st}")

        except _Stop:
            pass
        if stop is not None:
            P.emit()
            return nc
        stg = carve(0, [3392], F32)
        stg2 = carve(16384, [2048], F32)
        for r in range(7):
            bb = bank()
            gs = list(range(r * 4, min(27, r * 4 + 4)))
            for g in gs:
                n = 128 if g < 26 else 32
                P.mm(bb[0:5, (g - r * 4) * 128:(g - r * 4) * 128 + n], carry[:n, g, :], identf[:n, :n])
            c0 = r * 512
            c1 = min(RWC, c0 + 512)
            P.ins("dve", "tensor_copy", out=stg[0:5, c0:c1], in_=bb[0:5, 0:c1 - c0])
        P.dma("sync", o_shift, stg[0:5, 0:RWC], "ofin")
        for r in range(4):
            bb = bank()
            for g in range(r * 4, r * 4 + 4):
                P.mm(bb[0:15, (g - r * 4) * 128:(g - r * 4 + 1) * 128],
                     ccar[:, g, :, :].rearrange("p s j -> p (s j)"), identf[:, :])
            P.ins("dve", "tensor_copy", out=stg2[0:15, r * 512:(r + 1) * 512], in_=bb[0:15, :])
        P.dma("sync", o_conv.rearrange("s j c -> (s j) c"), stg2[0:15, :], "ofin")
        P.dma("sync", o_m.rearrange("s h -> h s"), mrow[0:4, 0:5], "ofin", allow_slow_non_contiguous=True)
        P.emit()
    return nc


_CACHE = {}


def _get_nc(NPT, dbg=False):
    k = (NPT, dbg)
    if k not in _CACHE:
        _CACHE[k] = build(NPT, dbg)
    return _CACHE[k]


def make_in_maps(inputs, ncores, NPT):
    cst = make_consts()
    maps = []
    for c in range(ncores):
        m = {
            "xp": np.ascontiguousarray(inputs["x_prompt"][c, :NPT * 512]),
            "xs": np.ascontiguousarray(inputs["x_sample"][4 * c:4 * c + 4].reshape(64, D)),
            "meta": np.ascontiguousarray(inputs["meta_tokens"]),
            "st_shift": np.ascontiguousarray(inputs["state_rwkv_shift"][0, 4 * c:4 * c + 4]),
            "st_wkv": np.ascontiguousarray(inputs["state_rwkv_wkv"][0, 4 * c:4 * c + 4]),
            "st_conv": np.ascontiguousarray(inputs["state_mlstm_conv"][0, 4 * c:4 * c + 4]),
            "st_C": np.ascontiguousarray(inputs["state_mlstm_C"][0, 4 * c:4 * c + 4]),
            "st_n": np.ascontiguousarray(inputs["state_mlstm_n"][0, 4 * c:4 * c + 4]),
            "st_m": np.ascontiguousarray(inputs["state_mlstm_m"][0, 4 * c:4 * c + 4]),
            "cst": cst,
        }
        for nm, shp in WSHAPES:
            m[nm] = np.ascontiguousarray(np.asarray(inputs[nm]).reshape(shp))
        maps.append(m)
    return maps


def assemble(results, ncores):
    f = np.float32
    cat = lambda k, sl: np.concatenate([np.asarray(r[k])[sl] for r in results], 0)
    y_prompt = np.stack([np.asarray(r["yp"]) for r in results], 0).astype(f)
    y_sample = np.concatenate([np.asarray(r["ys"]).reshape(4, 16, D) for r in results], 0).astype(f)
    outs = [y_prompt, y_sample]
    for k in ("o_shift", "o_wkv", "o_conv", "o_C", "o_n", "o_m"):
        outs.append(cat(k, slice(4, 5))[None].astype(f))
    for k in ("o_shift", "o_wkv", "o_conv", "o_C", "o_n", "o_m"):
        outs.append(cat(k, slice(0, 4))[None].astype(f))
    return tuple(outs)


def kernel(**inputs):
    inputs = {k: np.asarray(v) for k, v in inputs.items()}
    NPT = inputs["x_prompt"].shape[1] // 512
    ncores = inputs["x_prompt"].shape[0]
    nc = _get_nc(NPT)
    in_maps = make_in_maps(inputs, ncores, NPT)
    res = run_bass_kernel_spmd(nc, in_maps, core_ids=list(range(ncores)))
    return assemble(res.results, ncores)
```

```python
import numpy as np
from contextlib import ExitStack
import concourse.bass as bass
import concourse.mybir as mybir
from concourse.bass_utils import run_bass_kernel_spmd

F32 = mybir.dt.float32
BF16 = mybir.dt.bfloat16
AF = mybir.ActivationFunctionType
ALU = mybir.AluOpType
AX = mybir.AxisListType

SEM_ROT = 20000
_DTSZ = {}


def _dsz(dt):
    s = _DTSZ.get(dt)
    if s is None:
        s = 2 if dt == BF16 else 4
        _DTSZ[dt] = s
    return s


def _rect(ap):
    a = ap.ap
    pstep, pn = a[0]
    off = ap.offset
    if pstep == 0:
        p0 = 0
        f0 = off
    else:
        p0 = off // pstep
        f0 = off % pstep
    ext = 0
    for st, cnt in a[1:]:
        ext += (cnt - 1) * abs(st)
    sz = _dsz(ap.dtype)
    return (ap.tensor.name, p0, p0 + pn, f0 * sz, (f0 + ext + 1) * sz)


class Op:
    __slots__ = ("eng", "fn", "waits", "sig", "dkey", "dcount", "pos", "semidx", "count", "idx")

    def __init__(self, eng, fn):
        self.eng = eng
        self.fn = fn
        self.waits = []
        self.sig = False
        self.dkey = None
        self.dcount = 0
        self.pos = 0


class Prog:
    ENGS = ("pe", "act", "dve", "pool", "sync")

    def __init__(self, nc):
        self.nc = nc
        self.ops = []
        self.by_eng = {e: [] for e in self.ENGS}
        self.recs = {}
        self.waited = {}
        self.dma_counts = {}

    @staticmethod
    def _is_ap(v):
        return hasattr(v, "ap") and hasattr(v, "tensor") and hasattr(v, "offset")

    def _track(self, v):
        if not self._is_ap(v):
            return None
        sp = str(v.space)
        if "PSUM" in sp:
            return (v.tensor.name, 0, 128, 0, 1 << 20)
        if "SB" in sp:
            return _rect(v)
        return None

    def add(self, eng, fn, reads, writes, dkey=None, after=None):
        op = Op(eng, fn)
        op.pos = len(self.by_eng[eng])
        idx = len(self.ops)
        op.idx = idx
        deps = set(o.idx for o in (after or []))
        rl = list(dict.fromkeys(r for r in (self._track(v) for v in reads) if r is not None))
        wl = list(dict.fromkeys(r for r in (self._track(v) for v in writes) if r is not None))
        wl = list(dict.fromkeys(wl + [r for r in rl if r[0].startswith("ps")]))
        rl = [r for r in rl if not r[0].startswith("ps")]
        for (nm, p0, p1, f0, f1) in rl:
            for rec in self.recs.setdefault(nm, []):
                if rec[5] and rec[0] < p1 and p0 < rec[1] and rec[2] < f1 and f0 < rec[3]:
                    deps.add(rec[4])
        for (nm, p0, p1, f0, f1) in wl:
            for rec in self.recs.setdefault(nm, []):
                if rec[0] < p1 and p0 < rec[1] and rec[2] < f1 and f0 < rec[3]:
                    deps.add(rec[4])
        for (nm, p0, p1, f0, f1) in wl:
            lst = self.recs[nm]
            lst[:] = [rec for rec in lst if not (p0 <= rec[0] and rec[1] <= p1 and f0 <= rec[2] and rec[3] <= f1)]
            lst.append([p0, p1, f0, f1, idx, True])
        for (nm, p0, p1, f0, f1) in rl:
            lst = self.recs[nm]
            lst[:] = [rec for rec in lst if not ((not rec[5]) and rec[0] == p0 and rec[1] == p1 and rec[2] == f0
                                                  and rec[3] == f1 and rec[4] != idx and self.ops[rec[4]].eng == eng)]
            lst.append([p0, p1, f0, f1, idx, False])
        deps.discard(idx)
        for j in sorted(deps):
            oj = self.ops[j]
            if oj.dkey is not None:
                k = (eng, "d", oj.dkey)
                if self.waited.get(k, 0) >= oj.dcount:
                    continue
                self.waited[k] = oj.dcount
                op.waits.append(("d", oj.dkey, j))
            else:
                if oj.eng == eng and eng == "pe":
                    continue
                k = (eng, "e", oj.eng)
                if self.waited.get(k, -1) >= oj.pos:
                    continue
                self.waited[k] = oj.pos
                oj.sig = True
                op.waits.append(("e", oj.eng, j))
        if dkey is not None:
            op.dkey = dkey
            self.dma_counts[dkey] = self.dma_counts.get(dkey, 0) + 16
            op.dcount = self.dma_counts[dkey]
        self.ops.append(op)
        self.by_eng[eng].append(op)
        return op

    def seal_key(self, key):
        tot = self.dma_counts.get(key, 0)
        for op in self.ops:
            if op.dkey == key:
                op.dcount = tot

    def ins(self, eng, meth, *, reads=None, writes=None, **kw):
        r = list(reads or [])
        w = list(writes or [])
        for k, v in kw.items():
            if self._is_ap(v):
                if k in ("out", "accum_out", "ap"):
                    w.append(v)
                else:
                    r.append(v)
        return self.add(eng, lambda e: getattr(e, meth)(**kw), r, w)

    def mm(self, out, lhsT, rhs, start=True, stop=True, tp=None):
        if tp is None:
            return self.add("pe", lambda e: e.matmul(out, lhsT, rhs, start=start, stop=stop), [lhsT, rhs], [out])
        return self.add("pe", lambda e: e.matmul(out, lhsT, rhs, start=start, stop=stop, tile_position=tp),
                        [lhsT, rhs], [out])

    def dma(self, eng, out, in_, key, after=None, **kw):
        return self.add(eng, lambda e: e.dma_start(out=out, in_=in_, **kw), [in_], [out], dkey=key, after=after)

    def emit(self):
        nc = self.nc
        with ExitStack() as es:
            esems = {}
            for eng in self.ENGS:
                c = 0
                for op in self.by_eng[eng]:
                    if op.sig:
                        c += 1
                        op.semidx = (c - 1) // SEM_ROT
                        op.count = (c - 1) % SEM_ROT + 1
                nsem = (c + SEM_ROT - 1) // SEM_ROT
                esems[eng] = [es.enter_context(nc.semaphore(f"s_{eng}_{i}")) for i in range(max(nsem, 1))]
            dsems = {k: es.enter_context(nc.semaphore(f"d_{k}")) for k in self.dma_counts}
            block = es.enter_context(nc.Block())
            ops = self.ops

            def run(eng_name):
                def body(e):
                    for op in self.by_eng[eng_name]:
                        for (kind, key, j) in op.waits:
                            oj = ops[j]
                            if kind == "d":
                                e.wait_ge(dsems[key], oj.dcount)
                            else:
                                e.wait_ge(esems[key][oj.semidx], oj.count)
                        inst = op.fn(e)
                        if op.dkey is not None:
                            inst.then_inc(dsems[op.dkey], 16)
                        elif op.sig:
                            inst.then_inc(esems[eng_name][op.semidx], 1)
                    if eng_name == "sync":
                        for k, tot in self.dma_counts.items():
                            e.wait_ge(dsems[k], tot)
                return body

            block.tensor(run("pe"))
            block.scalar(run("act"))
            block.vector(run("dve"))
            block.gpsimd(run("pool"))
            block.sync(run("sync"))


D = 2048
DFF = 5504
NKC = 16
NFC = 43
RWC = 3360
O_R, O_K, O_V, O_LW, O_LG = 0, 1024, 2048, 3072, 3200
O_MQ, O_MV, O_MO, O_MI, O_MF, O_G1, O_G2 = 3360, 5408, 6432, 7456, 7460, 7464, 9512
INC = 11560
KAPPA = -0.6065306597126334
RW_EPS = 64e-5
ML_EPS = 1e-5

WSHAPES = [
    ("ffn1_norm", (D,)), ("ffn1_w_gate", (D, DFF)), ("ffn1_w_up", (D, DFF)), ("ffn1_w_down", (DFF, D)),
    ("mix_norm", (D,)), ("w_in", (D, INC)), ("rw_mu", (RWC,)), ("rw_w0", (1024,)), ("rw_w2", (64, 1024)),
    ("rw_a0", (1024,)), ("rw_a2", (64, 1024)), ("rw_g2", (160, 1024)), ("rw_kk", (1024,)), ("rw_ka", (1024,)),
    ("rw_rk", (1024,)), ("rw_ln_w", (1024,)), ("rw_ln_b", (1024,)), ("ml_conv_w", (4, 2048)),
    ("ml_conv_b", (2048,)), ("ml_i_b", (4,)), ("ml_f_b", (4,)), ("ml_norm_w", (1024,)),
    ("w_br_rw", (1024, D)), ("w_br_ml", (1024, D)), ("w_out", (D, D)), ("ffn2_norm", (D,)),
    ("ffn2_w_gate", (D, DFF)), ("ffn2_w_up", (D, DFF)), ("ffn2_w_down", (DFF, D)), ("final_norm", (D,)),
]

C_ID = 0
C_M5 = 128
C_OBD = 448
C_R64 = 576
C_R16 = 1088
C_ML = 1600
C_SEL = 1728
C_ONE = 2240
CST_N = 2752


class _Stop(Exception):
    pass


def make_consts():
    c = np.zeros((128, CST_N), np.float32)
    c[:, C_ID:C_ID + 128] = np.eye(128, dtype=np.float32)
    s = np.arange(64)[:, None]
    t = np.arange(64)[None, :]
    strict = (s < t).astype(np.float32)
    strictT = (t < s).astype(np.float32)
    incl = (s <= t).astype(np.float32)
    for h in range(2):
        r = slice(h * 64, h * 64 + 64)
        for i, m in enumerate((strict, strictT, strict, incl, incl)):
            c[r, C_M5 + i * 64:C_M5 + (i + 1) * 64] = m
        c[r, C_OBD + h * 64:C_OBD + h * 64 + 64] = 1.0
    r64 = np.ones(512, np.float32)
    r64[::64] = 0.0
    r16 = np.ones(512, np.float32)
    r16[::16] = 0.0
    c[:, C_R64:C_R64 + 512] = r64[None]
    c[:, C_R16:C_R16 + 512] = r16[None]
    s2 = np.arange(128)[:, None]
    t2 = np.arange(128)[None, :]
    c[:, C_ML:C_ML + 128] = (s2 <= t2).astype(np.float32)
    for h in range(4):
        c[h, C_SEL + h * 128:C_SEL + (h + 1) * 128] = 1.0
    c[:, C_ONE:C_ONE + 512] = 1.0
    return c


def build(NPT, dbg=False, stop=None):
    SEQ = NPT * 512
    nc = bass.Bass("TRN2", target_bir_lowering=False)

    def din(name, shape):
        return nc.dram_tensor(name, list(shape), F32, kind="ExternalInput").ap()

    def dout(name, shape):
        return nc.dram_tensor(name, list(shape), F32, kind="ExternalOutput").ap()

    xp = din("xp", [SEQ, D])
    xs = din("xs", [64, D])
    meta = din("meta", [16, D])
    st_shift = din("st_shift", [4, RWC])
    st_wkv = din("st_wkv", [4, 16, 64, 64])
    st_conv = din("st_conv", [4, 3, 2048])
    st_C = din("st_C", [4, 4, 256, 256])
    st_n = din("st_n", [4, 4, 256])
    st_m = din("st_m", [4, 4])
    cst_d = din("cst", [128, CST_N])
    W = {nm: din(nm, shp) for nm, shp in WSHAPES}
    yp = dout("yp", [SEQ, D])
    ys = dout("ys", [64, D])
    o_shift = dout("o_shift", [5, RWC])
    o_wkv = dout("o_wkv", [5, 16, 64, 64])
    o_conv = dout("o_conv", [5, 3, 2048])
    o_C = dout("o_C", [5, 4, 256, 256])
    o_n = dout("o_n", [5, 4, 256])
    o_m = dout("o_m", [5, 4])
    dbg_out = {}
    if dbg:
        dbg_out["d_x1"] = dout("d_x1", [512, D])
        dbg_out["d_x2"] = dout("d_x2", [512, D])
        dbg_out["d_yrw"] = dout("d_yrw", [128, 8, 512])
        dbg_out["d_yml"] = dout("d_yml", [128, 8, 512])

    es = ExitStack()
    with es:
        def sb(name, shape, dt):
            return es.enter_context(nc.sbuf_tensor(name, list(shape), dt))

        P = Prog(nc)
        NRING = 6
        xres = sb("xres", [128, 4, D], F32)
        xnT = sb("xnT", [128, NKC, 512], BF16)
        SCRB = 52224
        scr = sb("scr", [128, SCRB // 4], F32)
        wring = sb("wring", [128, NRING, 4096], BF16)
        sgb = sb("sgb", [128, 1, 512], F32)
        yrwT = sb("yrwT", [128, 8, 512], BF16)
        ymlT = sb("ymlT", [128, 8, 512], BF16)
        Pst = sb("Pst", [128, 8, 64], F32)
        Pbf = sb("Pbf", [128, 8, 64], BF16)
        Cst = sb("Cst", [128, 4, 2, 257], F32)
        Cbf = sb("Cbf", [128, 4, 2, 258], BF16)
        gfin = sb("gfin", [128, D], F32)
        identb = sb("identb", [128, 128], BF16)
        identf = sb("identf", [128, 128], F32)
        mask5 = sb("mask5", [128, 5, 64], BF16)
        onesbd = sb("onesbd", [128, 128], F32)
        r64 = sb("r64", [128, 512], BF16)
        r16 = sb("r16", [128, 512], BF16)
        one512 = sb("one512", [128, 512], BF16)
        mlmask = sb("mlmask", [128, 128], BF16)
        sel4 = sb("sel4", [4, 512], F32)
        w2a2 = sb("w2a2", [128, 1024], BF16)
        g2a = sb("g2a", [128, 1024], BF16)
        g2b = sb("g2b", [32, 1024], BF16)
        gcols = sb("gcols", [128, 3, 16], F32)
        mucols = sb("mucols", [128, 27], F32)
        rwc = sb("rwc", [128, 7, 8], F32)
        cwc = sb("cwc", [128, 16, 4], F32)
        cbc = sb("cbc", [128, 16], F32)
        nwc = sb("nwc", [128, 8], F32)
        ifb = sb("ifb", [4, 2], F32)
        carry = sb("carry", [128, 27, 5], F32)
        ccar = sb("ccar", [128, 16, 5, 3], F32)
        mrow = sb("mrow", [4, 8], F32)
        small = sb("small", [128, 64], F32)

        banks = [es.enter_context(nc.psum_tensor(f"ps{i}", [128, 512], F32)) for i in range(8)]
        bstate = {"i": 0, "w": 0}
        cur = {"ti": 0}

        chk_cnt = {}

        def chk(tag, ti=None):
            if stop is None:
                return
            want = stop[0]
            k = 1
            if "#" in want:
                want, k = want.split("#")
                k = int(k)
            if want == tag and (ti is None or stop[1] == ti):
                chk_cnt[tag] = chk_cnt.get(tag, 0) + 1
                if chk_cnt[tag] >= k:
                    raise _Stop()

        def bank():
            b = banks[bstate["i"] % 8]
            bstate["i"] += 1
            return b

        def carve(off, shape, dt):
            n = int(np.prod(shape))
            sz = _dsz(dt)
            assert off % 4 == 0 and off + n * sz <= SCRB, (off, shape)
            nf = (n * sz + 3) // 4
            v = scr[:, off // 4: off // 4 + nf]
            if dt == BF16:
                v = v.bitcast(BF16)[:, 0:n]
            if len(shape) == 1:
                return v
            names = " ".join(f"a{i}" for i in range(len(shape)))
            kw = {f"a{i}": int(s) for i, s in enumerate(shape)}
            return v.rearrange(f"p ({names}) -> p {names}", **kw)

        NWL = 219
        wscr = nc.dram_tensor("wscr", [NWL, 128, 4096], BF16).ap()
        wr_ops = {}
        wl_state = {"n": 0}

        def wload(src, shape):
            s = bstate["w"] % NRING
            bstate["w"] += 1
            i = wl_state["n"]
            wl_state["n"] += 1
            a, b = shape
            assert a * b <= 4096 and i < NWL
            flat = wring[:, s, 0:a * b]
            dst = flat.rearrange("p (a b) -> p a b", a=a)
            if cur["ti"] == 0:
                P.dma("pool", dst, src, f"wp{s}")
                if NPT > 0:
                    wr_ops[i] = P.dma("sync", wscr[i, :, 0:a * b], flat, f"sw{s}")
            else:
                P.dma("sync", flat, wscr[i, :, 0:a * b], f"w{s}", after=[wr_ops[i]])
            return dst

        xsbs = [carve(SCRB - 8192, [D], BF16), carve(SCRB - 4096, [D], BF16)]
        cstg = carve(0, [CST_N], F32)
        stgA = carve(11008, [3392], F32)
        stgB = carve(11008 + 13568, [2048], F32)
        P.dma("sync", cstg, cst_d, "const")
        P.dma("sync", gfin[:], W["final_norm"].partition_broadcast(128), "const")
        stgV = [carve(36864, [128], F32), carve(36864 + 512, [128], F32)]
        P.ins("pool", "memset", ap=stgV[0][:, :], constant=0.0)
        P.ins("pool", "memset", ap=stgV[1][:, :], constant=0.0)

        def vrows(t, r0, ap, n):
            P.dma("sync", stgV[t][r0:r0 + n, :], ap.rearrange("(c p) -> c p", p=128), "const")
        vrows(0, 0, W["ffn1_norm"], 16)
        vrows(0, 16, W["mix_norm"], 16)
        vrows(0, 32, W["ffn2_norm"], 16)
        vrows(0, 48, W["rw_mu"][0:3328], 26)
        P.dma("sync", stgV[0][74:75, 0:32], W["rw_mu"][3328:3360].rearrange("(c p) -> c p", p=32), "const")
        for i, nm in enumerate(("rw_w0", "rw_a0", "rw_kk", "rw_ka", "rw_rk", "rw_ln_w")):
            vrows(0, 75 + 8 * i, W[nm], 8)
        vrows(1, 0, W["rw_ln_b"], 8)
        for j in range(4):
            vrows(1, 8 + 16 * j, W["ml_conv_w"][j], 16)
        vrows(1, 72, W["ml_conv_b"], 16)
        vrows(1, 88, W["ml_norm_w"], 8)
        P.dma("sync", ifb[:, 0:1], W["ml_i_b"].rearrange("(p c) -> p c", c=1), "const",
              allow_slow_non_contiguous=True)
        P.dma("sync", ifb[:, 1:2], W["ml_f_b"].rearrange("(p c) -> p c", c=1), "const",
              allow_slow_non_contiguous=True)
        P.dma("pool", w2a2[0:64, :], W["rw_w2"], "constp")
        P.dma("pool", w2a2[64:128, :], W["rw_a2"], "constp")
        P.dma("pool", g2a[:, :], W["rw_g2"][0:128, :], "constp")
        P.dma("pool", g2b[:, :], W["rw_g2"][128:160, :], "constp")
        P.dma("sync", stgA[0:4, 0:RWC], st_shift, "const")
        P.dma("sync", stgB[0:12, 0:2048], st_conv.rearrange("s j c -> (s j) c"), "const")
        P.dma("sync", mrow[0:4, 0:4], st_m.rearrange("s h -> h s"), "const", allow_slow_non_contiguous=True)
        P.seal_key("const")
        P.seal_key("constp")

        P.ins("dve", "tensor_copy", out=identb[:], in_=cstg[:, C_ID:C_ID + 128])
        P.ins("dve", "tensor_copy", out=identf[:], in_=cstg[:, C_ID:C_ID + 128])
        bA = bank()
        bB = bank()
        P.mm(bA[:, 0:123], stgV[0][0:123, :], identf[0:123, 0:123])
        P.mm(bB[:, 0:96], stgV[1][0:96, :], identf[0:96, 0:96])
        P.ins("dve", "tensor_copy", out=gcols[:].rearrange("p a b -> p (a b)"), in_=bA[:, 0:48])
        P.ins("dve", "tensor_copy", out=mucols[:, :], in_=bA[:, 48:75])
        P.ins("dve", "tensor_copy", out=rwc[:, 0:6, :].rearrange("p a b -> p (a b)"), in_=bA[:, 75:123])
        P.ins("dve", "tensor_copy", out=rwc[:, 6, :], in_=bB[:, 0:8])
        P.ins("dve", "tensor_copy", out=cwc[:].rearrange("p c j -> p j c"),
              in_=bB[:, 8:72].rearrange("p (j c) -> p j c", j=4))
        P.ins("dve", "tensor_copy", out=cbc[:, :], in_=bB[:, 72:88])
        P.ins("dve", "tensor_copy", out=nwc[:, :], in_=bB[:, 88:96])
        P.ins("dve", "tensor_copy", out=mask5[:].rearrange("p a b -> p (a b)"), in_=cstg[:, C_M5:C_M5 + 320])
        P.ins("dve", "tensor_copy", out=onesbd[:], in_=cstg[:, C_OBD:C_OBD + 128])
        P.ins("dve", "tensor_copy", out=r64[:], in_=cstg[:, C_R64:C_R64 + 512])
        P.ins("dve", "tensor_copy", out=r16[:], in_=cstg[:, C_R16:C_R16 + 512])
        P.ins("dve", "tensor_copy", out=one512[:], in_=cstg[:, C_ONE:C_ONE + 512])
        P.ins("dve", "tensor_copy", out=mlmask[:], in_=cstg[:, C_ML:C_ML + 128])
        P.ins("dve", "tensor_copy", out=sel4[:], in_=cstg[0:4, C_SEL:C_SEL + 512])
        P.ins("pool", "memset", ap=carry[:], constant=0.0)
        P.ins("pool", "memset", ap=ccar[:], constant=0.0)
        P.ins("pool", "memset", ap=mrow[0:4, 4:8], constant=0.0)
        P.ins("pool", "memset", ap=small[:], constant=0.0)
        b = bank()
        for g in range(27):
            n = 128 if g < 26 else 32
            P.mm(b[:n, g * 4:g * 4 + 4], stgA[0:4, g * 128:g * 128 + n], identf[0:4, 0:4])
        P.ins("dve", "tensor_copy", out=carry[:, 0:26, 0:4], in_=b[:, 0:104].rearrange("p (g s) -> p g s", s=4))
        P.ins("dve", "tensor_copy", out=carry[0:32, 26, 0:4], in_=b[0:32, 104:108])
        b = bank()
        for g in range(16):
            P.mm(b[:, g * 12:g * 12 + 12], stgB[0:12, g * 128:g * 128 + 128], identf[0:12, 0:12])
        P.ins("dve", "tensor_copy", out=ccar[:, :, 0:4, :],
              in_=b[:, 0:192].rearrange("p (g s j) -> p g s j", s=4, j=3))

        def rmsnorm_T(gi, subt):
            for st, n in subt:
                xsb = xsbs[st % 2]
                ssq = small[:n, 8 + 2 * st:9 + 2 * st]
                rstd = small[:n, 9 + 2 * st:10 + 2 * st]
                P.ins("act", "activation", out=xsb[:n, :], in_=xres[:n, st, :], func=AF.Square, accum_out=ssq)
                P.ins("dve", "tensor_scalar", out=rstd, in0=ssq, scalar1=1.0 / D, scalar2=1e-6,
                      op0=ALU.mult, op1=ALU.add)
                P.ins("act", "activation", out=rstd, in_=rstd, func=AF.Sqrt)
                P.ins("dve", "reciprocal", out=rstd, in_=rstd)
                P.ins("dve", "tensor_scalar", out=xsb[:n, :], in0=xres[:n, st, :], scalar1=rstd, scalar2=None,
                      op0=ALU.mult)
                for c0 in range(0, 16, 4):
                    bb = bank()
                    for c in range(4):
                        P.mm(bb[:, c * 128:c * 128 + n], xsb[:n, (c0 + c) * 128:(c0 + c + 1) * 128], identb[:n, :n])
                    P.ins("dve", "tensor_tensor",
                          out=xnT[:, c0:c0 + 4, st * 128:st * 128 + n],
                          in0=bb[:, :].rearrange("p (c t) -> p c t", t=128)[:, :, 0:n],
                          in1=gcols[:, gi, c0:c0 + 4].unsqueeze(2).to_broadcast([128, 4, n]), op=ALU.mult)

        def ffn(pref, gi, NT, subt):
            rmsnorm_T(gi, subt)
            chk(pref + "_norm", cur["ti"])
            hT = carve(0, [NFC, 512], BF16)
            Wg = W[pref + "_w_gate"].rearrange("(kc p) f -> p kc f", p=128)
            Wu = W[pref + "_w_up"].rearrange("(kc p) f -> p kc f", p=128)
            Wd = W[pref + "_w_down"].rearrange("(kc p) d -> p kc d", p=128)
            for f0 in range(0, DFF, 256):
                fw = min(256, DFF - f0)
                wg = wload(Wg[:, :, f0:f0 + fw], [16, fw])
                wu = wload(Wu[:, :, f0:f0 + fw], [16, fw])
                for j in range(fw // 128):
                    fc = f0 // 128 + j
                    bg = bank()
                    bu = bank()
                    for kc in range(16):
                        P.mm(bg[:, :NT], wg[:, kc, j * 128:(j + 1) * 128], xnT[:, kc, :NT], start=kc == 0, stop=kc == 15)
                    for kc in range(16):
                        P.mm(bu[:, :NT], wu[:, kc, j * 128:(j + 1) * 128], xnT[:, kc, :NT], start=kc == 0, stop=kc == 15)
                    sg = sgb[:, 0, :NT]
                    P.ins("act", "activation", out=sg, in_=bg[:, :NT], func=AF.Silu)
                    P.ins("dve", "tensor_tensor", out=hT[:, fc, :NT], in0=sg, in1=bu[:, :NT], op=ALU.mult)
            chk(pref + "_gu", cur["ti"])
            for cb in range(4):
                bks = [bank() for _ in subt]
                for k0 in range(0, NFC, 8):
                    nk = min(8, NFC - k0)
                    wd = wload(Wd[:, k0:k0 + nk, cb * 512:(cb + 1) * 512], [nk, 512])
                    for si, (st, n) in enumerate(subt):
                        for k in range(nk):
                            P.mm(bks[si][:n, :], hT[:, k0 + k, st * 128:st * 128 + n], wd[:, k, :],
                                 start=(k0 + k == 0), stop=(k0 + k == NFC - 1))
                chk(pref + "_dmm%d" % cb, cur["ti"])
                for si, (st, n) in enumerate(subt):
                    xs_ = xres[:n, st, cb * 512:(cb + 1) * 512]
                    P.ins("dve", "scalar_tensor_tensor", out=xs_, in0=bks[si][:n, :], scalar=0.5, in1=xs_,
                          op0=ALU.mult, op1=ALU.add)
                chk(pref + "_dev%d" % cb, cur["ti"])

        Win = W["w_in"].rearrange("(kc p) c -> p kc c", p=128)

        def proj_fm(col0, ncols, NT, outs):
            wt = wload(Win[:, :, col0:col0 + ncols], [16, ncols])
            for j in range((ncols + 127) // 128):
                m = min(128, ncols - j * 128)
                bb = bank()
                for kc in range(16):
                    P.mm(bb[:m, :NT], wt[:, kc, j * 128:j * 128 + m], xnT[:, kc, :NT], start=kc == 0, stop=kc == 15)
                outs(j, bb, m)

        def rwkv(NT, segs, C, last_tile):
            nseg = len(segs)
            chunks = []
            for (c0, ln, sq) in segs:
                for cc in range(0, ln, C):
                    chunks.append((c0 + cc, sq, cc == 0, cc + C >= ln))
            nch = len(chunks)
            rmask = r64 if C == 64 else r16
            NJ = 6 if C == 64 else 4
            o = 0

            def cv(shape, dt):
                nonlocal o
                v = carve(o, shape, dt)
                o += ((int(np.prod(shape)) * _dsz(dt) + 3) // 4) * 4
                return v
            T = [cv([514], F32) for _ in range(8)]
            YQ = carve(2 * 2056, [2, 8, 64], F32)
            YSQ = carve(4 * 2056, [2, 8, 64], F32)
            YN = carve(6 * 2056, [2, 8, 64], BF16)
            OPS = cv([2, 7, 512], BF16)
            GF = cv([2, 512], F32)
            BON = cv([2, 512], F32)
            WC = cv([2, 8], F32)
            LST = cv([6, 16], F32)
            SC = [cv([2, 5, 64], BF16) for _ in range(2)]
            TM3 = [cv([2, 3, 64], BF16) for _ in range(2)]
            RF = [cv([2, 64], BF16) for _ in range(2)]
            ao = [4 * 2056]

            def av_(shape, dt):
                v = carve(ao[0], shape, dt)
                ao[0] += ((int(np.prod(shape)) * _dsz(dt) + 3) // 4) * 4
                assert ao[0] <= 8 * 2056
                return v
            SC += [av_([2, 5, 64], BF16) for _ in range(2)]
            TM3 += [av_([2, 3, 64], BF16) for _ in range(2)]
            RF += [av_([2, 64], BF16) for _ in range(2)]
            QP = [[av_([2, 2, 64], BF16) for _ in range(2)] for _ in range(2)]
            RR = [[av_([2, 64], BF16) for _ in range(2)] for _ in range(2)]
            XU = cv([2, 2, 64], BF16)
            TL = cv([512], BF16)
            SG1 = cv([512], BF16)
            SG2 = cv([512], BF16)
            LTMP = T[7]

            def shift_u(bb, m, g, dst, NTl=NT):
                praw = T[0]
                dd = T[4]
                P.ins("act", "activation", out=praw[:m, 1:1 + NT], in_=bb[:m, :NT], func=AF.Copy)
                P.ins("dve", "tensor_tensor", out=dd[:m, 0:NT], in0=praw[:m, 0:NT], in1=praw[:m, 1:1 + NT],
                      op=ALU.subtract)
                for (c0, ln, sq) in segs:
                    P.ins("dve", "tensor_tensor", out=dd[:m, c0:c0 + 1], in0=carry[:m, g, sq:sq + 1],
                          in1=praw[:m, 1 + c0:2 + c0], op=ALU.subtract)
                P.ins("dve", "scalar_tensor_tensor", out=dst[:m, 0:NT], in0=dd[:m, 0:NT], scalar=mucols[:m, g:g + 1],
                      in1=praw[:m, 1:1 + NT], op0=ALU.mult, op1=ALU.add)
                for (c0, ln, sq) in segs:
                    P.ins("act", "activation", out=carry[:m, g, sq:sq + 1], in_=praw[:m, c0 + ln:c0 + ln + 1],
                          func=AF.Copy)

            def lora_out(j, bb, m):
                if m == 32:
                    j = 2
                if j == 0:
                    shift_u(bb, 128, 24, LTMP)
                    P.ins("act", "activation", out=TL[0:64, :NT], in_=LTMP[0:64, :NT], func=AF.Tanh)
                    P.ins("act", "activation", out=TL[64:128, :NT], in_=LTMP[64:128, :NT], func=AF.Copy)
                elif j == 1:
                    shift_u(bb, 128, 25, LTMP)
                    P.ins("act", "activation", out=SG1[:, :NT], in_=LTMP[:, :NT], func=AF.Sigmoid)
                else:
                    shift_u(bb, 32, 26, LTMP)
                    P.ins("act", "activation", out=SG2[0:32, :NT], in_=LTMP[0:32, :NT], func=AF.Sigmoid)
            proj_fm(O_LW, 256, NT, lora_out)
            proj_fm(O_LW + 256, 32, NT, lora_out)
            chk("rw_lora", cur["ti"])

            hk = [slice(0, 64), slice(64, 128)]
            hs = [slice(0, C), slice(64, 64 + C)]

            for q in range(4):
                for pi in range(2):
                    pp = 2 * q + pi
                    At, Rt, Bt, Kt, Bh, Kh, Vb = [OPS[:, pi, i, :] for i in range(7)]
                    ur, uk, uv = T[1], T[2], T[3]
                    wt1 = wload(Win[:, :, O_R + pp * 128:O_R + pp * 128 + 128], [16, 128])
                    wt2 = wload(Win[:, :, O_K + pp * 128:O_K + pp * 128 + 128], [16, 128])
                    wt3 = wload(Win[:, :, O_V + pp * 128:O_V + pp * 128 + 128], [16, 128])
                    for wt, g, dst in ((wt1, pp, ur), (wt2, 8 + pp, uk), (wt3, 16 + pp, uv)):
                        bb = bank()
                        for kc in range(16):
                            P.mm(bb[:, :NT], wt[:, kc, :], xnT[:, kc, :NT], start=kc == 0, stop=kc == 15)
                        shift_u(bb, 128, g, dst)
                    cols = slice(pp * 128, pp * 128 + 128)
                    ba = bank()
                    P.mm(ba[:, :NT], w2a2[64:128, cols], TL[64:128, :NT])
                    av = T[5]
                    P.ins("act", "activation", out=av[:, :NT], in_=ba[:, :NT], func=AF.Sigmoid, bias=rwc[:, 1, pp:pp + 1])
                    bw = bank()
                    P.mm(bw[:, :NT], w2a2[0:64, cols], TL[0:64, :NT])
                    sgm = T[6]
                    P.ins("act", "activation", out=sgm[:, :NT], in_=bw[:, :NT], func=AF.Sigmoid, bias=rwc[:, 0, pp:pp + 1])
                    bg_ = bank()
                    P.mm(bg_[:, :NT], g2a[:, cols], SG1[:, :NT], start=True, stop=False)
                    P.mm(bg_[:, :NT], g2b[0:32, cols], SG2[0:32, :NT], start=False, stop=True)
                    P.ins("act", "activation", out=GF[:, pi, :NT], in_=bg_[:, :NT], func=AF.Copy)
                    kkr = T[0]
                    P.ins("dve", "tensor_scalar", out=kkr[:, :NT], in0=uk[:, :NT], scalar1=rwc[:, 2, pp:pp + 1],
                          scalar2=None, op0=ALU.mult)
                    sq_ = T[4]
                    P.ins("act", "activation", out=sq_[:, :NT], in_=kkr[:, :NT], func=AF.Square)
                    bs = bank()
                    P.mm(bs[:, :NT], onesbd[:, :], sq_[:, :NT])
                    P.ins("dve", "tensor_scalar", out=sq_[:, :NT], in0=bs[:, :NT], scalar1=1e-24, scalar2=None,
                          op0=ALU.max)
                    P.ins("act", "activation", out=sq_[:, :NT], in_=sq_[:, :NT], func=AF.Sqrt)
                    P.ins("dve", "reciprocal", out=sq_[:, :NT], in_=sq_[:, :NT])
                    P.ins("dve", "tensor_tensor", out=kkr[:, :NT], in0=kkr[:, :NT], in1=sq_[:, :NT], op=ALU.mult)
                    P.ins("dve", "tensor_scalar", out=sq_[:, :NT], in0=av[:, :NT], scalar1=-1.0,
                          scalar2=rwc[:, 3, pp:pp + 1], op0=ALU.add, op1=ALU.mult)
                    P.ins("dve", "scalar_tensor_tensor", out=uk[:, :NT], in0=sq_[:, :NT], scalar=1.0, in1=uk[:, :NT],
                          op0=ALU.add, op1=ALU.mult)
                    P.ins("dve", "scalar_tensor_tensor", out=sq_[:, :NT], in0=ur[:, :NT], scalar=rwc[:, 4, pp:pp + 1],
                          in1=uk[:, :NT], op0=ALU.mult, op1=ALU.mult)
                    bs2 = bank()
                    P.mm(bs2[:, :NT], onesbd[:, :], sq_[:, :NT])
                    P.ins("dve", "tensor_tensor", out=BON[:, pi, :NT], in0=bs2[:, :NT], in1=uv[:, :NT], op=ALU.mult)
                    P.ins("dve", "tensor_scalar", out=BON[:, pi, :NT], in0=BON[:, pi, :NT],
                          scalar1=rwc[:, 6, pp:pp + 1], scalar2=None, op0=ALU.add)
                    P.ins("dve", "tensor_tensor", out=av[:, :NT], in0=kkr[:, :NT], in1=av[:, :NT], op=ALU.mult)
                    cs = T[7]
                    P.ins("dve", "tensor_tensor_scan", out=cs[:, :NT], data0=rmask[:, :NT], data1=sgm[:, :NT],
                          initial=0.0, op0=ALU.mult, op1=ALU.add)
                    P.ins("dve", "tensor_tensor", out=sgm[:, :NT], in0=cs[:, :NT], in1=sgm[:, :NT], op=ALU.subtract)
                    P.ins("act", "activation", out=sgm[:, :NT], in_=sgm[:, :NT], func=AF.Exp, scale=KAPPA)
                    P.ins("dve", "scalar_tensor_tensor", out=At[:, :NT], in0=kkr[:, :NT], scalar=-1.0, in1=sgm[:, :NT],
                          op0=ALU.mult, op1=ALU.mult)
                    P.ins("act", "activation", out=sgm[:, :NT], in_=cs[:, :NT], func=AF.Exp, scale=KAPPA)
                    P.ins("dve", "tensor_tensor", out=Rt[:, :NT], in0=ur[:, :NT], in1=sgm[:, :NT], op=ALU.mult)
                    P.ins("act", "activation", out=sgm[:, :NT], in_=cs[:, :NT], func=AF.Exp, scale=-KAPPA)
                    P.ins("dve", "tensor_tensor", out=Bt[:, :NT], in0=av[:, :NT], in1=sgm[:, :NT], op=ALU.mult)
                    P.ins("dve", "tensor_tensor", out=Kt[:, :NT], in0=uk[:, :NT], in1=sgm[:, :NT], op=ALU.mult)
                    csv = cs[:, 0:nch * C].rearrange("p (c t) -> p c t", t=C)
                    P.ins("act", "activation", out=WC[:, pi, 0:nch], in_=csv[:, :, C - 1], func=AF.Exp, scale=KAPPA)
                    P.ins("dve", "tensor_tensor", out=sgm[:, 0:nch * C].rearrange("p (c t) -> p c t", t=C),
                          in0=csv[:, :, C - 1:C].to_broadcast([128, nch, C]), in1=csv, op=ALU.subtract)
                    P.ins("act", "activation", out=sgm[:, :NT], in_=sgm[:, :NT], func=AF.Exp, scale=KAPPA)
                    P.ins("dve", "tensor_tensor", out=Bh[:, :NT], in0=av[:, :NT], in1=sgm[:, :NT], op=ALU.mult)
                    P.ins("dve", "tensor_tensor", out=Kh[:, :NT], in0=uk[:, :NT], in1=sgm[:, :NT], op=ALU.mult)
                    P.ins("act", "activation", out=Vb[:, :NT], in_=uv[:, :NT], func=AF.Copy)
                    chk("rw_pre", cur["ti"])
                    if pi == 1:
                        chk("rw_pre2", cur["ti"])

                def load_state(ci):
                    c0, sq, first, last = chunks[ci]
                    if not (first and NT == 80):
                        return
                    if sq < 4:
                        for pi in range(2):
                            pp = 2 * q + pi
                            stw = T[pi][:, 0:64]
                            P.dma("sync", stw, st_wkv[sq, 2 * pp:2 * pp + 2].rearrange("h v k -> (h v) k"),
                                  f"stw{pi}")
                            bb = bank()
                            for h in range(2):
                                P.mm(bb[hk[h], 0:64], T[pi][hk[h], 0:64], identf[hk[h], hk[h]],
                                     tp=(h * 64, h * 64))
                            P.ins("dve", "tensor_copy", out=Pst[:, pp, :], in_=bb[:, 0:64])
                            P.ins("act", "activation", out=Pbf[:, pp, :], in_=Pst[:, pp, :], func=AF.Copy)
                    else:
                        for pi in range(2):
                            pp = 2 * q + pi
                            P.ins("dve", "memset", ap=Pst[:, pp, :], constant=0.0)
                            P.ins("dve", "memset", ap=Pbf[:, pp, :], constant=0.0)

                def part_A(ci):
                    c0, sq, first, last = chunks[ci]
                    rg = ci % 4
                    cc = slice(c0, c0 + C)
                    for pi in range(2):
                        At, Rt, Bt, Kt, Bh, Kh, Vb = [OPS[:, pi, i, :] for i in range(7)]
                        bsc = bank()
                        for h in range(2):
                            tp = (h * 64, h * 64)
                            for i, (l_, r_) in enumerate(((Bt, At), (At, Bt), (Kt, At), (Bt, Rt), (Kt, Rt))):
                                P.mm(bsc[hs[h], i * 64:i * 64 + C], l_[hk[h], cc], r_[hk[h], cc], tp=tp)
                        for h in (range(1) if C == 64 else range(2)):
                            rws = slice(0, 128) if C == 64 else hs[h]
                            P.ins("dve", "tensor_tensor", out=SC[rg][rws, pi, :, 0:C],
                                  in0=bsc[rws, 0:320].rearrange("p (i t) -> p i t", t=64)[:, :, 0:C],
                                  in1=mask5[rws, :, 0:C], op=ALU.mult)
                        btm = bank()
                        for h in range(2):
                            tp = (h * 64, h * 64)
                            for i, src in enumerate((Vb, Bh, Kh)):
                                P.mm(btm[hs[h], i * 64:i * 64 + 64], src[hk[h], cc], identb[hk[h], hk[h]], tp=tp)
                        for h in (range(1) if C == 64 else range(2)):
                            rws = slice(0, 128) if C == 64 else hs[h]
                            P.ins("act", "activation", out=TM3[rg][rws, pi, :, :],
                                  in_=btm[rws, 0:192].rearrange("p (i t) -> p i t", t=64), func=AF.Copy)

                def part_B(ci, j):
                    rg4 = ci % 4
                    rg = ci % 2
                    for pi in range(2):
                        if j == 0:
                            Qc = SC[rg4][:, pi, 0, :]
                            Pc = SC[rg4][:, pi, 1, :]
                        else:
                            Qc = QP[rg][(j - 1) % 2][:, pi, 0, :]
                            Pc = QP[rg][(j - 1) % 2][:, pi, 1, :]
                        dstR = RF[rg4][:, pi, :] if j == NJ - 1 else RR[rg][j % 2][:, pi, :]
                        if j == 0:
                            for hh in range(2):
                                P.ins("dve", "tensor_tensor", out=dstR[hs[hh], 0:C], in0=Qc[hs[hh], 0:C],
                                      in1=identb[hs[hh], hh * 64:hh * 64 + C], op=ALU.add)
                        else:
                            Rp = RR[rg][(j - 1) % 2][:, pi, :]
                            bq = bank()
                            for h in range(2):
                                P.mm(bq[hs[h], 0:C], Pc[hs[h], 0:C], Rp[hs[h], 0:C], tp=(h * 64, h * 64))
                            for h in (range(1) if C == 64 else range(2)):
                                rws = slice(0, 128) if C == 64 else hs[h]
                                P.ins("dve", "tensor_tensor", out=dstR[rws, 0:C], in0=bq[rws, 0:C], in1=Rp[rws, 0:C],
                                      op=ALU.add)
                        if j < NJ - 1:
                            lastsq = j == NJ - 2
                            bq2 = bank()
                            for h in range(2):
                                tp = (h * 64, h * 64)
                                if not lastsq:
                                    P.mm(bq2[hs[h], 0:C], Pc[hs[h], 0:C], Qc[hs[h], 0:C], tp=tp)
                                P.mm(bq2[hs[h], 64:64 + C], Qc[hs[h], 0:C], Pc[hs[h], 0:C], tp=tp)
                            for h in (range(1) if C == 64 else range(2)):
                                rws = slice(0, 128) if C == 64 else hs[h]
                                if lastsq:
                                    P.ins("act", "activation", out=QP[rg][j % 2][rws, pi, 1, 0:C],
                                          in_=bq2[rws, 64:64 + C], func=AF.Copy)
                                else:
                                    P.ins("act", "activation", out=QP[rg][j % 2][rws, pi, :, 0:C],
                                          in_=bq2[rws, 0:128].rearrange("p (i t) -> p i t", t=64)[:, :, 0:C],
                                          func=AF.Copy)

                def part_C1(ci):
                    c0, sq, first, last = chunks[ci]
                    rg = ci % 4
                    cc = slice(c0, c0 + C)
                    load_state(ci)
                    for pi in range(2):
                        pp = 2 * q + pi
                        At = OPS[:, pi, 0, :]
                        X0 = XU[:, pi, 0, :]
                        bx = bank()
                        for h in range(2):
                            tp = (h * 64, h * 64)
                            P.mm(bx[hs[h], 0:64], At[hk[h], cc], Pbf[hk[h], pp, :], start=True, stop=False, tp=tp)
                            P.mm(bx[hs[h], 0:64], SC[rg][hs[h], pi, 2, 0:C], TM3[rg][hs[h], pi, 0, :],
                                 start=False, stop=True, tp=tp)
                        for h in (range(1) if C == 64 else range(2)):
                            rws = slice(0, 128) if C == 64 else hs[h]
                            P.ins("act", "activation", out=X0[rws, :], in_=bx[rws, 0:64], func=AF.Copy)

                def part_C2(ci):
                    rg = ci % 4
                    for pi in range(2):
                        X0 = XU[:, pi, 0, :]
                        U = XU[:, pi, 1, :]
                        bu_ = bank()
                        for h in range(2):
                            tp = (h * 64, h * 64)
                            P.mm(bu_[hs[h], 0:64], RF[rg][hs[h], pi, 0:C], X0[hs[h], :], tp=tp)
                        for h in (range(1) if C == 64 else range(2)):
                            rws = slice(0, 128) if C == 64 else hs[h]
                            P.ins("dve", "tensor_copy", out=U[rws, :], in_=bu_[rws, 0:64])

                def part_C3(ci):
                    c0, sq, first, last = chunks[ci]
                    rg = ci % 4
                    cc = slice(c0, c0 + C)
                    for pi in range(2):
                        pp = 2 * q + pi
                        Rt = OPS[:, pi, 1, :]
                        U = XU[:, pi, 1, :]
                        by = bank()
                        for h in range(2):
                            tp = (h * 64, h * 64)
                            P.mm(by[hs[h], 0:64], Rt[hk[h], cc], Pbf[hk[h], pp, :], start=True, stop=False, tp=tp)
                            P.mm(by[hs[h], 0:64], SC[rg][hs[h], pi, 3, 0:C], U[hs[h], :], start=False, stop=False, tp=tp)
                            P.mm(by[hs[h], 0:64], SC[rg][hs[h], pi, 4, 0:C], TM3[rg][hs[h], pi, 0, :],
                                 start=False, stop=True, tp=tp)
                        for h in (range(1) if C == 64 else range(2)):
                            rws = slice(0, 128) if C == 64 else hs[h]
                            P.ins("act", "activation", out=YQ[rws, pi, ci, :], in_=by[rws, 0:64], func=AF.Copy)
                        bp = bank()
                        for h in range(2):
                            tp = (h * 64, h * 64)
                            P.mm(bp[hk[h], 0:64], TM3[rg][hs[h], pi, 1, :], U[hs[h], :], start=True, stop=False, tp=tp)
                            P.mm(bp[hk[h], 0:64], TM3[rg][hs[h], pi, 2, :], TM3[rg][hs[h], pi, 0, :],
                                 start=False, stop=True, tp=tp)
                        P.ins("dve", "scalar_tensor_tensor", out=Pst[:, pp, :], in0=Pst[:, pp, :],
                              scalar=WC[:, pi, ci:ci + 1], in1=bp[:, 0:64], op0=ALU.mult, op1=ALU.add)
                        P.ins("act", "activation", out=Pbf[:, pp, :], in_=Pst[:, pp, :], func=AF.Copy)
                        if last and (sq < 4 or last_tile):
                            bb = bank()
                            for h in range(2):
                                P.mm(bb[hk[h], 0:64], Pst[hk[h], pp, :], identf[hk[h], hk[h]], tp=(h * 64, h * 64))
                            so = T[0][:, 64 * pi:64 * pi + 64]
                            P.ins("dve", "tensor_copy", out=so, in_=bb[:, 0:64])
                            P.dma("sync", o_wkv[sq, 2 * pp:2 * pp + 2].rearrange("h v k -> (h v) k"), so, f"owkv{pi}")

                def c_steps(grp):
                    st_ = []
                    for ci in grp:
                        st_ += [lambda ci=ci: part_C1(ci), lambda ci=ci: part_C2(ci), lambda ci=ci: part_C3(ci)]
                    return st_

                pend = []
                for g0 in range(0, nch, 2):
                    grp = list(range(g0, min(nch, g0 + 2)))
                    for ci in grp:
                        part_A(ci)
                    for j in range(NJ):
                        for ci in grp:
                            part_B(ci, j)
                        if pend:
                            pend.pop(0)()
                    while pend:
                        pend.pop(0)()
                    pend = c_steps(grp)
                while pend:
                    pend.pop(0)()
                chk("rw_dep", cur["ti"])

                ng = 2 * nch
                yq = YQ[:, :, 0:nch, :]
                s1 = LST[:, 0, 0:ng].rearrange("p (a b) -> p a b", a=2)
                s2 = LST[:, 1, 0:ng].rearrange("p (a b) -> p a b", a=2)
                mean = LST[:, 2, 0:ng].rearrange("p (a b) -> p a b", a=2)
                var = LST[:, 3, 0:ng].rearrange("p (a b) -> p a b", a=2)
                P.ins("dve", "tensor_reduce", out=s1, in_=yq, axis=AX.X, op=ALU.add)
                P.ins("act", "activation", out=YSQ[:, :, 0:nch, :], in_=yq, func=AF.Square)
                P.ins("dve", "tensor_reduce", out=s2, in_=YSQ[:, :, 0:nch, :], axis=AX.X, op=ALU.add)
                P.ins("dve", "tensor_scalar", out=mean, in0=s1, scalar1=1.0 / 64, scalar2=None, op0=ALU.mult)
                P.ins("dve", "tensor_tensor", out=var, in0=mean, in1=mean, op=ALU.mult)
                P.ins("dve", "scalar_tensor_tensor", out=var, in0=s2, scalar=1.0 / 64, in1=var, op0=ALU.mult,
                      op1=ALU.subtract)
                P.ins("dve", "tensor_scalar", out=var, in0=var, scalar1=RW_EPS, scalar2=None, op0=ALU.add)
                P.ins("act", "activation", out=var, in_=var, func=AF.Sqrt)
                P.ins("dve", "reciprocal", out=var, in_=var)
                P.ins("dve", "tensor_tensor", out=YSQ[:, :, 0:nch, :], in0=yq,
                      in1=mean.unsqueeze(3).to_broadcast([128, 2, nch, 64]), op=ALU.subtract)
                P.ins("dve", "tensor_tensor", out=YN[:, :, 0:nch, :], in0=YSQ[:, :, 0:nch, :],
                      in1=var.unsqueeze(3).to_broadcast([128, 2, nch, 64]), op=ALU.mult)
                for pi in range(2):
                    pp = 2 * q + pi
                    bt_ = bank()
                    for ci, (c0, sq, first, last) in enumerate(chunks):
                        for h in range(2):
                            P.mm(bt_[hk[h], c0:c0 + C], YN[hs[h], pi, ci, :], identb[hs[h], h * 64:h * 64 + C],
                                 tp=(h * 64, h * 64))
                    zt = T[0]
                    P.ins("dve", "scalar_tensor_tensor", out=zt[:, :NT], in0=bt_[:, :NT], scalar=rwc[:, 5, pp:pp + 1],
                          in1=BON[:, pi, :NT], op0=ALU.mult, op1=ALU.add)
                    P.ins("dve", "tensor_tensor", out=yrwT[:, pp, :NT], in0=zt[:, :NT], in1=GF[:, pi, :NT], op=ALU.mult)

        def mlstm(NT, segs, L, last_tile):
            nseg = len(segs)
            chunks = []
            for si, (c0, ln, sq) in enumerate(segs):
                for cc in range(0, ln, L):
                    chunks.append((c0 + cc, sq, cc == 0, cc + L >= ln, len(chunks), si))
            nch = len(chunks)
            o = 0

            def cv(shape, dt):
                nonlocal o
                v = carve(o, shape, dt)
                o += ((int(np.prod(shape)) * _dsz(dt) + 3) // 4) * 4
                return v
            cbuf = cv([544], F32)
            acc = cv([512], F32)
            stmp = cv([512], F32)
            irow = cv([520], F32)
            frow = cv([520], F32)
            Frow = cv([520], F32)
            Mrow = cv([520], F32)
            gcol = cv([5, 12], F32)
            qT = cv([4, 512], BF16)
            kT = cv([4, 512], BF16)
            vext = cv([5, 2, 258], BF16)
            sigo = cv([4, 512], F32)
            NUMr = [cv([2, 258], F32) for _ in range(2)]
            tqc = [cv([258], F32) for _ in range(2)]
            Eb = [cv([128], F32) for _ in range(2)]
            STb = [cv([128], BF16) for _ in range(2)]
            kTM = [cv([256], BF16) for _ in range(2)]
            hn = cv([2, 256], F32)
            hsq = cv([2, 256], F32)
            ynb = cv([2, 256], BF16)
            mpe_r = [cv([8], F32) for _ in range(2)]
            lst = cv([8, 2], F32)
            wcol_r = [cv([16], F32) for _ in range(2)]

            wt = wload(Win[:, :, O_MI:O_MI + 8], [16, 8])
            bi = bank()
            bf_ = bank()
            for kc in range(16):
                P.mm(bi[0:4, :NT], wt[:, kc, 0:4], xnT[:, kc, :NT], start=kc == 0, stop=kc == 15)
            for kc in range(16):
                P.mm(bf_[0:4, :NT], wt[:, kc, 4:8], xnT[:, kc, :NT], start=kc == 0, stop=kc == 15)
            P.ins("act", "activation", out=irow[0:4, :NT], in_=bi[0:4, :NT], func=AF.Identity, bias=ifb[0:4, 0:1])
            P.ins("act", "activation", out=frow[0:4, :NT], in_=bf_[0:4, :NT], func=AF.Sigmoid, bias=ifb[0:4, 1:2])
            P.ins("act", "activation", out=frow[0:4, :NT], in_=frow[0:4, :NT], func=AF.Ln)
            for si, (c0, ln, sq) in enumerate(segs):
                sl = slice(c0, c0 + ln)
                P.ins("dve", "tensor_tensor_scan", out=Frow[0:4, sl], data0=one512[0:4, 0:ln], data1=frow[0:4, sl],
                      initial=0.0, op0=ALU.mult, op1=ALU.add)
                P.ins("dve", "tensor_tensor", out=irow[0:4, sl], in0=irow[0:4, sl], in1=Frow[0:4, sl], op=ALU.subtract)
                mp = c0 + si + 1
                P.ins("dve", "tensor_copy", out=Mrow[0:4, mp - 1:mp], in_=mrow[0:4, sq:sq + 1])
                P.ins("dve", "tensor_tensor_scan", out=Mrow[0:4, mp:mp + ln], data0=one512[0:4, 0:ln],
                      data1=irow[0:4, sl], initial=mrow[0:4, sq:sq + 1], op0=ALU.mult, op1=ALU.max)
                P.ins("dve", "tensor_tensor", out=Frow[0:4, sl], in0=Frow[0:4, sl], in1=Mrow[0:4, mp:mp + ln], op=ALU.add)
                P.ins("dve", "tensor_copy", out=mrow[0:4, sq:sq + 1], in_=Frow[0:4, c0 + ln - 1:c0 + ln])
            for (c0, sq, first, last, slot, si) in chunks:
                bb = bank()
                mp = c0 + si + 1
                P.mm(bb[:L, 0:4], Mrow[0:4, mp:mp + L], identf[0:4, 0:4])
                P.mm(bb[:L, 4:8], irow[0:4, c0:c0 + L], identf[0:4, 0:4])
                P.mm(bb[:L, 8:12], Frow[0:4, c0:c0 + L], identf[0:4, 0:4])
                P.ins("dve", "tensor_copy", out=gcol[:L, slot, :], in_=bb[:L, 0:12])

            P.ins("pool", "memset", ap=vext[:, :, :, 256:257], constant=1.0)
            chk("ml_gate", cur["ti"])

            for hp in range(2):
                for which, base, dstT, scl in (("q", O_MQ, qT, 1.0), ("k", O_MQ + 1024, kT, 0.0625)):
                    def conv_out(j, bb, m, which=which, base=base, dstT=dstT, scl=scl):
                        g = (0 if which == "q" else 8) + hp * 4 + 2 * half + j
                        for si, (c0, ln, sq) in enumerate(segs):
                            pos = c0 + 3 * si
                            P.ins("act", "activation", out=cbuf[:, pos + 3:pos + 3 + ln], in_=bb[:, c0:c0 + ln],
                                  func=AF.Copy)
                            P.ins("dve", "tensor_copy", out=cbuf[:, pos:pos + 3], in_=ccar[:, g, sq, :])
                        ln = segs[0][1]
                        cvw = cbuf[:, 0:nseg * (ln + 3)].rearrange("p (s t) -> p s t", t=ln + 3)
                        av_ = acc[:, 0:nseg * ln].rearrange("p (s t) -> p s t", t=ln)
                        P.ins("dve", "tensor_scalar", out=av_, in0=cvw[:, :, 0:ln], scalar1=cwc[:, g, 0:1],
                              scalar2=cbc[:, g:g + 1], op0=ALU.mult, op1=ALU.add)
                        for tap in range(1, 4):
                            P.ins("dve", "scalar_tensor_tensor", out=av_, in0=cvw[:, :, tap:tap + ln],
                                  scalar=cwc[:, g, tap:tap + 1], in1=av_, op0=ALU.mult, op1=ALU.add)
                        for si, (c0, ln_, sq) in enumerate(segs):
                            pos = c0 + 3 * si
                            P.ins("act", "activation", out=ccar[:, g, sq, :], in_=cbuf[:, pos + ln_:pos + ln_ + 3],
                                  func=AF.Copy)
                        dd = dstT[:, 2 * half + j, :NT]
                        if scl == 1.0:
                            P.ins("act", "activation", out=dd, in_=acc[:, :NT], func=AF.Silu)
                        else:
                            P.ins("act", "activation", out=stmp[:, :NT], in_=acc[:, :NT], func=AF.Silu)
                            P.ins("dve", "tensor_scalar", out=dd, in0=stmp[:, :NT], scalar1=scl, scalar2=None,
                                  op0=ALU.mult)
                    for half in range(2):
                        proj_fm(base + hp * 512 + half * 256, 256, NT, conv_out)
                def po_out(j, bb, m):
                    P.ins("act", "activation", out=sigo[:, 2 * half + j, :NT], in_=bb[:, :NT], func=AF.Sigmoid)
                for half in range(2):
                    proj_fm(O_MO + hp * 512 + half * 256, 256, NT, po_out)
                for hh in range(2):
                    h = 2 * hp + hh
                    wv = wload(Win[:, :, O_MV + h * 256:O_MV + h * 256 + 256], [16, 256])
                    for (c0, sq, first, last, slot, si) in chunks:
                        bb = bank()
                        for kc in range(16):
                            P.mm(bb[:L, 0:256], xnT[:, kc, c0:c0 + L], wv[:, kc, :], start=kc == 0, stop=kc == 15)
                        P.ins("act", "activation", out=vext[:L, slot, hh, 0:256], in_=bb[:L, 0:256], func=AF.Copy)
                chk("ml_proj", cur["ti"])

                for (c0, sq, first, last, slot, si) in chunks:
                    cc = slice(c0, c0 + L)
                    mp = c0 + si + 1
                    if first and NT == 80:
                        if sq < 4:
                            for hh in range(2):
                                h = 2 * hp + hh
                                P.dma("sync", Cst[:, h, :, 0:256],
                                      st_C[sq, h].rearrange("(dc p) v -> p dc v", p=128), f"stc{hh}")
                                P.dma("sync", Cst[:, h, :, 256:257],
                                      st_n[sq, h].rearrange("(dc p one) -> p dc one", p=128, one=1), f"stc{hh}",
                                      allow_slow_non_contiguous=True)
                        else:
                            for hh in range(2):
                                h = 2 * hp + hh
                                P.ins("dve", "memset", ap=Cst[:, h, :, :], constant=0.0)
                        for hh in range(2):
                            h = 2 * hp + hh
                            P.ins("act", "activation", out=Cbf[:, h, :, 0:257], in_=Cst[:, h, :, :], func=AF.Copy)
                    NUM = NUMr[slot % 2]
                    for hh in range(2):
                        h = 2 * hp + hh
                        r2 = (slot * 2 + hh) % 2
                        mpe = mpe_r[hh]
                        wcol = wcol_r[hh]
                        bS = bank()
                        for dc in range(2):
                            P.mm(bS[:L, 0:L], kT[:, 2 * hh + dc, cc], qT[:, 2 * hh + dc, cc], start=dc == 0, stop=dc == 1)
                        bM = bank()
                        P.mm(bM[:, 0:L + 1], sel4[0:4, h * 128:h * 128 + 128], Mrow[0:4, mp - 1:mp + L])
                        P.ins("act", "activation", out=mpe[:, 0:1], in_=bM[:, 0:1], func=AF.Copy)
                        P.ins("act", "activation", out=mpe[:, 1:2], in_=bM[:, L:L + 1], func=AF.Copy)
                        E = Eb[r2]
                        P.ins("act", "activation", out=E[:L, 0:L], in_=bM[:L, 1:L + 1], func=AF.Exp, scale=-1.0,
                              bias=gcol[:L, slot, 4 + h:5 + h])
                        P.ins("pool", "tensor_tensor", out=E[:L, 0:L], in0=E[:L, 0:L], in1=mlmask[:L, 0:L], op=ALU.mult)
                        ST = STb[r2]
                        P.ins("dve", "tensor_tensor", out=ST[:L, 0:L], in0=bS[:L, 0:L], in1=E[:L, 0:L], op=ALU.mult)
                        P.ins("act", "activation", out=wcol[:L, 0:1], in_=gcol[:L, slot, h:h + 1], func=AF.Exp,
                              scale=-1.0, bias=mpe[:L, 0:1])
                        P.ins("act", "activation", out=wcol[:, 1:2], in_=mpe[:, 1:2], func=AF.Exp, scale=-1.0,
                              bias=mpe[:, 0:1])
                        P.ins("act", "activation", out=wcol[:L, 2:3], in_=mpe[:L, 1:2], func=AF.Exp, scale=-1.0,
                              bias=gcol[:L, slot, 4 + h:5 + h])
                        bQ = bank()
                        for dc in range(2):
                            P.mm(bQ[:L, 0:257], qT[:, 2 * hh + dc, cc], Cbf[:, h, dc, 0:257], start=dc == 0, stop=dc == 1)
                        bI = bank()
                        P.mm(bI[:L, 0:257], ST[:L, 0:L], vext[:L, slot, hh, 0:257])
                        tq = tqc[r2]
                        P.ins("act", "activation", out=tq[:L, 0:257], in_=bQ[:L, 0:257], func=AF.Identity,
                              scale=wcol[:L, 0:1])
                        P.ins("dve", "tensor_tensor", out=NUM[:L, hh, 0:257], in0=tq[:L, 0:257], in1=bI[:L, 0:257],
                              op=ALU.add)
                        chk("ml_num", cur["ti"])
                        bK = bank()
                        for dc in range(2):
                            P.mm(bK[:L, dc * 128:(dc + 1) * 128], kT[:, 2 * hh + dc, cc], identb[:, :])
                        km = kTM[r2]
                        P.ins("act", "activation", out=km[:L, 0:256], in_=bK[:L, 0:256], func=AF.Identity,
                              scale=wcol[:L, 2:3])
                        for dc in range(2):
                            bC = bank()
                            P.mm(bC[:, 0:257], km[:L, dc * 128:(dc + 1) * 128], vext[:L, slot, hh, 0:257])
                            P.ins("dve", "scalar_tensor_tensor", out=Cst[:, h, dc, :], in0=Cst[:, h, dc, :],
                                  scalar=wcol[:, 1:2], in1=bC[:, 0:257], op0=ALU.mult, op1=ALU.add)
                        P.ins("act", "activation", out=Cbf[:, h, :, 0:257], in_=Cst[:, h, :, :], func=AF.Copy)
                        if last and (sq < 4 or last_tile):
                            pass
                        chk("ml_st", cur["ti"])
                        if last and (sq < 4 or last_tile):
                            P.dma("sync", o_C[sq, h].rearrange("(dc p) v -> p dc v", p=128), Cst[:, h, :, 0:256],
                                  f"oc{hh}")
                            P.dma("sync", o_n[sq, h].rearrange("(dc p one) -> p dc one", p=128, one=1),
                                  Cst[:, h, :, 256:257], f"oc{hh}", allow_slow_non_contiguous=True)
                    P.ins("act", "activation", out=lst[:L, 0, :], in_=NUM[:L, :, 256], func=AF.Abs)
                    P.ins("act", "activation", out=lst[:L, 1, :], in_=gcol[:L, slot, 8 + 2 * hp:10 + 2 * hp],
                          func=AF.Exp, scale=-1.0)
                    P.ins("dve", "tensor_tensor", out=lst[:L, 0, :], in0=lst[:L, 0, :], in1=lst[:L, 1, :], op=ALU.max)
                    P.ins("dve", "reciprocal", out=lst[:L, 0, :], in_=lst[:L, 0, :])
                    P.ins("dve", "tensor_tensor", out=hn[:L, :, :], in0=NUM[:L, :, 0:256],
                          in1=lst[:L, 0, :].unsqueeze(2).to_broadcast([L, 2, 256]), op=ALU.mult)
                    P.ins("dve", "tensor_reduce", out=lst[:L, 2, :], in_=hn[:L, :, :], axis=AX.X, op=ALU.add)
                    P.ins("act", "activation", out=hsq[:L, :, :], in_=hn[:L, :, :], func=AF.Square)
                    P.ins("dve", "tensor_reduce", out=lst[:L, 3, :], in_=hsq[:L, :, :], axis=AX.X, op=ALU.add)
                    P.ins("dve", "tensor_scalar", out=lst[:L, 2, :], in0=lst[:L, 2, :], scalar1=1.0 / 256, scalar2=None,
                          op0=ALU.mult)
                    P.ins("dve", "tensor_tensor", out=lst[:L, 4, :], in0=lst[:L, 2, :], in1=lst[:L, 2, :], op=ALU.mult)
                    P.ins("dve", "scalar_tensor_tensor", out=lst[:L, 4, :], in0=lst[:L, 3, :], scalar=1.0 / 256,
                          in1=lst[:L, 4, :], op0=ALU.mult, op1=ALU.subtract)
                    P.ins("dve", "tensor_scalar", out=lst[:L, 4, :], in0=lst[:L, 4, :], scalar1=ML_EPS, scalar2=None,
                          op0=ALU.add)
                    P.ins("act", "activation", out=lst[:L, 4, :], in_=lst[:L, 4, :], func=AF.Sqrt)
                    P.ins("dve", "reciprocal", out=lst[:L, 4, :], in_=lst[:L, 4, :])
                    P.ins("dve", "tensor_tensor", out=hsq[:L, :, :], in0=hn[:L, :, :],
                          in1=lst[:L, 2, :].unsqueeze(2).to_broadcast([L, 2, 256]), op=ALU.subtract)
                    P.ins("dve", "tensor_tensor", out=ynb[:L, :, :], in0=hsq[:L, :, :],
                          in1=lst[:L, 4, :].unsqueeze(2).to_broadcast([L, 2, 256]), op=ALU.mult)
                    chk("ml_ln", cur["ti"])
                    bT = bank()
                    ynf = ynb[:, :, :].rearrange("p a b -> p (a b)")
                    for gq in range(4):
                        P.mm(bT[:, gq * 128:gq * 128 + L], ynf[:L, gq * 128:(gq + 1) * 128], identb[:L, :L])
                    for gq in range(4):
                        g = hp * 4 + gq
                        P.ins("dve", "scalar_tensor_tensor", out=ymlT[:, g, cc], in0=bT[:, gq * 128:gq * 128 + L],
                              scalar=nwc[:, g:g + 1], in1=sigo[:, gq, cc], op0=ALU.mult, op1=ALU.mult)
                    chk("ml_epi", cur["ti"])

        def merge(NT, subt):
            mg = carve(0, [NKC, 512], BF16)
            t1 = carve(16384, [512], F32)
            t2 = carve(18432, [512], F32)
            t3 = carve(20480, [512], F32)
            Wr = W["w_br_rw"].rearrange("(kc p) d -> p kc d", p=128)
            Wm = W["w_br_ml"].rearrange("(kc p) d -> p kc d", p=128)
            for f0 in range(0, D, 256):
                wg1 = wload(Win[:, :, O_G1 + f0:O_G1 + f0 + 256], [16, 256])
                wg2 = wload(Win[:, :, O_G2 + f0:O_G2 + f0 + 256], [16, 256])
                wr = wload(Wr[:, :, f0:f0 + 256], [8, 256])
                wm = wload(Wm[:, :, f0:f0 + 256], [8, 256])
                for j in range(2):
                    fc = f0 // 128 + j
                    cs_ = slice(j * 128, (j + 1) * 128)
                    b1, b2, b3, b4 = bank(), bank(), bank(), bank()
                    for kc in range(16):
                        P.mm(b1[:, :NT], wg1[:, kc, cs_], xnT[:, kc, :NT], start=kc == 0, stop=kc == 15)
                    for kc in range(16):
                        P.mm(b2[:, :NT], wg2[:, kc, cs_], xnT[:, kc, :NT], start=kc == 0, stop=kc == 15)
                    for kc in range(8):
                        P.mm(b3[:, :NT], wr[:, kc, cs_], yrwT[:, kc, :NT], start=kc == 0, stop=kc == 7)
                    for kc in range(8):
                        P.mm(b4[:, :NT], wm[:, kc, cs_], ymlT[:, kc, :NT], start=kc == 0, stop=kc == 7)
                    P.ins("act", "activation", out=t1[:, :NT], in_=b1[:, :NT], func=AF.Sigmoid)
                    P.ins("act", "activation", out=t2[:, :NT], in_=b2[:, :NT], func=AF.Sigmoid)
                    P.ins("dve", "tensor_tensor", out=t1[:, :NT], in0=t1[:, :NT], in1=b3[:, :NT], op=ALU.mult)
                    P.ins("dve", "tensor_tensor", out=t2[:, :NT], in0=t2[:, :NT], in1=b4[:, :NT], op=ALU.mult)
                    P.ins("pool", "tensor_tensor", out=mg[:, fc, :NT], in0=t1[:, :NT], in1=t2[:, :NT], op=ALU.add)
            Wo = W["w_out"].rearrange("(kc p) d -> p kc d", p=128)
            for cb in range(4):
                bks = [bank() for _ in subt]
                for k0 in range(0, 16, 8):
                    wo = wload(Wo[:, k0:k0 + 8, cb * 512:(cb + 1) * 512], [8, 512])
                    for si, (st, n) in enumerate(subt):
                        for k in range(8):
                            P.mm(bks[si][:n, :], mg[:, k0 + k, st * 128:st * 128 + n], wo[:, k, :],
                                 start=(k0 + k == 0), stop=(k0 + k == 15))
                for si, (st, n) in enumerate(subt):
                    xs_ = xres[:n, st, cb * 512:(cb + 1) * 512]
                    P.ins("dve", "tensor_tensor", out=xs_, in0=bks[si][:n, :], in1=xs_, op=ALU.add)

        try:
            for ti in range(NPT + 1):
                last_tile = ti == NPT
                cur["ti"] = ti
                wl_state["n"] = 0
                if ti == 0:
                    NT = 80
                    subt = [(0, 80)]
                    P.dma("sync", xres[0:64, 0, :], xs, "xin0")
                    P.dma("sync", xres[64:80, 0, :], meta, "xin0")
                    segs = [(16 * j, 16, j) for j in range(5)]
                    Crw, Lml = 16, 16
                else:
                    NT = 512
                    subt = [(st, 128) for st in range(4)]
                    for st in range(4):
                        r0 = (ti - 1) * 512 + st * 128
                        P.dma("sync", xres[:, st, :], xp[r0:r0 + 128, :], f"xin{st}")
                    segs = [(0, 512, 4)]
                    Crw, Lml = 64, 128
                chk("load", ti)
                ffn("ffn1", 0, NT, subt)
                chk("ffn1", ti)
                if dbg and ti == 1:
                    for st in range(4):
                        P.dma("sync", dbg_out["d_x1"][st * 128:(st + 1) * 128, :], xres[:, st, :], "dbg")
                rmsnorm_T(1, subt)
                rwkv(NT, segs, Crw, last_tile)
                chk("rwkv", ti)
                mlstm(NT, segs, Lml, last_tile)
                chk("mlstm", ti)
                if dbg and ti == 1:
                    P.dma("sync", dbg_out["d_yrw"], yrwT[:], "dbg")
                    P.dma("sync", dbg_out["d_yml"], ymlT[:], "dbg")
                merge(NT, subt)
                chk("merge", ti)
                if dbg and ti == 1:
                    for st in range(4):
                        P.dma("sync", dbg_out["d_x2"][st * 128:(st + 1) * 128, :], xres[:, st, :], "dbg")
                ffn("ffn2", 2, NT, subt)
                for st, n in subt:
                    xsb = xsbs[st % 2]
                    ssq = small[:n, 16 + 2 * st:17 + 2 * st]
                    rstd = small[:n, 17 + 2 * st:18 + 2 * st]
                    P.ins("act", "activation", out=xsb[:n, :], in_=xres[:n, st, :], func=AF.Square, accum_out=ssq)
                    P.ins("dve", "tensor_scalar", out=rstd, in0=ssq, scalar1=1.0 / D, scalar2=1e-6, op0=ALU.mult, op1=ALU.add)
                    P.ins("act", "activation", out=rstd, in_=rstd, func=AF.Sqrt)
                    P.ins("dve", "reciprocal", out=rstd, in_=rstd)
                    P.ins("dve", "scalar_tensor_tensor", out=xres[:n, st, :], in0=xres[:n, st, :], scalar=rstd,
                          in1=gfin[:n, :], op0=ALU.mult, op1=ALU.mult)
                    if ti == 0:
                        P.dma("sync", ys, xres[0:64, 0, :], "yout0")
                    else:
                        r0 = (ti - 1) * 512 + st * 128
                        P.dma("sync", yp[r0:r0 + 128, :], xres[:, st, :], f"yout{st}")

        except _Stop:
            pass
        if stop is not None:
            P.emit()
            return nc
        stg = carve(0, [3392], F32)
        stg2 = carve(16384, [2048], F32)
        for r in range(7):
            bb = bank()
            gs = list(range(r * 4, min(27, r * 4 + 4)))
            for g in gs:
                n = 128 if g < 26 else 32
                P.mm(bb[0:5, (g - r * 4) * 128:(g - r * 4) * 128 + n], carry[:n, g, :], identf[:n, :n])
            c0 = r * 512
            c1 = min(RWC, c0 + 512)
            P.ins("dve", "tensor_copy", out=stg[0:5, c0:c1], in_=bb[0:5, 0:c1 - c0])
        P.dma("sync", o_shift, stg[0:5, 0:RWC], "ofin")
        for r in range(4):
            bb = bank()
            for g in range(r * 4, r * 4 + 4):
                P.mm(bb[0:15, (g - r * 4) * 128:(g - r * 4 + 1) * 128],
                     ccar[:, g, :, :].rearrange("p s j -> p (s j)"), identf[:, :])
            P.ins("dve", "tensor_copy", out=stg2[0:15, r * 512:(r + 1) * 512], in_=bb[0:15, :])
        P.dma("sync", o_conv.rearrange("s j c -> (s j) c"), stg2[0:15, :], "ofin")
        P.dma("sync", o_m.rearrange("s h -> h s"), mrow[0:4, 0:5], "ofin", allow_slow_non_contiguous=True)
        P.emit()
    return nc


_CACHE = {}


def _get_nc(NPT, dbg=False):
    k = (NPT, dbg)
    if k not in _CACHE:
        _CACHE[k] = build(NPT, dbg)
    return _CACHE[k]


def make_in_maps(inputs, ncores, NPT):
    cst = make_consts()
    maps = []
    for c in range(ncores):
        m = {
            "xp": np.ascontiguousarray(inputs["x_prompt"][c, :NPT * 512]),
            "xs": np.ascontiguousarray(inputs["x_sample"][4 * c:4 * c + 4].reshape(64, D)),
            "meta": np.ascontiguousarray(inputs["meta_tokens"]),
            "st_shift": np.ascontiguousarray(inputs["state_rwkv_shift"][0, 4 * c:4 * c + 4]),
            "st_wkv": np.ascontiguousarray(inputs["state_rwkv_wkv"][0, 4 * c:4 * c + 4]),
            "st_conv": np.ascontiguousarray(inputs["state_mlstm_conv"][0, 4 * c:4 * c + 4]),
            "st_C": np.ascontiguousarray(inputs["state_mlstm_C"][0, 4 * c:4 * c + 4]),
            "st_n": np.ascontiguousarray(inputs["state_mlstm_n"][0, 4 * c:4 * c + 4]),
            "st_m": np.ascontiguousarray(inputs["state_mlstm_m"][0, 4 * c:4 * c + 4]),
            "cst": cst,
        }
        for nm, shp in WSHAPES:
            m[nm] = np.ascontiguousarray(np.asarray(inputs[nm]).reshape(shp))
        maps.append(m)
    return maps


def assemble(results, ncores):
    f = np.float32
    cat = lambda k, sl: np.concatenate([np.asarray(r[k])[sl] for r in results], 0)
    y_prompt = np.stack([np.asarray(r["yp"]) for r in results], 0).astype(f)
    y_sample = np.concatenate([np.asarray(r["ys"]).reshape(4, 16, D) for r in results], 0).astype(f)
    outs = [y_prompt, y_sample]
    for k in ("o_shift", "o_wkv", "o_conv", "o_C", "o_n", "o_m"):
        outs.append(cat(k, slice(4, 5))[None].astype(f))
    for k in ("o_shift", "o_wkv", "o_conv", "o_C", "o_n", "o_m"):
        outs.append(cat(k, slice(0, 4))[None].astype(f))
    return tuple(outs)


def kernel(**inputs):
    inputs = {k: np.asarray(v) for k, v in inputs.items()}
    NPT = inputs["x_prompt"].shape[1] // 512
    ncores = inputs["x_prompt"].shape[0]
    nc = _get_nc(NPT)
    in_maps = make_in_maps(inputs, ncores, NPT)
    res = run_bass_kernel_spmd(nc, in_maps, core_ids=list(range(ncores)))
    return assemble(res.results, ncores)
```

```python
import numpy as np
from contextlib import ExitStack
import concourse.bass as bass
import concourse.mybir as mybir
from concourse.bass_utils import run_bass_kernel_spmd

F32 = mybir.dt.float32
BF16 = mybir.dt.bfloat16
AF = mybir.ActivationFunctionType
ALU = mybir.AluOpType
AX = mybir.AxisListType

SEM_ROT = 20000
_DTSZ = {}


def _dsz(dt):
    s = _DTSZ.get(dt)
    if s is None:
        s = 2 if dt == BF16 else 4
        _DTSZ[dt] = s
    return s


def _rect(ap):
    a = ap.ap
    pstep, pn = a[0]
    off = ap.offset
    if pstep == 0:
        p0 = 0
        f0 = off
    else:
        p0 = off // pstep
        f0 = off % pstep
    ext = 0
    for st, cnt in a[1:]:
        ext += (cnt - 1) * abs(st)
    sz = _dsz(ap.dtype)
    return (ap.tensor.name, p0, p0 + pn, f0 * sz, (f0 + ext + 1) * sz)


class Op:
    __slots__ = ("eng", "fn", "waits", "sig", "dkey", "dcount", "pos", "semidx", "count", "idx")

    def __init__(self, eng, fn):
        self.eng = eng
        self.fn = fn
        self.waits = []
        self.sig = False
        self.dkey = None
        self.dcount = 0
        self.pos = 0


class Prog:
    ENGS = ("pe", "act", "dve", "pool", "sync")

    def __init__(self, nc):
        self.nc = nc
        self.ops = []
        self.by_eng = {e: [] for e in self.ENGS}
        self.recs = {}
        self.waited = {}
        self.dma_counts = {}

    @staticmethod
    def _is_ap(v):
        return hasattr(v, "ap") and hasattr(v, "tensor") and hasattr(v, "offset")

    def _track(self, v):
        if not self._is_ap(v):
            return None
        sp = str(v.space)
        if "PSUM" in sp:
            return (v.tensor.name, 0, 128, 0, 1 << 20)
        if "SB" in sp:
            return _rect(v)
        return None

    def add(self, eng, fn, reads, writes, dkey=None, after=None):
        op = Op(eng, fn)
        op.pos = len(self.by_eng[eng])
        idx = len(self.ops)
        op.idx = idx
        deps = set(o.idx for o in (after or []))
        rl = list(dict.fromkeys(r for r in (self._track(v) for v in reads) if r is not None))
        wl = list(dict.fromkeys(r for r in (self._track(v) for v in writes) if r is not None))
        wl = list(dict.fromkeys(wl + [r for r in rl if r[0].startswith("ps")]))
        rl = [r for r in rl if not r[0].startswith("ps")]
        for (nm, p0, p1, f0, f1) in rl:
            for rec in self.recs.setdefault(nm, []):
                if rec[5] and rec[0] < p1 and p0 < rec[1] and rec[2] < f1 and f0 < rec[3]:
                    deps.add(rec[4])
        for (nm, p0, p1, f0, f1) in wl:
            for rec in self.recs.setdefault(nm, []):
                if rec[0] < p1 and p0 < rec[1] and rec[2] < f1 and f0 < rec[3]:
                    deps.add(rec[4])
        for (nm, p0, p1, f0, f1) in wl:
            lst = self.recs[nm]
            lst[:] = [rec for rec in lst if not (p0 <= rec[0] and rec[1] <= p1 and f0 <= rec[2] and rec[3] <= f1)]
            lst.append([p0, p1, f0, f1, idx, True])
        for (nm, p0, p1, f0, f1) in rl:
            lst = self.recs[nm]
            lst[:] = [rec for rec in lst if not ((not rec[5]) and rec[0] == p0 and rec[1] == p1 and rec[2] == f0
                                                  and rec[3] == f1 and rec[4] != idx and self.ops[rec[4]].eng == eng)]
            lst.append([p0, p1, f0, f1, idx, False])
        deps.discard(idx)
        for j in sorted(deps):
            oj = self.ops[j]
            if oj.dkey is not None:
                k = (eng, "d", oj.dkey)
                if self.waited.get(k, 0) >= oj.dcount:
                    continue
                self.waited[k] = oj.dcount
                op.waits.append(("d", oj.dkey, j))
            else:
                if oj.eng == eng and eng == "pe":
                    continue
                k = (eng, "e", oj.eng)
                if self.waited.get(k, -1) >= oj.pos:
                    continue
                self.waited[k] = oj.pos
                oj.sig = True
                op.waits.append(("e", oj.eng, j))
        if dkey is not None:
            op.dkey = dkey
            self.dma_counts[dkey] = self.dma_counts.get(dkey, 0) + 16
            op.dcount = self.dma_counts[dkey]
        self.ops.append(op)
        self.by_eng[eng].append(op)
        return op

    def seal_key(self, key):
        tot = self.dma_counts.get(key, 0)
        for op in self.ops:
            if op.dkey == key:
                op.dcount = tot

    def ins(self, eng, meth, *, reads=None, writes=None, **kw):
        r = list(reads or [])
        w = list(writes or [])
        for k, v in kw.items():
            if self._is_ap(v):
                if k in ("out", "accum_out", "ap"):
                    w.append(v)
                else:
                    r.append(v)
        return self.add(eng, lambda e: getattr(e, meth)(**kw), r, w)

    def mm(self, out, lhsT, rhs, start=True, stop=True, tp=None):
        if tp is None:
            return self.add("pe", lambda e: e.matmul(out, lhsT, rhs, start=start, stop=stop), [lhsT, rhs], [out])
        return self.add("pe", lambda e: e.matmul(out, lhsT, rhs, start=start, stop=stop, tile_position=tp),
                        [lhsT, rhs], [out])

    def dma(self, eng, out, in_, key, after=None, **kw):
        return self.add(eng, lambda e: e.dma_start(out=out, in_=in_, **kw), [in_], [out], dkey=key, after=after)

    def emit(self):
        nc = self.nc
        with ExitStack() as es:
            esems = {}
            for eng in self.ENGS:
                c = 0
                for op in self.by_eng[eng]:
                    if op.sig:
                        c += 1
                        op.semidx = (c - 1) // SEM_ROT
                        op.count = (c - 1) % SEM_ROT + 1
                nsem = (c + SEM_ROT - 1) // SEM_ROT
                esems[eng] = [es.enter_context(nc.semaphore(f"s_{eng}_{i}")) for i in range(max(nsem, 1))]
            dsems = {k: es.enter_context(nc.semaphore(f"d_{k}")) for k in self.dma_counts}
            block = es.enter_context(nc.Block())
            ops = self.ops

            def run(eng_name):
                def body(e):
                    for op in self.by_eng[eng_name]:
                        for (kind, key, j) in op.waits:
                            oj = ops[j]
                            if kind == "d":
                                e.wait_ge(dsems[key], oj.dcount)
                            else:
                                e.wait_ge(esems[key][oj.semidx], oj.count)
                        inst = op.fn(e)
                        if op.dkey is not None:
                            inst.then_inc(dsems[op.dkey], 16)
                        elif op.sig:
                            inst.then_inc(esems[eng_name][op.semidx], 1)
                    if eng_name == "sync":
                        for k, tot in self.dma_counts.items():
                            e.wait_ge(dsems[k], tot)
                return body

            block.tensor(run("pe"))
            block.scalar(run("act"))
            block.vector(run("dve"))
            block.gpsimd(run("pool"))
            block.sync(run("sync"))


D = 2048
DFF = 5504
NKC = 16
NFC = 43
RWC = 3360
O_R, O_K, O_V, O_LW, O_LG = 0, 1024, 2048, 3072, 3200
O_MQ, O_MV, O_MO, O_MI, O_MF, O_G1, O_G2 = 3360, 5408, 6432, 7456, 7460, 7464, 9512
INC = 11560
KAPPA = -0.6065306597126334
RW_EPS = 64e-5
ML_EPS = 1e-5

WSHAPES = [
    ("ffn1_norm", (D,)), ("ffn1_w_gate", (D, DFF)), ("ffn1_w_up", (D, DFF)), ("ffn1_w_down", (DFF, D)),
    ("mix_norm", (D,)), ("w_in", (D, INC)), ("rw_mu", (RWC,)), ("rw_w0", (1024,)), ("rw_w2", (64, 1024)),
    ("rw_a0", (1024,)), ("rw_a2", (64, 1024)), ("rw_g2", (160, 1024)), ("rw_kk", (1024,)), ("rw_ka", (1024,)),
    ("rw_rk", (1024,)), ("rw_ln_w", (1024,)), ("rw_ln_b", (1024,)), ("ml_conv_w", (4, 2048)),
    ("ml_conv_b", (2048,)), ("ml_i_b", (4,)), ("ml_f_b", (4,)), ("ml_norm_w", (1024,)),
    ("w_br_rw", (1024, D)), ("w_br_ml", (1024, D)), ("w_out", (D, D)), ("ffn2_norm", (D,)),
    ("ffn2_w_gate", (D, DFF)), ("ffn2_w_up", (D, DFF)), ("ffn2_w_down", (DFF, D)), ("final_norm", (D,)),
]

C_ID = 0
C_M5 = 128
C_OBD = 448
C_R64 = 576
C_R16 = 1088
C_ML = 1600
C_SEL = 1728
C_ONE = 2240
CST_N = 2752


class _Stop(Exception):
    pass


def make_consts():
    c = np.zeros((128, CST_N), np.float32)
    c[:, C_ID:C_ID + 128] = np.eye(128, dtype=np.float32)
    s = np.arange(64)[:, None]
    t = np.arange(64)[None, :]
    strict = (s < t).astype(np.float32)
    strictT = (t < s).astype(np.float32)
    incl = (s <= t).astype(np.float32)
    for h in range(2):
        r = slice(h * 64, h * 64 + 64)
        for i, m in enumerate((strict, strictT, strict, incl, incl)):
            c[r, C_M5 + i * 64:C_M5 + (i + 1) * 64] = m
        c[r, C_OBD + h * 64:C_OBD + h * 64 + 64] = 1.0
    r64 = np.ones(512, np.float32)
    r64[::64] = 0.0
    r16 = np.ones(512, np.float32)
    r16[::16] = 0.0
    c[:, C_R64:C_R64 + 512] = r64[None]
    c[:, C_R16:C_R16 + 512] = r16[None]
    s2 = np.arange(128)[:, None]
    t2 = np.arange(128)[None, :]
    c[:, C_ML:C_ML + 128] = (s2 <= t2).astype(np.float32)
    for h in range(4):
        c[h, C_SEL + h * 128:C_SEL + (h + 1) * 128] = 1.0
    c[:, C_ONE:C_ONE + 512] = 1.0
    return c


def build(NPT, dbg=False, stop=None):
    SEQ = NPT * 512
    nc = bass.Bass("TRN2", target_bir_lowering=False)

    def din(name, shape):
        return nc.dram_tensor(name, list(shape), F32, kind="ExternalInput").ap()

    def dout(name, shape):
        return nc.dram_tensor(name, list(shape), F32, kind="ExternalOutput").ap()

    xp = din("xp", [SEQ, D])
    xs = din("xs", [64, D])
    meta = din("meta", [16, D])
    st_shift = din("st_shift", [4, RWC])
    st_wkv = din("st_wkv", [4, 16, 64, 64])
    st_conv = din("st_conv", [4, 3, 2048])
    st_C = din("st_C", [4, 4, 256, 256])
    st_n = din("st_n", [4, 4, 256])
    st_m = din("st_m", [4, 4])
    cst_d = din("cst", [128, CST_N])
    W = {nm: din(nm, shp) for nm, shp in WSHAPES}
    yp = dout("yp", [SEQ, D])
    ys = dout("ys", [64, D])
    o_shift = dout("o_shift", [5, RWC])
    o_wkv = dout("o_wkv", [5, 16, 64, 64])
    o_conv = dout("o_conv", [5, 3, 2048])
    o_C = dout("o_C", [5, 4, 256, 256])
    o_n = dout("o_n", [5, 4, 256])
    o_m = dout("o_m", [5, 4])
    dbg_out = {}
    if dbg:
        dbg_out["d_x1"] = dout("d_x1", [512, D])
        dbg_out["d_x2"] = dout("d_x2", [512, D])
        dbg_out["d_yrw"] = dout("d_yrw", [128, 8, 512])
        dbg_out["d_yml"] = dout("d_yml", [128, 8, 512])

    es = ExitStack()
    with es:
        def sb(name, shape, dt):
            return es.enter_context(nc.sbuf_tensor(name, list(shape), dt))

        P = Prog(nc)
        NRING = 6
        xres = sb("xres", [128, 4, D], F32)
        xnT = sb("xnT", [128, NKC, 512], BF16)
        SCRB = 52224
        scr = sb("scr", [128, SCRB // 4], F32)
        wring = sb("wring", [128, NRING, 4096], BF16)
        sgb = sb("sgb", [128, 1, 512], F32)
        yrwT = sb("yrwT", [128, 8, 512], BF16)
        ymlT = sb("ymlT", [128, 8, 512], BF16)
        Pst = sb("Pst", [128, 8, 64], F32)
        Pbf = sb("Pbf", [128, 8, 64], BF16)
        Cst = sb("Cst", [128, 4, 2, 257], F32)
        Cbf = sb("Cbf", [128, 4, 2, 258], BF16)
        gfin = sb("gfin", [128, D], F32)
        identb = sb("identb", [128, 128], BF16)
        identf = sb("identf", [128, 128], F32)
        mask5 = sb("mask5", [128, 5, 64], BF16)
        onesbd = sb("onesbd", [128, 128], F32)
        r64 = sb("r64", [128, 512], BF16)
        r16 = sb("r16", [128, 512], BF16)
        one512 = sb("one512", [128, 512], BF16)
        mlmask = sb("mlmask", [128, 128], BF16)
        sel4 = sb("sel4", [4, 512], F32)
        w2a2 = sb("w2a2", [128, 1024], BF16)
        g2a = sb("g2a", [128, 1024], BF16)
        g2b = sb("g2b", [32, 1024], BF16)
        gcols = sb("gcols", [128, 3, 16], F32)
        mucols = sb("mucols", [128, 27], F32)
        rwc = sb("rwc", [128, 7, 8], F32)
        cwc = sb("cwc", [128, 16, 4], F32)
        cbc = sb("cbc", [128, 16], F32)
        nwc = sb("nwc", [128, 8], F32)
        ifb = sb("ifb", [4, 2], F32)
        carry = sb("carry", [128, 27, 5], F32)
        ccar = sb("ccar", [128, 16, 5, 3], F32)
        mrow = sb("mrow", [4, 8], F32)
        small = sb("small", [128, 64], F32)

        banks = [es.enter_context(nc.psum_tensor(f"ps{i}", [128, 512], F32)) for i in range(8)]
        bstate = {"i": 0, "w": 0}
        cur = {"ti": 0}

        chk_cnt = {}

        def chk(tag, ti=None):
            if stop is None:
                return
            want = stop[0]
            k = 1
            if "#" in want:
                want, k = want.split("#")
                k = int(k)
            if want == tag and (ti is None or stop[1] == ti):
                chk_cnt[tag] = chk_cnt.get(tag, 0) + 1
                if chk_cnt[tag] >= k:
                    raise _Stop()

        def bank():
            b = banks[bstate["i"] % 8]
            bstate["i"] += 1
            return b

        def carve(off, shape, dt):
            n = int(np.prod(shape))
            sz = _dsz(dt)
            assert off % 4 == 0 and off + n * sz <= SCRB, (off, shape)
            nf = (n * sz + 3) // 4
            v = scr[:, off // 4: off // 4 + nf]
            if dt == BF16:
                v = v.bitcast(BF16)[:, 0:n]
            if len(shape) == 1:
                return v
            names = " ".join(f"a{i}" for i in range(len(shape)))
            kw = {f"a{i}": int(s) for i, s in enumerate(shape)}
            return v.rearrange(f"p ({names}) -> p {names}", **kw)

        NWL = 219
        wscr = nc.dram_tensor("wscr", [NWL, 128, 4096], BF16).ap()
        wr_ops = {}
        wl_state = {"n": 0}

        def wload(src, shape):
            s = bstate["w"] % NRING
            bstate["w"] += 1
            i = wl_state["n"]
            wl_state["n"] += 1
            a, b = shape
            assert a * b <= 4096 and i < NWL
            flat = wring[:, s, 0:a * b]
            dst = flat.rearrange("p (a b) -> p a b", a=a)
            if cur["ti"] == 0:
                P.dma("pool", dst, src, f"wp{s}")
                if NPT > 0:
                    wr_ops[i] = P.dma("sync", wscr[i, :, 0:a * b], flat, f"sw{s}")
            else:
                P.dma("sync", flat, wscr[i, :, 0:a * b], f"w{s}", after=[wr_ops[i]])
            return dst

        xsbs = [carve(SCRB - 8192, [D], BF16), carve(SCRB - 4096, [D], BF16)]
        cstg = carve(0, [CST_N], F32)
        stgA = carve(11008, [3392], F32)
        stgB = carve(11008 + 13568, [2048], F32)
        P.dma("sync", cstg, cst_d, "const")
        P.dma("sync", gfin[:], W["final_norm"].partition_broadcast(128), "const")
        stgV = [carve(36864, [128], F32), carve(36864 + 512, [128], F32)]
        P.ins("pool", "memset", ap=stgV[0][:, :], constant=0.0)
        P.ins("pool", "memset", ap=stgV[1][:, :], constant=0.0)

        def vrows(t, r0, ap, n):
            P.dma("sync", stgV[t][r0:r0 + n, :], ap.rearrange("(c p) -> c p", p=128), "const")
        vrows(0, 0, W["ffn1_norm"], 16)
        vrows(0, 16, W["mix_norm"], 16)
        vrows(0, 32, W["ffn2_norm"], 16)
        vrows(0, 48, W["rw_mu"][0:3328], 26)
        P.dma("sync", stgV[0][74:75, 0:32], W["rw_mu"][3328:3360].rearrange("(c p) -> c p", p=32), "const")
        for i, nm in enumerate(("rw_w0", "rw_a0", "rw_kk", "rw_ka", "rw_rk", "rw_ln_w")):
            vrows(0, 75 + 8 * i, W[nm], 8)
        vrows(1, 0, W["rw_ln_b"], 8)
        for j in range(4):
            vrows(1, 8 + 16 * j, W["ml_conv_w"][j], 16)
        vrows(1, 72, W["ml_conv_b"], 16)
        vrows(1, 88, W["ml_norm_w"], 8)
        P.dma("sync", ifb[:, 0:1], W["ml_i_b"].rearrange("(p c) -> p c", c=1), "const",
              allow_slow_non_contiguous=True)
        P.dma("sync", ifb[:, 1:2], W["ml_f_b"].rearrange("(p c) -> p c", c=1), "const",
              allow_slow_non_contiguous=True)
        P.dma("pool", w2a2[0:64, :], W["rw_w2"], "constp")
        P.dma("pool", w2a2[64:128, :], W["rw_a2"], "constp")
        P.dma("pool", g2a[:, :], W["rw_g2"][0:128, :], "constp")
        P.dma("pool", g2b[:, :], W["rw_g2"][128:160, :], "constp")
        P.dma("sync", stgA[0:4, 0:RWC], st_shift, "const")
        P.dma("sync", stgB[0:12, 0:2048], st_conv.rearrange("s j c -> (s j) c"), "const")
        P.dma("sync", mrow[0:4, 0:4], st_m.rearrange("s h -> h s"), "const", allow_slow_non_contiguous=True)
        P.seal_key("const")
        P.seal_key("constp")

        P.ins("dve", "tensor_copy", out=identb[:], in_=cstg[:, C_ID:C_ID + 128])
        P.ins("dve", "tensor_copy", out=identf[:], in_=cstg[:, C_ID:C_ID + 128])
        bA = bank()
        bB = bank()
        P.mm(bA[:, 0:123], stgV[0][0:123, :], identf[0:123, 0:123])
        P.mm(bB[:, 0:96], stgV[1][0:96, :], identf[0:96, 0:96])
        P.ins("dve", "tensor_copy", out=gcols[:].rearrange("p a b -> p (a b)"), in_=bA[:, 0:48])
        P.ins("dve", "tensor_copy", out=mucols[:, :], in_=bA[:, 48:75])
        P.ins("dve", "tensor_copy", out=rwc[:, 0:6, :].rearrange("p a b -> p (a b)"), in_=bA[:, 75:123])
        P.ins("dve", "tensor_copy", out=rwc[:, 6, :], in_=bB[:, 0:8])
        P.ins("dve", "tensor_copy", out=cwc[:].rearrange("p c j -> p j c"),
              in_=bB[:, 8:72].rearrange("p (j c) -> p j c", j=4))
        P.ins("dve", "tensor_copy", out=cbc[:, :], in_=bB[:, 72:88])
        P.ins("dve", "tensor_copy", out=nwc[:, :], in_=bB[:, 88:96])
        P.ins("dve", "tensor_copy", out=mask5[:].rearrange("p a b -> p (a b)"), in_=cstg[:, C_M5:C_M5 + 320])
        P.ins("dve", "tensor_copy", out=onesbd[:], in_=cstg[:, C_OBD:C_OBD + 128])
        P.ins("dve", "tensor_copy", out=r64[:], in_=cstg[:, C_R64:C_R64 + 512])
        P.ins("dve", "tensor_copy", out=r16[:], in_=cstg[:, C_R16:C_R16 + 512])
        P.ins("dve", "tensor_copy", out=one512[:], in_=cstg[:, C_ONE:C_ONE + 512])
        P.ins("dve", "tensor_copy", out=mlmask[:], in_=cstg[:, C_ML:C_ML + 128])
        P.ins("dve", "tensor_copy", out=sel4[:], in_=cstg[0:4, C_SEL:C_SEL + 512])
        P.ins("pool", "memset", ap=carry[:], constant=0.0)
        P.ins("pool", "memset", ap=ccar[:], constant=0.0)
        P.ins("pool", "memset", ap=mrow[0:4, 4:8], constant=0.0)
        P.ins("pool", "memset", ap=small[:], constant=0.0)
        b = bank()
        for g in range(27):
            n = 128 if g < 26 else 32
            P.mm(b[:n, g * 4:g * 4 + 4], stgA[0:4, g * 128:g * 128 + n], identf[0:4, 0:4])
        P.ins("dve", "tensor_copy", out=carry[:, 0:26, 0:4], in_=b[:, 0:104].rearrange("p (g s) -> p g s", s=4))
        P.ins("dve", "tensor_copy", out=carry[0:32, 26, 0:4], in_=b[0:32, 104:108])
        b = bank()
        for g in range(16):
            P.mm(b[:, g * 12:g * 12 + 12], stgB[0:12, g * 128:g * 128 + 128], identf[0:12, 0:12])
        P.ins("dve", "tensor_copy", out=ccar[:, :, 0:4, :],
              in_=b[:, 0:192].rearrange("p (g s j) -> p g s j", s=4, j=3))

        def rmsnorm_T(gi, subt):
            for st, n in subt:
                xsb = xsbs[st % 2]
                ssq = small[:n, 8 + 2 * st:9 + 2 * st]
                rstd = small[:n, 9 + 2 * st:10 + 2 * st]
                P.ins("act", "activation", out=xsb[:n, :], in_=xres[:n, st, :], func=AF.Square, accum_out=ssq)
                P.ins("dve", "tensor_scalar", out=rstd, in0=ssq, scalar1=1.0 / D, scalar2=1e-6,
                      op0=ALU.mult, op1=ALU.add)
                P.ins("act", "activation", out=rstd, in_=rstd, func=AF.Sqrt)
                P.ins("dve", "reciprocal", out=rstd, in_=rstd)
                P.ins("dve", "tensor_scalar", out=xsb[:n, :], in0=xres[:n, st, :], scalar1=rstd, scalar2=None,
                      op0=ALU.mult)
                for c0 in range(0, 16, 4):
                    bb = bank()
                    for c in range(4):
                        P.mm(bb[:, c * 128:c * 128 + n], xsb[:n, (c0 + c) * 128:(c0 + c + 1) * 128], identb[:n, :n])
                    P.ins("dve", "tensor_tensor",
                          out=xnT[:, c0:c0 + 4, st * 128:st * 128 + n],
                          in0=bb[:, :].rearrange("p (c t) -> p c t", t=128)[:, :, 0:n],
                          in1=gcols[:, gi, c0:c0 + 4].unsqueeze(2).to_broadcast([128, 4, n]), op=ALU.mult)

        def ffn(pref, gi, NT, subt):
            rmsnorm_T(gi, subt)
            chk(pref + "_norm", cur["ti"])
            hT = carve(0, [NFC, 512], BF16)
            Wg = W[pref + "_w_gate"].rearrange("(kc p) f -> p kc f", p=128)
            Wu = W[pref + "_w_up"].rearrange("(kc p) f -> p kc f", p=128)
            Wd = W[pref + "_w_down"].rearrange("(kc p) d -> p kc d", p=128)
            for f0 in range(0, DFF, 256):
                fw = min(256, DFF - f0)
                wg = wload(Wg[:, :, f0:f0 + fw], [16, fw])
                wu = wload(Wu[:, :, f0:f0 + fw], [16, fw])
                for j in range(fw // 128):
                    fc = f0 // 128 + j
                    bg = bank()
                    bu = bank()
                    for kc in range(16):
                        P.mm(bg[:, :NT], wg[:, kc, j * 128:(j + 1) * 128], xnT[:, kc, :NT], start=kc == 0, stop=kc == 15)
                    for kc in range(16):
                        P.mm(bu[:, :NT], wu[:, kc, j * 128:(j + 1) * 128], xnT[:, kc, :NT], start=kc == 0, stop=kc == 15)
                    sg = sgb[:, 0, :NT]
                    P.ins("act", "activation", out=sg, in_=bg[:, :NT], func=AF.Silu)
                    P.ins("dve", "tensor_tensor", out=hT[:, fc, :NT], in0=sg, in1=bu[:, :NT], op=ALU.mult)
            chk(pref + "_gu", cur["ti"])
            for cb in range(4):
                bks = [bank() for _ in subt]
                for k0 in range(0, NFC, 8):
                    nk = min(8, NFC - k0)
                    wd = wload(Wd[:, k0:k0 + nk, cb * 512:(cb + 1) * 512], [nk, 512])
                    for si, (st, n) in enumerate(subt):
                        for k in range(nk):
                            P.mm(bks[si][:n, :], hT[:, k0 + k, st * 128:st * 128 + n], wd[:, k, :],
                                 start=(k0 + k == 0), stop=(k0 + k == NFC - 1))
                chk(pref + "_dmm%d" % cb, cur["ti"])
                for si, (st, n) in enumerate(subt):
                    xs_ = xres[:n, st, cb * 512:(cb + 1) * 512]
                    P.ins("dve", "scalar_tensor_tensor", out=xs_, in0=bks[si][:n, :], scalar=0.5, in1=xs_,
                          op0=ALU.mult, op1=ALU.add)
                chk(pref + "_dev%d" % cb, cur["ti"])

        Win = W["w_in"].rearrange("(kc p) c -> p kc c", p=128)

        def proj_fm(col0, ncols, NT, outs):
            wt = wload(Win[:, :, col0:col0 + ncols], [16, ncols])
            for j in range((ncols + 127) // 128):
                m = min(128, ncols - j * 128)
                bb = bank()
                for kc in range(16):
                    P.mm(bb[:m, :NT], wt[:, kc, j * 128:j * 128 + m], xnT[:, kc, :NT], start=kc == 0, stop=kc == 15)
                outs(j, bb, m)

        def rwkv(NT, segs, C, last_tile):
            nseg = len(segs)
            chunks = []
            for (c0, ln, sq) in segs:
                for cc in range(0, ln, C):
                    chunks.append((c0 + cc, sq, cc == 0, cc + C >= ln))
            nch = len(chunks)
            rmask = r64 if C == 64 else r16
            NJ = 6 if C == 64 else 4
            o = 0

            def cv(shape, dt):
                nonlocal o
                v = carve(o, shape, dt)
                o += ((int(np.prod(shape)) * _dsz(dt) + 3) // 4) * 4
                return v
            T = [cv([514], F32) for _ in range(8)]
            YQ = carve(2 * 2056, [2, 8, 64], F32)
            YSQ = carve(4 * 2056, [2, 8, 64], F32)
            YN = carve(6 * 2056, [2, 8, 64], BF16)
            OPS = cv([2, 7, 512], BF16)
            GF = cv([2, 512], F32)
            BON = cv([2, 512], F32)
            WC = cv([2, 8], F32)
            LST = cv([6, 16], F32)
            SC = [cv([2, 5, 64], BF16) for _ in range(2)]
            TM3 = [cv([2, 3, 64], BF16) for _ in range(2)]
            RF = [cv([2, 64], BF16) for _ in range(2)]
            ao = [4 * 2056]

            def av_(shape, dt):
                v = carve(ao[0], shape, dt)
                ao[0] += ((int(np.prod(shape)) * _dsz(dt) + 3) // 4) * 4
                assert ao[0] <= 8 * 2056
                return v
            SC += [av_([2, 5, 64], BF16) for _ in range(2)]
            TM3 += [av_([2, 3, 64], BF16) for _ in range(2)]
            RF += [av_([2, 64], BF16) for _ in range(2)]
            QP = [[av_([2, 2, 64], BF16) for _ in range(2)] for _ in range(2)]
            RR = [[av_([2, 64], BF16) for _ in range(2)] for _ in range(2)]
            XU = cv([2, 2, 64], BF16)
            TL = cv([512], BF16)
            SG1 = cv([512], BF16)
            SG2 = cv([512], BF16)
            LTMP = T[7]

            def shift_u(bb, m, g, dst, NTl=NT):
                praw = T[0]
                dd = T[4]
                P.ins("act", "activation", out=praw[:m, 1:1 + NT], in_=bb[:m, :NT], func=AF.Copy)
                P.ins("dve", "tensor_tensor", out=dd[:m, 0:NT], in0=praw[:m, 0:NT], in1=praw[:m, 1:1 + NT],
                      op=ALU.subtract)
                for (c0, ln, sq) in segs:
                    P.ins("dve", "tensor_tensor", out=dd[:m, c0:c0 + 1], in0=carry[:m, g, sq:sq + 1],
                          in1=praw[:m, 1 + c0:2 + c0], op=ALU.subtract)
                P.ins("dve", "scalar_tensor_tensor", out=dst[:m, 0:NT], in0=dd[:m, 0:NT], scalar=mucols[:m, g:g + 1],
                      in1=praw[:m, 1:1 + NT], op0=ALU.mult, op1=ALU.add)
                for (c0, ln, sq) in segs:
                    P.ins("act", "activation", out=carry[:m, g, sq:sq + 1], in_=praw[:m, c0 + ln:c0 + ln + 1],
                          func=AF.Copy)

            def lora_out(j, bb, m):
                if m == 32:
                    j = 2
                if j == 0:
                    shift_u(bb, 128, 24, LTMP)
                    P.ins("act", "activation", out=TL[0:64, :NT], in_=LTMP[0:64, :NT], func=AF.Tanh)
                    P.ins("act", "activation", out=TL[64:128, :NT], in_=LTMP[64:128, :NT], func=AF.Copy)
                elif j == 1:
                    shift_u(bb, 128, 25, LTMP)
                    P.ins("act", "activation", out=SG1[:, :NT], in_=LTMP[:, :NT], func=AF.Sigmoid)
                else:
                    shift_u(bb, 32, 26, LTMP)
                    P.ins("act", "activation", out=SG2[0:32, :NT], in_=LTMP[0:32, :NT], func=AF.Sigmoid)
            proj_fm(O_LW, 256, NT, lora_out)
            proj_fm(O_LW + 256, 32, NT, lora_out)
            chk("rw_lora", cur["ti"])

            hk = [slice(0, 64), slice(64, 128)]
            hs = [slice(0, C), slice(64, 64 + C)]

            for q in range(4):
                for pi in range(2):
                    pp = 2 * q + pi
                    At, Rt, Bt, Kt, Bh, Kh, Vb = [OPS[:, pi, i, :] for i in range(7)]
                    ur, uk, uv = T[1], T[2], T[3]
                    wt1 = wload(Win[:, :, O_R + pp * 128:O_R + pp * 128 + 128], [16, 128])
                    wt2 = wload(Win[:, :, O_K + pp * 128:O_K + pp * 128 + 128], [16, 128])
                    wt3 = wload(Win[:, :, O_V + pp * 128:O_V + pp * 128 + 128], [16, 128])
                    for wt, g, dst in ((wt1, pp, ur), (wt2, 8 + pp, uk), (wt3, 16 + pp, uv)):
                        bb = bank()
                        for kc in range(16):
                            P.mm(bb[:, :NT], wt[:, kc, :], xnT[:, kc, :NT], start=kc == 0, stop=kc == 15)
                        shift_u(bb, 128, g, dst)
                    cols = slice(pp * 128, pp * 128 + 128)
                    ba = bank()
                    P.mm(ba[:, :NT], w2a2[64:128, cols], TL[64:128, :NT])
                    av = T[5]
                    P.ins("act", "activation", out=av[:, :NT], in_=ba[:, :NT], func=AF.Sigmoid, bias=rwc[:, 1, pp:pp + 1])
                    bw = bank()
                    P.mm(bw[:, :NT], w2a2[0:64, cols], TL[0:64, :NT])
                    sgm = T[6]
                    P.ins("act", "activation", out=sgm[:, :NT], in_=bw[:, :NT], func=AF.Sigmoid, bias=rwc[:, 0, pp:pp + 1])
                    bg_ = bank()
                    P.mm(bg_[:, :NT], g2a[:, cols], SG1[:, :NT], start=True, stop=False)
                    P.mm(bg_[:, :NT], g2b[0:32, cols], SG2[0:32, :NT], start=False, stop=True)
                    P.ins("act", "activation", out=GF[:, pi, :NT], in_=bg_[:, :NT], func=AF.Copy)
                    kkr = T[0]
                    P.ins("dve", "tensor_scalar", out=kkr[:, :NT], in0=uk[:, :NT], scalar1=rwc[:, 2, pp:pp + 1],
                          scalar2=None, op0=ALU.mult)
                    sq_ = T[4]
                    P.ins("act", "activation", out=sq_[:, :NT], in_=kkr[:, :NT], func=AF.Square)
                    bs = bank()
                    P.mm(bs[:, :NT], onesbd[:, :], sq_[:, :NT])
                    P.ins("dve", "tensor_scalar", out=sq_[:, :NT], in0=bs[:, :NT], scalar1=1e-24, scalar2=None,
                          op0=ALU.max)
                    P.ins("act", "activation", out=sq_[:, :NT], in_=sq_[:, :NT], func=AF.Sqrt)
                    P.ins("dve", "reciprocal", out=sq_[:, :NT], in_=sq_[:, :NT])
                    P.ins("dve", "tensor_tensor", out=kkr[:, :NT], in0=kkr[:, :NT], in1=sq_[:, :NT], op=ALU.mult)
                    P.ins("dve", "tensor_scalar", out=sq_[:, :NT], in0=av[:, :NT], scalar1=-1.0,
                          scalar2=rwc[:, 3, pp:pp + 1], op0=ALU.add, op1=ALU.mult)
                    P.ins("dve", "scalar_tensor_tensor", out=uk[:, :NT], in0=sq_[:, :NT], scalar=1.0, in1=uk[:, :NT],
                          op0=ALU.add, op1=ALU.mult)
                    P.ins("dve", "scalar_tensor_tensor", out=sq_[:, :NT], in0=ur[:, :NT], scalar=rwc[:, 4, pp:pp + 1],
                          in1=uk[:, :NT], op0=ALU.mult, op1=ALU.mult)
                    bs2 = bank()
                    P.mm(bs2[:, :NT], onesbd[:, :], sq_[:, :NT])
                    P.ins("dve", "tensor_tensor", out=BON[:, pi, :NT], in0=bs2[:, :NT], in1=uv[:, :NT], op=ALU.mult)
                    P.ins("dve", "tensor_scalar", out=BON[:, pi, :NT], in0=BON[:, pi, :NT],
                          scalar1=rwc[:, 6, pp:pp + 1], scalar2=None, op0=ALU.add)
                    P.ins("dve", "tensor_tensor", out=av[:, :NT], in0=kkr[:, :NT], in1=av[:, :NT], op=ALU.mult)
                    cs = T[7]
                    P.ins("dve", "tensor_tensor_scan", out=cs[:, :NT], data0=rmask[:, :NT], data1=sgm[:, :NT],
                          initial=0.0, op0=ALU.mult, op1=ALU.add)
                    P.ins("dve", "tensor_tensor", out=sgm[:, :NT], in0=cs[:, :NT], in1=sgm[:, :NT], op=ALU.subtract)
                    P.ins("act", "activation", out=sgm[:, :NT], in_=sgm[:, :NT], func=AF.Exp, scale=KAPPA)
                    P.ins("dve", "scalar_tensor_tensor", out=At[:, :NT], in0=kkr[:, :NT], scalar=-1.0, in1=sgm[:, :NT],
                          op0=ALU.mult, op1=ALU.mult)
                    P.ins("act", "activation", out=sgm[:, :NT], in_=cs[:, :NT], func=AF.Exp, scale=KAPPA)
                    P.ins("dve", "tensor_tensor", out=Rt[:, :NT], in0=ur[:, :NT], in1=sgm[:, :NT], op=ALU.mult)
                    P.ins("act", "activation", out=sgm[:, :NT], in_=cs[:, :NT], func=AF.Exp, scale=-KAPPA)
                    P.ins("dve", "tensor_tensor", out=Bt[:, :NT], in0=av[:, :NT], in1=sgm[:, :NT], op=ALU.mult)
                    P.ins("dve", "tensor_tensor", out=Kt[:, :NT], in0=uk[:, :NT], in1=sgm[:, :NT], op=ALU.mult)
                    csv = cs[:, 0:nch * C].rearrange("p (c t) -> p c t", t=C)
                    P.ins("act", "activation", out=WC[:, pi, 0:nch], in_=csv[:, :, C - 1], func=AF.Exp, scale=KAPPA)
                    P.ins("dve", "tensor_tensor", out=sgm[:, 0:nch * C].rearrange("p (c t) -> p c t", t=C),
                          in0=csv[:, :, C - 1:C].to_broadcast([128, nch, C]), in1=csv, op=ALU.subtract)
                    P.ins("act", "activation", out=sgm[:, :NT], in_=sgm[:, :NT], func=AF.Exp, scale=KAPPA)
                    P.ins("dve", "tensor_tensor", out=Bh[:, :NT], in0=av[:, :NT], in1=sgm[:, :NT], op=ALU.mult)
                    P.ins("dve", "tensor_tensor", out=Kh[:, :NT], in0=uk[:, :NT], in1=sgm[:, :NT], op=ALU.mult)
                    P.ins("act", "activation", out=Vb[:, :NT], in_=uv[:, :NT], func=AF.Copy)
                    chk("rw_pre", cur["ti"])
                    if pi == 1:
                        chk("rw_pre2", cur["ti"])

                def load_state(ci):
                    c0, sq, first, last = chunks[ci]
                    if not (first and NT == 80):
                        return
                    if sq < 4:
                        for pi in range(2):
                            pp = 2 * q + pi
                            stw = T[pi][:, 0:64]
                            P.dma("sync", stw, st_wkv[sq, 2 * pp:2 * pp + 2].rearrange("h v k -> (h v) k"),
                                  f"stw{pi}")
                            bb = bank()
                            for h in range(2):
                                P.mm(bb[hk[h], 0:64], T[pi][hk[h], 0:64], identf[hk[h], hk[h]],
                                     tp=(h * 64, h * 64))
                            P.ins("dve", "tensor_copy", out=Pst[:, pp, :], in_=bb[:, 0:64])
                            P.ins("act", "activation", out=Pbf[:, pp, :], in_=Pst[:, pp, :], func=AF.Copy)
                    else:
                        for pi in range(2):
                            pp = 2 * q + pi
                            P.ins("dve", "memset", ap=Pst[:, pp, :], constant=0.0)
                            P.ins("dve", "memset", ap=Pbf[:, pp, :], constant=0.0)

                def part_A(ci):
                    c0, sq, first, last = chunks[ci]
                    rg = ci % 4
                    cc = slice(c0, c0 + C)
                    for pi in range(2):
                        At, Rt, Bt, Kt, Bh, Kh, Vb = [OPS[:, pi, i, :] for i in range(7)]
                        bsc = bank()
                        for h in range(2):
                            tp = (h * 64, h * 64)
                            for i, (l_, r_) in enumerate(((Bt, At), (At, Bt), (Kt, At), (Bt, Rt), (Kt, Rt))):
                                P.mm(bsc[hs[h], i * 64:i * 64 + C], l_[hk[h], cc], r_[hk[h], cc], tp=tp)
                        for h in (range(1) if C == 64 else range(2)):
                            rws = slice(0, 128) if C == 64 else hs[h]
                            P.ins("dve", "tensor_tensor", out=SC[rg][rws, pi, :, 0:C],
                                  in0=bsc[rws, 0:320].rearrange("p (i t) -> p i t", t=64)[:, :, 0:C],
                                  in1=mask5[rws, :, 0:C], op=ALU.mult)
                        btm = bank()
                        for h in range(2):
                            tp = (h * 64, h * 64)
                            for i, src in enumerate((Vb, Bh, Kh)):
                                P.mm(btm[hs[h], i * 64:i * 64 + 64], src[hk[h], cc], identb[hk[h], hk[h]], tp=tp)
                        for h in (range(1) if C == 64 else range(2)):
                            rws = slice(0, 128) if C == 64 else hs[h]
                            P.ins("act", "activation", out=TM3[rg][rws, pi, :, :],
                                  in_=btm[rws, 0:192].rearrange("p (i t) -> p i t", t=64), func=AF.Copy)

                def part_B(ci, j):
                    rg4 = ci % 4
                    rg = ci % 2
                    for pi in range(2):
                        if j == 0:
                            Qc = SC[rg4][:, pi, 0, :]
                            Pc = SC[rg4][:, pi, 1, :]
                        else:
                            Qc = QP[rg][(j - 1) % 2][:, pi, 0, :]
                            Pc = QP[rg][(j - 1) % 2][:, pi, 1, :]
                        dstR = RF[rg4][:, pi, :] if j == NJ - 1 else RR[rg][j % 2][:, pi, :]
                        if j == 0:
                            for hh in range(2):
                                P.ins("dve", "tensor_tensor", out=dstR[hs[hh], 0:C], in0=Qc[hs[hh], 0:C],
                                      in1=identb[hs[hh], hh * 64:hh * 64 + C], op=ALU.add)
                        else:
                            Rp = RR[rg][(j - 1) % 2][:, pi, :]
                            bq = bank()
                            for h in range(2):
                                P.mm(bq[hs[h], 0:C], Pc[hs[h], 0:C], Rp[hs[h], 0:C], tp=(h * 64, h * 64))
                            for h in (range(1) if C == 64 else range(2)):
                                rws = slice(0, 128) if C == 64 else hs[h]
                                P.ins("dve", "tensor_tensor", out=dstR[rws, 0:C], in0=bq[rws, 0:C], in1=Rp[rws, 0:C],
                                      op=ALU.add)
                        if j < NJ - 1:
                            lastsq = j == NJ - 2
                            bq2 = bank()
                            for h in range(2):
                                tp = (h * 64, h * 64)
                                if not lastsq:
                                    P.mm(bq2[hs[h], 0:C], Pc[hs[h], 0:C], Qc[hs[h], 0:C], tp=tp)
                                P.mm(bq2[hs[h], 64:64 + C], Qc[hs[h], 0:C], Pc[hs[h], 0:C], tp=tp)
                            for h in (range(1) if C == 64 else range(2)):
                                rws = slice(0, 128) if C == 64 else hs[h]
                                if lastsq:
                                    P.ins("act", "activation", out=QP[rg][j % 2][rws, pi, 1, 0:C],
                                          in_=bq2[rws, 64:64 + C], func=AF.Copy)
                                else:
                                    P.ins("act", "activation", out=QP[rg][j % 2][rws, pi, :, 0:C],
                                          in_=bq2[rws, 0:128].rearrange("p (i t) -> p i t", t=64)[:, :, 0:C],
                                          func=AF.Copy)

                def part_C1(ci):
                    c0, sq, first, last = chunks[ci]
                    rg = ci % 4
                    cc = slice(c0, c0 + C)
                    load_state(ci)
                    for pi in range(2):
                        pp = 2 * q + pi
                        At = OPS[:, pi, 0, :]
                        X0 = XU[:, pi, 0, :]
                        bx = bank()
                        for h in range(2):
                            tp = (h * 64, h * 64)
                            P.mm(bx[hs[h], 0:64], At[hk[h], cc], Pbf[hk[h], pp, :], start=True, stop=False, tp=tp)
                            P.mm(bx[hs[h], 0:64], SC[rg][hs[h], pi, 2, 0:C], TM3[rg][hs[h], pi, 0, :],
                                 start=False, stop=True, tp=tp)
                        for h in (range(1) if C == 64 else range(2)):
                            rws = slice(0, 128) if C == 64 else hs[h]
                            P.ins("act", "activation", out=X0[rws, :], in_=bx[rws, 0:64], func=AF.Copy)

                def part_C2(ci):
                    rg = ci % 4
                    for pi in range(2):
                        X0 = XU[:, pi, 0, :]
                        U = XU[:, pi, 1, :]
                        bu_ = bank()
                        for h in range(2):
                            tp = (h * 64, h * 64)
                            P.mm(bu_[hs[h], 0:64], RF[rg][hs[h], pi, 0:C], X0[hs[h], :], tp=tp)
                        for h in (range(1) if C == 64 else range(2)):
                            rws = slice(0, 128) if C == 64 else hs[h]
                            P.ins("dve", "tensor_copy", out=U[rws, :], in_=bu_[rws, 0:64])

                def part_C3(ci):
                    c0, sq, first, last = chunks[ci]
                    rg = ci % 4
                    cc = slice(c0, c0 + C)
                    for pi in range(2):
                        pp = 2 * q + pi
                        Rt = OPS[:, pi, 1, :]
                        U = XU[:, pi, 1, :]
                        by = bank()
                        for h in range(2):
                            tp = (h * 64, h * 64)
                            P.mm(by[hs[h], 0:64], Rt[hk[h], cc], Pbf[hk[h], pp, :], start=True, stop=False, tp=tp)
                            P.mm(by[hs[h], 0:64], SC[rg][hs[h], pi, 3, 0:C], U[hs[h], :], start=False, stop=False, tp=tp)
                            P.mm(by[hs[h], 0:64], SC[rg][hs[h], pi, 4, 0:C], TM3[rg][hs[h], pi, 0, :],
                                 start=False, stop=True, tp=tp)
                        for h in (range(1) if C == 64 else range(2)):
                            rws = slice(0, 128) if C == 64 else hs[h]
                            P.ins("act", "activation", out=YQ[rws, pi, ci, :], in_=by[rws, 0:64], func=AF.Copy)
                        bp = bank()
                        for h in range(2):
                            tp = (h * 64, h * 64)
                            P.mm(bp[hk[h], 0:64], TM3[rg][hs[h], pi, 1, :], U[hs[h], :], start=True, stop=False, tp=tp)
                            P.mm(bp[hk[h], 0:64], TM3[rg][hs[h], pi, 2, :], TM3[rg][hs[h], pi, 0, :],
                                 start=False, stop=True, tp=tp)
                        P.ins("dve", "scalar_tensor_tensor", out=Pst[:, pp, :], in0=Pst[:, pp, :],
                              scalar=WC[:, pi, ci:ci + 1], in1=bp[:, 0:64], op0=ALU.mult, op1=ALU.add)
                        P.ins("act", "activation", out=Pbf[:, pp, :], in_=Pst[:, pp, :], func=AF.Copy)
                        if last and (sq < 4 or last_tile):
                            bb = bank()
                            for h in range(2):
                                P.mm(bb[hk[h], 0:64], Pst[hk[h], pp, :], identf[hk[h], hk[h]], tp=(h * 64, h * 64))
                            so = T[0][:, 64 * pi:64 * pi + 64]
                            P.ins("dve", "tensor_copy", out=so, in_=bb[:, 0:64])
                            P.dma("sync", o_wkv[sq, 2 * pp:2 * pp + 2].rearrange("h v k -> (h v) k"), so, f"owkv{pi}")

                def c_steps(grp):
                    st_ = []
                    for ci in grp:
                        st_ += [lambda ci=ci: part_C1(ci), lambda ci=ci: part_C2(ci), lambda ci=ci: part_C3(ci)]
                    return st_

                pend = []
                for g0 in range(0, nch, 2):
                    grp = list(range(g0, min(nch, g0 + 2)))
                    for ci in grp:
                        part_A(ci)
                    for j in range(NJ):
                        for ci in grp:
                            part_B(ci, j)
                        if pend:
                            pend.pop(0)()
                    while pend:
                        pend.pop(0)()
                    pend = c_steps(grp)
                while pend:
                    pend.pop(0)()
                chk("rw_dep", cur["ti"])

                ng = 2 * nch
                yq = YQ[:, :, 0:nch, :]
                s1 = LST[:, 0, 0:ng].rearrange("p (a b) -> p a b", a=2)
                s2 = LST[:, 1, 0:ng].rearrange("p (a b) -> p a b", a=2)
                mean = LST[:, 2, 0:ng].rearrange("p (a b) -> p a b", a=2)
                var = LST[:, 3, 0:ng].rearrange("p (a b) -> p a b", a=2)
                P.ins("dve", "tensor_reduce", out=s1, in_=yq, axis=AX.X, op=ALU.add)
                P.ins("act", "activation", out=YSQ[:, :, 0:nch, :], in_=yq, func=AF.Square)
                P.ins("dve", "tensor_reduce", out=s2, in_=YSQ[:, :, 0:nch, :], axis=AX.X, op=ALU.add)
                P.ins("dve", "tensor_scalar", out=mean, in0=s1, scalar1=1.0 / 64, scalar2=None, op0=ALU.mult)
                P.ins("dve", "tensor_tensor", out=var, in0=mean, in1=mean, op=ALU.mult)
                P.ins("dve", "scalar_tensor_tensor", out=var, in0=s2, scalar=1.0 / 64, in1=var, op0=ALU.mult,
                      op1=ALU.subtract)
                P.ins("dve", "tensor_scalar", out=var, in0=var, scalar1=RW_EPS, scalar2=None, op0=ALU.add)
                P.ins("act", "activation", out=var, in_=var, func=AF.Sqrt)
                P.ins("dve", "reciprocal", out=var, in_=var)
                P.ins("dve", "tensor_tensor", out=YSQ[:, :, 0:nch, :], in0=yq,
                      in1=mean.unsqueeze(3).to_broadcast([128, 2, nch, 64]), op=ALU.subtract)
                P.ins("dve", "tensor_tensor", out=YN[:, :, 0:nch, :], in0=YSQ[:, :, 0:nch, :],
                      in1=var.unsqueeze(3).to_broadcast([128, 2, nch, 64]), op=ALU.mult)
                for pi in range(2):
                    pp = 2 * q + pi
                    bt_ = bank()
                    for ci, (c0, sq, first, last) in enumerate(chunks):
                        for h in range(2):
                            P.mm(bt_[hk[h], c0:c0 + C], YN[hs[h], pi, ci, :], identb[hs[h], h * 64:h * 64 + C],
                                 tp=(h * 64, h * 64))
                    zt = T[0]
                    P.ins("dve", "scalar_tensor_tensor", out=zt[:, :NT], in0=bt_[:, :NT], scalar=rwc[:, 5, pp:pp + 1],
                          in1=BON[:, pi, :NT], op0=ALU.mult, op1=ALU.add)
                    P.ins("dve", "tensor_tensor", out=yrwT[:, pp, :NT], in0=zt[:, :NT], in1=GF[:, pi, :NT], op=ALU.mult)

        def mlstm(NT, segs, L, last_tile):
            nseg = len(segs)
            chunks = []
            for si, (c0, ln, sq) in enumerate(segs):
                for cc in range(0, ln, L):
                    chunks.append((c0 + cc, sq, cc == 0, cc + L >= ln, len(chunks), si))
            nch = len(chunks)
            o = 0

            def cv(shape, dt):
                nonlocal o
                v = carve(o, shape, dt)
                o += ((int(np.prod(shape)) * _dsz(dt) + 3) // 4) * 4
                return v
            cbuf = cv([544], F32)
            acc = cv([512], F32)
            stmp = cv([512], F32)
            irow = cv([520], F32)
            frow = cv([520], F32)
            Frow = cv([520], F32)
            Mrow = cv([520], F32)
            gcol = cv([5, 12], F32)
            qT = cv([4, 512], BF16)
            kT = cv([4, 512], BF16)
            vext = cv([5, 2, 258], BF16)
            sigo = cv([4, 512], F32)
            NUMr = [cv([2, 258], F32) for _ in range(2)]
            tqc = [cv([258], F32) for _ in range(2)]
            Eb = [cv([128], F32) for _ in range(2)]
            STb = [cv([128], BF16) for _ in range(2)]
            kTM = [cv([256], BF16) for _ in range(2)]
            hn = cv([2, 256], F32)
            hsq = cv([2, 256], F32)
            ynb = cv([2, 256], BF16)
            mpe_r = [cv([8], F32) for _ in range(2)]
            lst = cv([8, 2], F32)
            wcol_r = [cv([16], F32) for _ in range(2)]

            wt = wload(Win[:, :, O_MI:O_MI + 8], [16, 8])
            bi = bank()
            bf_ = bank()
            for kc in range(16):
                P.mm(bi[0:4, :NT], wt[:, kc, 0:4], xnT[:, kc, :NT], start=kc == 0, stop=kc == 15)
            for kc in range(16):
                P.mm(bf_[0:4, :NT], wt[:, kc, 4:8], xnT[:, kc, :NT], start=kc == 0, stop=kc == 15)
            P.ins("act", "activation", out=irow[0:4, :NT], in_=bi[0:4, :NT], func=AF.Identity, bias=ifb[0:4, 0:1])
            P.ins("act", "activation", out=frow[0:4, :NT], in_=bf_[0:4, :NT], func=AF.Sigmoid, bias=ifb[0:4, 1:2])
            P.ins("act", "activation", out=frow[0:4, :NT], in_=frow[0:4, :NT], func=AF.Ln)
            for si, (c0, ln, sq) in enumerate(segs):
                sl = slice(c0, c0 + ln)
                P.ins("dve", "tensor_tensor_scan", out=Frow[0:4, sl], data0=one512[0:4, 0:ln], data1=frow[0:4, sl],
                      initial=0.0, op0=ALU.mult, op1=ALU.add)
                P.ins("dve", "tensor_tensor", out=irow[0:4, sl], in0=irow[0:4, sl], in1=Frow[0:4, sl], op=ALU.subtract)
                mp = c0 + si + 1
                P.ins("dve", "tensor_copy", out=Mrow[0:4, mp - 1:mp], in_=mrow[0:4, sq:sq + 1])
                P.ins("dve", "tensor_tensor_scan", out=Mrow[0:4, mp:mp + ln], data0=one512[0:4, 0:ln],
                      data1=irow[0:4, sl], initial=mrow[0:4, sq:sq + 1], op0=ALU.mult, op1=ALU.max)
                P.ins("dve", "tensor_tensor", out=Frow[0:4, sl], in0=Frow[0:4, sl], in1=Mrow[0:4, mp:mp + ln], op=ALU.add)
                P.ins("dve", "tensor_copy", out=mrow[0:4, sq:sq + 1], in_=Frow[0:4, c0 + ln - 1:c0 + ln])
            for (c0, sq, first, last, slot, si) in chunks:
                bb = bank()
                mp = c0 + si + 1
                P.mm(bb[:L, 0:4], Mrow[0:4, mp:mp + L], identf[0:4, 0:4])
                P.mm(bb[:L, 4:8], irow[0:4, c0:c0 + L], identf[0:4, 0:4])
                P.mm(bb[:L, 8:12], Frow[0:4, c0:c0 + L], identf[0:4, 0:4])
                P.ins("dve", "tensor_copy", out=gcol[:L, slot, :], in_=bb[:L, 0:12])

            P.ins("pool", "memset", ap=vext[:, :, :, 256:257], constant=1.0)
            chk("ml_gate", cur["ti"])

            for hp in range(2):
                for which, base, dstT, scl in (("q", O_MQ, qT, 1.0), ("k", O_MQ + 1024, kT, 0.0625)):
                    def conv_out(j, bb, m, which=which, base=base, dstT=dstT, scl=scl):
                        g = (0 if which == "q" else 8) + hp * 4 + 2 * half + j
                        for si, (c0, ln, sq) in enumerate(segs):
                            pos = c0 + 3 * si
                            P.ins("act", "activation", out=cbuf[:, pos + 3:pos + 3 + ln], in_=bb[:, c0:c0 + ln],
                                  func=AF.Copy)
                            P.ins("dve", "tensor_copy", out=cbuf[:, pos:pos + 3], in_=ccar[:, g, sq, :])
                        ln = segs[0][1]
                        cvw = cbuf[:, 0:nseg * (ln + 3)].rearrange("p (s t) -> p s t", t=ln + 3)
                        av_ = acc[:, 0:nseg * ln].rearrange("p (s t) -> p s t", t=ln)
                        P.ins("dve", "tensor_scalar", out=av_, in0=cvw[:, :, 0:ln], scalar1=cwc[:, g, 0:1],
                              scalar2=cbc[:, g:g + 1], op0=ALU.mult, op1=ALU.add)
                        for tap in range(1, 4):
                            P.ins("dve", "scalar_tensor_tensor", out=av_, in0=cvw[:, :, tap:tap + ln],
                                  scalar=cwc[:, g, tap:tap + 1], in1=av_, op0=ALU.mult, op1=ALU.add)
                        for si, (c0, ln_, sq) in enumerate(segs):
                            pos = c0 + 3 * si
                            P.ins("act", "activation", out=ccar[:, g, sq, :], in_=cbuf[:, pos + ln_:pos + ln_ + 3],
                                  func=AF.Copy)
                        dd = dstT[:, 2 * half + j, :NT]
                        if scl == 1.0:
                            P.ins("act", "activation", out=dd, in_=acc[:, :NT], func=AF.Silu)
                        else:
                            P.ins("act", "activation", out=stmp[:, :NT], in_=acc[:, :NT], func=AF.Silu)
                            P.ins("dve", "tensor_scalar", out=dd, in0=stmp[:, :NT], scalar1=scl, scalar2=None,
                                  op0=ALU.mult)
                    for half in range(2):
                        proj_fm(base + hp * 512 + half * 256, 256, NT, conv_out)
                def po_out(j, bb, m):
                    P.ins("act", "activation", out=sigo[:, 2 * half + j, :NT], in_=bb[:, :NT], func=AF.Sigmoid)
                for half in range(2):
                    proj_fm(O_MO + hp * 512 + half * 256, 256, NT, po_out)
                for hh in range(2):
                    h = 2 * hp + hh
                    wv = wload(Win[:, :, O_MV + h * 256:O_MV + h * 256 + 256], [16, 256])
                    for (c0, sq, first, last, slot, si) in chunks:
                        bb = bank()
                        for kc in range(16):
                            P.mm(bb[:L, 0:256], xnT[:, kc, c0:c0 + L], wv[:, kc, :], start=kc == 0, stop=kc == 15)
                        P.ins("act", "activation", out=vext[:L, slot, hh, 0:256], in_=bb[:L, 0:256], func=AF.Copy)
                chk("ml_proj", cur["ti"])

                for (c0, sq, first, last, slot, si) in chunks:
                    cc = slice(c0, c0 + L)
                    mp = c0 + si + 1
                    if first and NT == 80:
                        if sq < 4:
                            for hh in range(2):
                                h = 2 * hp + hh
                                P.dma("sync", Cst[:, h, :, 0:256],
                                      st_C[sq, h].rearrange("(dc p) v -> p dc v", p=128), f"stc{hh}")
                                P.dma("sync", Cst[:, h, :, 256:257],
                                      st_n[sq, h].rearrange("(dc p one) -> p dc one", p=128, one=1), f"stn{hh}",
                                      allow_slow_non_contiguous=True)
                        else:
                            for hh in range(2):
                                h = 2 * hp + hh
                                P.ins("dve", "memset", ap=Cst[:, h, :, :], constant=0.0)
                        for hh in range(2):
                            h = 2 * hp + hh
                            P.ins("act", "activation", out=Cbf[:, h, :, 0:257], in_=Cst[:, h, :, :], func=AF.Copy)
                    NUM = NUMr[slot % 2]
                    for hh in range(2):
                        h = 2 * hp + hh
                        r2 = (slot * 2 + hh) % 2
                        mpe = mpe_r[hh]
                        wcol = wcol_r[hh]
                        bS = bank()
                        for dc in range(2):
                            P.mm(bS[:L, 0:L], kT[:, 2 * hh + dc, cc], qT[:, 2 * hh + dc, cc], start=dc == 0, stop=dc == 1)
                        bM = bank()
                        P.mm(bM[:, 0:L + 1], sel4[0:4, h * 128:h * 128 + 128], Mrow[0:4, mp - 1:mp + L])
                        P.ins("act", "activation", out=mpe[:, 0:1], in_=bM[:, 0:1], func=AF.Copy)
                        P.ins("act", "activation", out=mpe[:, 1:2], in_=bM[:, L:L + 1], func=AF.Copy)
                        E = Eb[r2]
                        P.ins("act", "activation", out=E[:L, 0:L], in_=bM[:L, 1:L + 1], func=AF.Exp, scale=-1.0,
                              bias=gcol[:L, slot, 4 + h:5 + h])
                        P.ins("pool", "tensor_tensor", out=E[:L, 0:L], in0=E[:L, 0:L], in1=mlmask[:L, 0:L], op=ALU.mult)
                        ST = STb[r2]
                        P.ins("dve", "tensor_tensor", out=ST[:L, 0:L], in0=bS[:L, 0:L], in1=E[:L, 0:L], op=ALU.mult)
                        P.ins("act", "activation", out=wcol[:L, 0:1], in_=gcol[:L, slot, h:h + 1], func=AF.Exp,
                              scale=-1.0, bias=mpe[:L, 0:1])
                        P.ins("act", "activation", out=wcol[:, 1:2], in_=mpe[:, 1:2], func=AF.Exp, scale=-1.0,
                              bias=mpe[:, 0:1])
                        P.ins("act", "activation", out=wcol[:L, 2:3], in_=mpe[:L, 1:2], func=AF.Exp, scale=-1.0,
                              bias=gcol[:L, slot, 4 + h:5 + h])
                        bQ = bank()
                        for dc in range(2):
                            P.mm(bQ[:L, 0:257], qT[:, 2 * hh + dc, cc], Cbf[:, h, dc, 0:257], start=dc == 0, stop=dc == 1)
                        bI = bank()
                        P.mm(bI[:L, 0:257], ST[:L, 0:L], vext[:L, slot, hh, 0:257])
                        tq = tqc[r2]
                        P.ins("act", "activation", out=tq[:L, 0:257], in_=bQ[:L, 0:257], func=AF.Identity,
                              scale=wcol[:L, 0:1])
                        P.ins("dve", "tensor_tensor", out=NUM[:L, hh, 0:257], in0=tq[:L, 0:257], in1=bI[:L, 0:257],
                              op=ALU.add)
                        chk("ml_num", cur["ti"])
                        bK = bank()
                        for dc in range(2):
                            P.mm(bK[:L, dc * 128:(dc + 1) * 128], kT[:, 2 * hh + dc, cc], identb[:, :])
                        km = kTM[r2]
                        P.ins("act", "activation", out=km[:L, 0:256], in_=bK[:L, 0:256], func=AF.Identity,
                              scale=wcol[:L, 2:3])
                        for dc in range(2):
                            bC = bank()
                            P.mm(bC[:, 0:257], km[:L, dc * 128:(dc + 1) * 128], vext[:L, slot, hh, 0:257])
                            P.ins("dve", "scalar_tensor_tensor", out=Cst[:, h, dc, :], in0=Cst[:, h, dc, :],
                                  scalar=wcol[:, 1:2], in1=bC[:, 0:257], op0=ALU.mult, op1=ALU.add)
                        P.ins("act", "activation", out=Cbf[:, h, :, 0:257], in_=Cst[:, h, :, :], func=AF.Copy)
                        if last and (sq < 4 or last_tile):
                            pass
                        chk("ml_st", cur["ti"])
                        if last and (sq < 4 or last_tile):
                            P.dma("sync", o_C[sq, h].rearrange("(dc p) v -> p dc v", p=128), Cst[:, h, :, 0:256],
                                  f"oc{hh}")
                            P.dma("sync", o_n[sq, h].rearrange("(dc p one) -> p dc one", p=128, one=1),
                                  Cst[:, h, :, 256:257], f"on{hh}", allow_slow_non_contiguous=True)
                    P.ins("act", "activation", out=lst[:L, 0, :], in_=NUM[:L, :, 256], func=AF.Abs)
                    P.ins("act", "activation", out=lst[:L, 1, :], in_=gcol[:L, slot, 8 + 2 * hp:10 + 2 * hp],
                          func=AF.Exp, scale=-1.0)
                    P.ins("dve", "tensor_tensor", out=lst[:L, 0, :], in0=lst[:L, 0, :], in1=lst[:L, 1, :], op=ALU.max)
                    P.ins("dve", "reciprocal", out=lst[:L, 0, :], in_=lst[:L, 0, :])
                    P.ins("dve", "tensor_tensor", out=hn[:L, :, :], in0=NUM[:L, :, 0:256],
                          in1=lst[:L, 0, :].unsqueeze(2).to_broadcast([L, 2, 256]), op=ALU.mult)
                    P.ins("dve", "tensor_reduce", out=lst[:L, 2, :], in_=hn[:L, :, :], axis=AX.X, op=ALU.add)
                    P.ins("act", "activation", out=hsq[:L, :, :], in_=hn[:L, :, :], func=AF.Square)
                    P.ins("dve", "tensor_reduce", out=lst[:L, 3, :], in_=hsq[:L, :, :], axis=AX.X, op=ALU.add)
                    P.ins("dve", "tensor_scalar", out=lst[:L, 2, :], in0=lst[:L, 2, :], scalar1=1.0 / 256, scalar2=None,
                          op0=ALU.mult)
                    P.ins("dve", "tensor_tensor", out=lst[:L, 4, :], in0=lst[:L, 2, :], in1=lst[:L, 2, :], op=ALU.mult)
                    P.ins("dve", "scalar_tensor_tensor", out=lst[:L, 4, :], in0=lst[:L, 3, :], scalar=1.0 / 256,
                          in1=lst[:L, 4, :], op0=ALU.mult, op1=ALU.subtract)
                    P.ins("dve", "tensor_scalar", out=lst[:L, 4, :], in0=lst[:L, 4, :], scalar1=ML_EPS, scalar2=None,
                          op0=ALU.add)
                    P.ins("act", "activation", out=lst[:L, 4, :], in_=lst[:L, 4, :], func=AF.Sqrt)
                    P.ins("dve", "reciprocal", out=lst[:L, 4, :], in_=lst[:L, 4, :])
                    P.ins("dve", "tensor_tensor", out=hsq[:L, :, :], in0=hn[:L, :, :],
                          in1=lst[:L, 2, :].unsqueeze(2).to_broadcast([L, 2, 256]), op=ALU.subtract)
                    P.ins("dve", "tensor_tensor", out=ynb[:L, :, :], in0=hsq[:L, :, :],
                          in1=lst[:L, 4, :].unsqueeze(2).to_broadcast([L, 2, 256]), op=ALU.mult)
                    chk("ml_ln", cur["ti"])
                    bT = bank()
                    ynf = ynb[:, :, :].rearrange("p a b -> p (a b)")
                    for gq in range(4):
                        P.mm(bT[:, gq * 128:gq * 128 + L], ynf[:L, gq * 128:(gq + 1) * 128], identb[:L, :L])
                    for gq in range(4):
                        g = hp * 4 + gq
                        P.ins("dve", "scalar_tensor_tensor", out=ymlT[:, g, cc], in0=bT[:, gq * 128:gq * 128 + L],
                              scalar=nwc[:, g:g + 1], in1=sigo[:, gq, cc], op0=ALU.mult, op1=ALU.mult)
                    chk("ml_epi", cur["ti"])

        def merge(NT, subt):
            mg = carve(0, [NKC, 512], BF16)
            t1 = carve(16384, [512], F32)
            t2 = carve(18432, [512], F32)
            t3 = carve(20480, [512], F32)
            Wr = W["w_br_rw"].rearrange("(kc p) d -> p kc d", p=128)
            Wm = W["w_br_ml"].rearrange("(kc p) d -> p kc d", p=128)
            for f0 in range(0, D, 256):
                wg1 = wload(Win[:, :, O_G1 + f0:O_G1 + f0 + 256], [16, 256])
                wg2 = wload(Win[:, :, O_G2 + f0:O_G2 + f0 + 256], [16, 256])
                wr = wload(Wr[:, :, f0:f0 + 256], [8, 256])
                wm = wload(Wm[:, :, f0:f0 + 256], [8, 256])
                for j in range(2):
                    fc = f0 // 128 + j
                    cs_ = slice(j * 128, (j + 1) * 128)
                    b1, b2, b3, b4 = bank(), bank(), bank(), bank()
                    for kc in range(16):
                        P.mm(b1[:, :NT], wg1[:, kc, cs_], xnT[:, kc, :NT], start=kc == 0, stop=kc == 15)
                    for kc in range(16):
                        P.mm(b2[:, :NT], wg2[:, kc, cs_], xnT[:, kc, :NT], start=kc == 0, stop=kc == 15)
                    for kc in range(8):
                        P.mm(b3[:, :NT], wr[:, kc, cs_], yrwT[:, kc, :NT], start=kc == 0, stop=kc == 7)
                    for kc in range(8):
                        P.mm(b4[:, :NT], wm[:, kc, cs_], ymlT[:, kc, :NT], start=kc == 0, stop=kc == 7)
                    P.ins("act", "activation", out=t1[:, :NT], in_=b1[:, :NT], func=AF.Sigmoid)
                    P.ins("act", "activation", out=t2[:, :NT], in_=b2[:, :NT], func=AF.Sigmoid)
                    P.ins("dve", "tensor_tensor", out=t1[:, :NT], in0=t1[:, :NT], in1=b3[:, :NT], op=ALU.mult)
                    P.ins("dve", "tensor_tensor", out=t2[:, :NT], in0=t2[:, :NT], in1=b4[:, :NT], op=ALU.mult)
                    P.ins("pool", "tensor_tensor", out=mg[:, fc, :NT], in0=t1[:, :NT], in1=t2[:, :NT], op=ALU.add)
            Wo = W["w_out"].rearrange("(kc p) d -> p kc d", p=128)
            for cb in range(4):
                bks = [bank() for _ in subt]
                for k0 in range(0, 16, 8):
                    wo = wload(Wo[:, k0:k0 + 8, cb * 512:(cb + 1) * 512], [8, 512])
                    for si, (st, n) in enumerate(subt):
                        for k in range(8):
                            P.mm(bks[si][:n, :], mg[:, k0 + k, st * 128:st * 128 + n], wo[:, k, :],
                                 start=(k0 + k == 0), stop=(k0 + k == 15))
                for si, (st, n) in enumerate(subt):
                    xs_ = xres[:n, st, cb * 512:(cb + 1) * 512]
                    P.ins("dve", "tensor_tensor", out=xs_, in0=bks[si][:n, :], in1=xs_, op=ALU.add)

        try:
            for ti in range(NPT + 1):
                last_tile = ti == NPT
                cur["ti"] = ti
                wl_state["n"] = 0
                if ti == 0:
                    NT = 80
                    subt = [(0, 80)]
                    P.dma("sync", xres[0:64, 0, :], xs, "xin0")
                    P.dma("sync", xres[64:80, 0, :], meta, "xinm")
                    segs = [(16 * j, 16, j) for j in range(5)]
                    Crw, Lml = 16, 16
                else:
                    NT = 512
                    subt = [(st, 128) for st in range(4)]
                    for st in range(4):
                        r0 = (ti - 1) * 512 + st * 128
                        P.dma("sync", xres[:, st, :], xp[r0:r0 + 128, :], f"xin{st}")
                    segs = [(0, 512, 4)]
                    Crw, Lml = 64, 128
                chk("load", ti)
                ffn("ffn1", 0, NT, subt)
                chk("ffn1", ti)
                if dbg and ti == 1:
                    for st in range(4):
                        P.dma("sync", dbg_out["d_x1"][st * 128:(st + 1) * 128, :], xres[:, st, :], "dbg")
                rmsnorm_T(1, subt)
                rwkv(NT, segs, Crw, last_tile)
                chk("rwkv", ti)
                mlstm(NT, segs, Lml, last_tile)
                chk("mlstm", ti)
                if dbg and ti == 1:
                    P.dma("sync", dbg_out["d_yrw"], yrwT[:], "dbg")
                    P.dma("sync", dbg_out["d_yml"], ymlT[:], "dbg")
                merge(NT, subt)
                chk("merge", ti)
                if dbg and ti == 1:
                    for st in range(4):
                        P.dma("sync", dbg_out["d_x2"][st * 128:(st + 1) * 128, :], xres[:, st, :], "dbg")
                ffn("ffn2", 2, NT, subt)
                for st, n in subt:
                    xsb = xsbs[st % 2]
                    ssq = small[:n, 16 + 2 * st:17 + 2 * st]
                    rstd = small[:n, 17 + 2 * st:18 + 2 * st]
                    P.ins("act", "activation", out=xsb[:n, :], in_=xres[:n, st, :], func=AF.Square, accum_out=ssq)
                    P.ins("dve", "tensor_scalar", out=rstd, in0=ssq, scalar1=1.0 / D, scalar2=1e-6, op0=ALU.mult, op1=ALU.add)
                    P.ins("act", "activation", out=rstd, in_=rstd, func=AF.Sqrt)
                    P.ins("dve", "reciprocal", out=rstd, in_=rstd)
                    P.ins("dve", "scalar_tensor_tensor", out=xres[:n, st, :], in0=xres[:n, st, :], scalar=rstd,
                          in1=gfin[:n, :], op0=ALU.mult, op1=ALU.mult)
                    if ti == 0:
                        P.dma("sync", ys, xres[0:64, 0, :], "yout0")
                    else:
                        r0 = (ti - 1) * 512 + st * 128
                        P.dma("sync", yp[r0:r0 + 128, :], xres[:, st, :], f"yout{st}")

        except _Stop:
            pass
        if stop is not None:
            P.emit()
            return nc
        stg = carve(0, [3392], F32)
        stg2 = carve(16384, [2048], F32)
        for r in range(7):
            bb = bank()
            gs = list(range(r * 4, min(27, r * 4 + 4)))
            for g in gs:
                n = 128 if g < 26 else 32
                P.mm(bb[0:5, (g - r * 4) * 128:(g - r * 4) * 128 + n], carry[:n, g, :], identf[:n, :n])
            c0 = r * 512
            c1 = min(RWC, c0 + 512)
            P.ins("dve", "tensor_copy", out=stg[0:5, c0:c1], in_=bb[0:5, 0:c1 - c0])
        P.dma("sync", o_shift, stg[0:5, 0:RWC], "ofin")
        for r in range(4):
            bb = bank()
            for g in range(r * 4, r * 4 + 4):
                P.mm(bb[0:15, (g - r * 4) * 128:(g - r * 4 + 1) * 128],
                     ccar[:, g, :, :].rearrange("p s j -> p (s j)"), identf[:, :])
            P.ins("dve", "tensor_copy", out=stg2[0:15, r * 512:(r + 1) * 512], in_=bb[0:15, :])
        P.dma("sync", o_conv.rearrange("s j c -> (s j) c"), stg2[0:15, :], "ofin")
        P.dma("sync", o_m.rearrange("s h -> h s"), mrow[0:4, 0:5], "ofin", allow_slow_non_contiguous=True)
        P.emit()
    return nc


_CACHE = {}


def _get_nc(NPT, dbg=False):
    k = (NPT, dbg)
    if k not in _CACHE:
        _CACHE[k] = build(NPT, dbg)
    return _CACHE[k]


def make_in_maps(inputs, ncores, NPT):
    cst = make_consts()
    maps = []
    for c in range(ncores):
        m = {
            "xp": np.ascontiguousarray(inputs["x_prompt"][c, :NPT * 512]),
            "xs": np.ascontiguousarray(inputs["x_sample"][4 * c:4 * c + 4].reshape(64, D)),
            "meta": np.ascontiguousarray(inputs["meta_tokens"]),
            "st_shift": np.ascontiguousarray(inputs["state_rwkv_shift"][0, 4 * c:4 * c + 4]),
            "st_wkv": np.ascontiguousarray(inputs["state_rwkv_wkv"][0, 4 * c:4 * c + 4]),
            "st_conv": np.ascontiguousarray(inputs["state_mlstm_conv"][0, 4 * c:4 * c + 4]),
            "st_C": np.ascontiguousarray(inputs["state_mlstm_C"][0, 4 * c:4 * c + 4]),
            "st_n": np.ascontiguousarray(inputs["state_mlstm_n"][0, 4 * c:4 * c + 4]),
            "st_m": np.ascontiguousarray(inputs["state_mlstm_m"][0, 4 * c:4 * c + 4]),
            "cst": cst,
        }
        for nm, shp in WSHAPES:
            m[nm] = np.ascontiguousarray(np.asarray(inputs[nm]).reshape(shp))
        maps.append(m)
    return maps


def assemble(results, ncores):
    f = np.float32
    cat = lambda k, sl: np.concatenate([np.asarray(r[k])[sl] for r in results], 0)
    y_prompt = np.stack([np.asarray(r["yp"]) for r in results], 0).astype(f)
    y_sample = np.concatenate([np.asarray(r["ys"]).reshape(4, 16, D) for r in results], 0).astype(f)
    outs = [y_prompt, y_sample]
    for k in ("o_shift", "o_wkv", "o_conv", "o_C", "o_n", "o_m"):
        outs.append(cat(k, slice(4, 5))[None].astype(f))
    for k in ("o_shift", "o_wkv", "o_conv", "o_C", "o_n", "o_m"):
        outs.append(cat(k, slice(0, 4))[None].astype(f))
    return tuple(outs)


def kernel(**inputs):
    inputs = {k: np.asarray(v) for k, v in inputs.items()}
    NPT = inputs["x_prompt"].shape[1] // 512
    ncores = inputs["x_prompt"].shape[0]
    nc = _get_nc(NPT)
    in_maps = make_in_maps(inputs, ncores, NPT)
    res = run_bass_kernel_spmd(nc, in_maps, core_ids=list(range(ncores)))
    return assemble(res.results, ncores)
```
